# Optimizing a Trainium2 kernel written in Bass

```python
import math
import jax, jax.numpy as jnp
from jax import lax
import numpy as np

D_MODEL = 1024
BATCH = 8
SEQ = 2048
DEPTH = 1

CHUNK = 64
Q_BLOCK = 128
NORM_EPS = 1e-6
ROPE_THETA = 10000.0

DA_WIDTH = 512
DA_HEADS = 4
DA_HALF_DIM = DA_WIDTH // (2 * DA_HEADS)
DA_VDIM = 2 * DA_HALF_DIM
DA_SUBLN_EPS = 1e-5

RW_WIDTH = 512
RW_HEAD = 64
RW_HEADS = RW_WIDTH // RW_HEAD
RW_DECAY_LORA = 64
RW_AAA_LORA = 64
RW_GATE_LORA = 128
RW_GN_EPS = RW_HEAD * 1e-5
RW_IN_WIDTH = 3 * RW_WIDTH + RW_DECAY_LORA + RW_AAA_LORA + RW_GATE_LORA
RW_SPLIT_IDX = [RW_WIDTH, 2 * RW_WIDTH, 3 * RW_WIDTH,
                3 * RW_WIDTH + RW_DECAY_LORA,
                3 * RW_WIDTH + RW_DECAY_LORA + RW_AAA_LORA]

DA_IN_WIDTH = 3 * DA_WIDTH
GATE_WIDTH = 2 * D_MODEL
D_IN = DA_IN_WIDTH + RW_IN_WIDTH + GATE_WIDTH

D_FF = 4 * D_MODEL

kernel_name = "hybrid_diffattn_rwkv7_gated_block"


def rms_norm(x, g, eps=NORM_EPS):
    xf = x.astype(jnp.float32)
    y = xf * lax.rsqrt(jnp.mean(xf * xf, axis=-1, keepdims=True) + eps)
    return (y * g.astype(jnp.float32)).astype(x.dtype)


def rope(x, pos):
    d = x.shape[-1]
    inv_freq = ROPE_THETA ** (-jnp.arange(0, d, 2, dtype=jnp.float32) / d)
    ang = pos[:, None] * inv_freq[None, :]
    cos = jnp.cos(ang)[None, :, None, :].astype(x.dtype)
    sin = jnp.sin(ang)[None, :, None, :].astype(x.dtype)
    x1, x2 = x[..., : d // 2], x[..., d // 2:]
    return jnp.concatenate([x1 * cos - x2 * sin, x2 * cos + x1 * sin], axis=-1)


def diff_attention(q, k, v, lq1, lk1, lq2, lk2, subln_g, lambda_init):
    B, S, _ = q.shape
    pos = jnp.arange(S, dtype=jnp.float32)
    q = rope(q.reshape(B, S, 2 * DA_HEADS, DA_HALF_DIM), pos).reshape(B, S, DA_HEADS, 2, DA_HALF_DIM)
    k = rope(k.reshape(B, S, 2 * DA_HEADS, DA_HALF_DIM), pos).reshape(B, S, DA_HEADS, 2, DA_HALF_DIM)
    v = v.reshape(B, S, DA_HEADS, DA_VDIM)
    lam = (jnp.exp(jnp.sum(lq1.astype(jnp.float32) * lk1.astype(jnp.float32)))
           - jnp.exp(jnp.sum(lq2.astype(jnp.float32) * lk2.astype(jnp.float32)))
           + lambda_init)
    nb = S // Q_BLOCK
    qb = q.reshape(B, nb, Q_BLOCK, DA_HEADS, 2, DA_HALF_DIM).transpose(1, 0, 2, 3, 4, 5)
    k_chunk = jnp.arange(S) // CHUNK
    scale = DA_HALF_DIM ** -0.5

    def attend(args):
        q_blk, bi = args
        s = jnp.einsum('bqhcd,bkhcd->bhcqk', q_blk, k,
                       preferred_element_type=jnp.float32) * scale
        q_chunk = (bi * Q_BLOCK + jnp.arange(Q_BLOCK)) // CHUNK
        allowed = k_chunk[None, :] <= q_chunk[:, None]
        s = jnp.where(allowed, s, -jnp.inf)
        p = jax.nn.softmax(s, axis=-1)
        w = p[:, :, 0] - lam * p[:, :, 1]
        return jnp.einsum('bhqk,bkhv->bqhv', w.astype(v.dtype), v)

    o = lax.map(attend, (qb, jnp.arange(nb)))
    o = o.transpose(1, 0, 2, 3, 4).reshape(B, S, DA_HEADS, DA_VDIM)
    o = rms_norm(o, subln_g, DA_SUBLN_EPS) * (1.0 - lambda_init)
    return o.reshape(B, S, DA_WIDTH)


def rwkv7_time_mix(z, mu, w0, w2, a0, a2, g2, k_k, k_a, r_k, ln_g, ln_b):
    B, S, _ = z.shape
    f32 = jnp.float32
    z_prev = jnp.pad(z, ((0, 0), (1, 0), (0, 0)))[:, :-1]
    z = z + (z_prev - z) * mu
    r, k, v, wd, ad, gd = jnp.split(z, RW_SPLIT_IDX, axis=-1)
    w = -jax.nn.softplus(-(w0 + jnp.tanh(wd) @ w2).astype(f32)) - 0.5
    decay = jnp.exp(-jnp.exp(w))
    a = jax.nn.sigmoid(a0 + ad @ a2)
    g = jax.nn.sigmoid(gd) @ g2
    hd = lambda t: t.reshape(B, S, RW_HEADS, RW_HEAD).astype(f32)
    kk = hd(k * k_k)
    kk = kk / jnp.maximum(jnp.sqrt(jnp.sum(kk * kk, axis=-1, keepdims=True)), 1e-12)
    k = k * (1.0 + (a - 1.0) * k_a)
    r_h, k_h, v_h, a_h, w_h = hd(r), hd(k), hd(v), hd(a), hd(decay)
    a_vec = -kk
    b_vec = kk * a_h

    def step(state, inp):
        r_t, w_t, k_t, v_t, a_t, b_t = inp
        sa = jnp.einsum('bhvk,bhk->bhv', state, a_t)
        state = (state * w_t[:, :, None, :] + sa[..., None] * b_t[:, :, None, :]
                 + v_t[..., None] * k_t[:, :, None, :])
        return state, jnp.einsum('bhvk,bhk->bhv', state, r_t)

    tm = lambda t: jnp.moveaxis(t, 1, 0)
    state0 = jnp.zeros((B, RW_HEADS, RW_HEAD, RW_HEAD), f32)
    _, y = lax.scan(step, state0, (tm(r_h), tm(w_h), tm(k_h), tm(v_h), tm(a_vec), tm(b_vec)))
    y = jnp.moveaxis(y, 0, 1)
    mean = jnp.mean(y, axis=-1, keepdims=True)
    var = jnp.mean(jnp.square(y - mean), axis=-1, keepdims=True)
    y = ((y - mean) * lax.rsqrt(var + RW_GN_EPS)).reshape(B, S, RW_WIDTH)
    y = y * ln_g.astype(f32) + ln_b.astype(f32)
    bonus = jnp.sum(r_h * k_h * r_k.astype(f32), axis=-1, keepdims=True) * v_h
    y = (y + bonus.reshape(B, S, RW_WIDTH)) * g.astype(f32)
    return y.astype(z.dtype)


def setup_inputs(seed: int = 0) -> dict:
    key = jax.random.key(seed)
    ks = jax.random.split(key, 32)
    nrm = lambda k, shape, s: jax.random.normal(k, shape, jnp.float32) * s
    L = DEPTH
    return {
        "x": nrm(ks[0], (BATCH, SEQ, D_MODEL), 1.0),
        "norm_mix_g": 1.0 + nrm(ks[1], (L, D_MODEL), 0.02),
        "w_in": nrm(ks[2], (L, D_MODEL, D_IN), D_MODEL ** -0.5),
        "rw_mu": jax.random.uniform(ks[3], (L, RW_IN_WIDTH), jnp.float32, 0.0, 1.0),
        "rw_w0": jax.random.uniform(ks[4], (L, RW_WIDTH), jnp.float32, -6.5, -1.5),
        "rw_w2": nrm(ks[5], (L, RW_DECAY_LORA, RW_WIDTH), 0.1 * RW_DECAY_LORA ** -0.5),
        "rw_a0": nrm(ks[6], (L, RW_WIDTH), 0.1),
        "rw_a2": nrm(ks[7], (L, RW_AAA_LORA, RW_WIDTH), RW_AAA_LORA ** -0.5),
        "rw_g2": nrm(ks[8], (L, RW_GATE_LORA, RW_WIDTH), RW_GATE_LORA ** -0.5),
        "rw_k_k": 0.85 + nrm(ks[9], (L, RW_WIDTH), 0.02),
        "rw_k_a": 1.0 + nrm(ks[10], (L, RW_WIDTH), 0.02),
        "rw_r_k": nrm(ks[11], (L, RW_HEADS, RW_HEAD), 0.1),
        "rw_ln_g": 1.0 + nrm(ks[12], (L, RW_WIDTH), 0.02),
        "rw_ln_b": nrm(ks[13], (L, RW_WIDTH), 0.02),
        "da_lq1": nrm(ks[14], (L, DA_HALF_DIM), 0.1),
        "da_lk1": nrm(ks[15], (L, DA_HALF_DIM), 0.1),
        "da_lq2": nrm(ks[16], (L, DA_HALF_DIM), 0.1),
        "da_lk2": nrm(ks[17], (L, DA_HALF_DIM), 0.1),
        "da_subln_g": 1.0 + nrm(ks[18], (L, DA_VDIM), 0.02),
        "w_branch_a": nrm(ks[19], (L, DA_WIDTH, D_MODEL), DA_WIDTH ** -0.5),
        "w_branch_b": nrm(ks[20], (L, RW_WIDTH, D_MODEL), RW_WIDTH ** -0.5),
        "w_o": nrm(ks[21], (L, D_MODEL, D_MODEL), D_MODEL ** -0.5),
        "norm_ffn_g": 1.0 + nrm(ks[22], (L, D_MODEL), 0.02),
        "w_ff1": nrm(ks[23], (L, D_MODEL, D_FF), D_MODEL ** -0.5),
        "w_ff2": nrm(ks[24], (L, D_FF, D_MODEL), D_FF ** -0.5),
        "norm_final_g": 1.0 + nrm(ks[25], (D_MODEL,), 0.02),
    }


def reference(x, norm_mix_g, w_in, rw_mu, rw_w0, rw_w2, rw_a0, rw_a2, rw_g2, rw_k_k,
              rw_k_a, rw_r_k, rw_ln_g, rw_ln_b, da_lq1, da_lk1, da_lq2, da_lk2,
              da_subln_g, w_branch_a, w_branch_b, w_o, norm_ffn_g, w_ff1, w_ff2,
              norm_final_g):
    for l in range(DEPTH):
        lambda_init = 0.8 - 0.6 * math.exp(-0.3 * l)
        h = rms_norm(x, norm_mix_g[l])
        proj = h @ w_in[l]
        da_in, rw_in, gates = jnp.split(proj, [DA_IN_WIDTH, DA_IN_WIDTH + RW_IN_WIDTH], axis=-1)
        q, k, v = jnp.split(da_in, 3, axis=-1)
        y_a = diff_attention(q, k, v, da_lq1[l], da_lk1[l], da_lq2[l], da_lk2[l],
                             da_subln_g[l], lambda_init)
        y_b = rwkv7_time_mix(rw_in, rw_mu[l], rw_w0[l], rw_w2[l], rw_a0[l], rw_a2[l],
                             rw_g2[l], rw_k_k[l], rw_k_a[l], rw_r_k[l], rw_ln_g[l], rw_ln_b[l])
        gate_a, gate_b = jnp.split(gates, 2, axis=-1)
        merged = (jax.nn.sigmoid(gate_a) * (y_a @ w_branch_a[l])
                  + jax.nn.sigmoid(gate_b) * (y_b @ w_branch_b[l]))
        x = x + merged @ w_o[l]
        h = rms_norm(x, norm_ffn_g[l])
        x = x + jnp.square(jax.nn.relu(h @ w_ff1[l])) @ w_ff2[l]
    return rms_norm(x, norm_final_g)
```

```python
import math
import os
SK = os.environ.get('MK_KERNEL_DEBUG_FLAGS', '').split(',')
from contextlib import ExitStack

import numpy as np
import ml_dtypes
import concourse.bass as bass
import concourse.mybir as mybir
from concourse.bass_utils import run_bass_kernel_spmd

F32 = mybir.dt.float32
BF16 = mybir.dt.bfloat16
I32 = mybir.dt.int32
AF = mybir.ActivationFunctionType
ALU = mybir.AluOpType
AX = mybir.AxisListType

S_LEN = 2048
D = 1024
NT = 16
OFF = 8
D_IN = 5376
ENGS = ("pe", "act", "dve", "pool", "sp")


class Sched:
    def __init__(self, nc, sems):
        self.nc = nc
        self.free_sems = list(sems)
        self.esem = {e: self.free_sems.pop() for e in ("pe", "act", "dve", "pool")}
        self.cnt = {e: 0 for e in ENGS}
        self.ops = {e: [] for e in ENGS}
        self.last_w = {}
        self.readers = {}
        self.known = {e: {} for e in ENGS}
        self.groups = {}

    def _deps(self, reads, writes):
        evs = []
        for r in reads:
            if r in self.last_w:
                evs.append(self.last_w[r])
        for w in writes:
            if w in self.last_w:
                evs.append(self.last_w[w])
            evs.extend(self.readers.get(w, ()))
        return evs

    def _commit(self, ev, reads, writes):
        for r in reads:
            self.readers.setdefault(r, []).append(ev)
        for w in writes:
            self.last_w[w] = ev
            self.readers[w] = []

    limit = None
    nrec = 0
    pe_inorder = True

    capture = None

    def replay(self, items):
        for eng, fn, reads, writes, pe_wait, flag in items:
            keep = self.pe_inorder
            self.pe_inorder = flag
            self.op(eng, fn, reads, writes, pe_wait)
            self.pe_inorder = keep

    def op(self, eng, fn, reads=(), writes=(), pe_wait=False):
        if self.capture is not None:
            self.capture.append((eng, fn, list(reads), list(writes), pe_wait, self.pe_inorder))
            return
        if self.limit is not None:
            self.nrec += 1
            if self.nrec > self.limit:
                return
        evs = self._deps(reads, writes)
        if eng == "pe" and self.pe_inorder and not pe_wait:
            evs = [ev for ev in evs if not (ev[0] == "e" and ev[1] == "pe")]
        self.cnt[eng] += 1
        ev = ("e", eng, self.cnt[eng])
        self.ops[eng].append((fn, evs, ("e", eng)))
        self._commit(ev, reads, writes)

    def dma(self, q, fn, group, reads=(), writes=(), serial=False):
        evs = [ev for ev in self._deps(reads, writes) if not (ev[0] == "g" and ev[1] == group)]
        if group not in self.groups:
            self.groups[group] = [self.free_sems.pop(), 0]
        self.groups[group][1] += 1
        ev = ("gs", group, self.groups[group][1]) if serial else ("g", group)
        self.ops[q].append((fn, evs, ("g", group)))
        self._commit(ev, reads, writes)

    def _resolve(self, ev):
        if ev[0] == "e":
            return self.esem[ev[1]], ev[2]
        g = self.groups[ev[1]]
        if ev[0] == "gs":
            return g[0], 16 * ev[2]
        return g[0], 16 * g[1]

    def barrier(self):
        evs = [("e", e, self.cnt[e]) for e in self.esem if self.cnt[e] > 0]
        evs += [("gs", g, v[1]) for g, v in self.groups.items()]
        for e in ENGS:
            self.ops[e].append((None, list(evs), None))

    def emit(self, eng, engobj):
        known = self.known[eng]
        for fn, evs, inc in self.ops[eng]:
            need = {}
            for ev in evs:
                sem, val = self._resolve(ev)
                k = id(sem)
                if known.get(k, 0) >= val:
                    continue
                if k not in need or need[k][1] < val:
                    need[k] = (sem, val)
            for k, (sem, val) in need.items():
                engobj.wait_ge(sem, val)
                known[k] = val
            if fn is None:
                continue
            ins = fn()
            if inc[0] == "e":
                ins.then_inc(self.esem[inc[1]], 1)
            else:
                ins.then_inc(self.groups[inc[1]][0], 16)
        self.ops[eng] = []

    def final_wait(self, engobj, groups):
        if groups:
            groups = list(self.groups.keys())
        for gname in groups:
            g = self.groups[gname]
            engobj.wait_ge(g[0], 16 * g[1])


def run_block(nc, S, final_groups=()):
    S.barrier()
    with nc.Block() as block:
        @block.sync
        def _(e):
            S.emit("sp", e)
            S.final_wait(e, final_groups)

        @block.scalar
        def _(e):
            S.emit("act", e)

        @block.vector
        def _(e):
            S.emit("dve", e)

        @block.gpsimd
        def _(e):
            S.emit("pool", e)

        @block.tensor
        def _(e):
            S.emit("pe", e)


SMALL = [("norm_mix_g", 1024), ("rw_mu", 1792), ("rw_w0", 512), ("rw_a0", 512), ("rw_k_k", 512), ("rw_k_a", 512),
         ("rw_r_k", 512), ("rw_ln_g", 512), ("rw_ln_b", 512), ("da_lq1", 64), ("da_lk1", 64), ("da_lq2", 64),
         ("da_lk2", 64), ("da_subln_g", 128), ("norm_ffn_g", 1024), ("norm_final_g", 1024)]
SMALL_OFF = {}
_o = 0
for _n, _l in SMALL:
    SMALL_OFF[_n] = (_o, _l)
    _o += _l
BLOB_LEN = _o
PARAM_SPECS = [
    ("blob", [BLOB_LEN]), ("rw_w2", [64, 512]), ("rw_a2", [64, 512]), ("rw_g2", [128, 512]),
    ("w_branch_a", [512, 1024]), ("w_branch_b", [512, 1024]),
    ("w_o", [1024, 1024]), ("w_ff1", [1024, 4096]), ("w_ff2", [4096, 1024]), ("w_in", [1024, D_IN]),
]


def pack_params(inp):
    m = {"blob": np.concatenate([np.asarray(inp[n], np.float32).reshape(-1) for n, _ in SMALL])}
    for name, shp in PARAM_SPECS[1:]:
        m[name] = np.ascontiguousarray(np.asarray(inp[name], np.float32).reshape(shp))
    return m


def host_consts():
    c = {}
    c["c_ident"] = np.eye(128, dtype=np.float32).astype(ml_dtypes.bfloat16)
    c["c_identf"] = np.eye(128, dtype=np.float32)
    p = np.arange(128)
    inv = (10000.0 ** (-(np.arange(0, 64, 2, dtype=np.float32)) / 64)).astype(np.float32)
    c["c_invf"] = inv[p % 32].reshape(128, 1).astype(np.float32)
    c["c_pos"] = np.broadcast_to(np.arange(S_LEN, dtype=np.float32), (128, S_LEN)).copy()
    mrow = (p >= 64).astype(np.float32).reshape(1, 128)
    mneg = np.where(p < 64, -30000.0, 0.0).astype(np.float32).reshape(1, 128)
    c["c_mask2"] = np.concatenate([mrow, mneg], 0).astype(ml_dtypes.bfloat16)
    sc = -math.exp(-0.5)
    j = p[:, None]; i = p[None, :]
    c["c_tri"] = np.stack([(j <= i) * sc, (j < i) * sc, (j > i) * sc], 1).astype(np.float32)
    c["c_mskL"] = (j > i).astype(np.float32)
    c["c_msk"] = np.stack([(j < i), (j <= i)], 1).astype(np.float32)
    blk = ((j // 64) == (i // 64)).astype(np.float32)
    c["c_blk"] = blk
    c["c_hsel"] = np.stack([(p < 64), (p >= 64)], 1).astype(np.float32)
    return c


class LoadCast:
    def __init__(self, nc, S, stage):
        self.nc, self.S, self.stage, self.i = nc, S, stage, 0
        self.skey = "stage"

    def pieces(self, W, src, nchunk, c0, ncols, key, r0=0, eng="act"):
        per = (4 * 512) // ncols
        out = []
        for g0 in range(0, nchunk, per):
            n = min(per, nchunk - g0)
            out.append(lambda g0=g0, n=n: self._piece(W, src, g0, n, c0, ncols, key, r0, eng))
        return out

    def _piece(self, W, src, g0, n, c0, ncols, key, r0, eng):
        nc, S = self.nc, self.S
        b = self.i % len(self.stage)
        self.i += 1
        st = self.stage[b][:].rearrange("p a b -> p (a b)")[:, 0:n * ncols].rearrange("p (a b) -> p a b", a=n)
        sap = src[r0 + g0 * 128:r0 + (g0 + n) * 128, c0:c0 + ncols].rearrange("(c p) n -> p c n", p=128)
        S.dma("sp", lambda: nc.sync.dma_start(out=st, in_=sap), f"ld{self.skey}{b}", writes=[f"{self.skey}{b}"], serial=True)
        if eng == "dve":
            S.op("dve", lambda: nc.vector.tensor_copy(out=W[:, g0:g0 + n, :], in_=st), reads=[f"{self.skey}{b}"], writes=[key])
        else:
            S.op("act", lambda: nc.scalar.copy(out=W[:, g0:g0 + n, :], in_=st), reads=[f"{self.skey}{b}"], writes=[key])

    def load(self, W, src, nchunk, c0, ncols, key, r0=0, eng="act"):
        nc, S = self.nc, self.S
        per = (4 * 512) // ncols
        for g0 in range(0, nchunk, per):
            n = min(per, nchunk - g0)
            b = self.i % len(self.stage)
            self.i += 1
            st = self.stage[b][:].rearrange("p a b -> p (a b)")[:, 0:n * ncols].rearrange("p (a b) -> p a b", a=n)
            sap = src[r0 + g0 * 128:r0 + (g0 + n) * 128, c0:c0 + ncols].rearrange("(c p) n -> p c n", p=128)
            S.dma("sp", lambda st=st, sap=sap: nc.sync.dma_start(out=st, in_=sap), f"ld{self.skey}{b}", writes=[f"{self.skey}{b}"], serial=True)
            if eng == "dve":
                S.op("dve", lambda st=st, W=W, g0=g0, n=n: nc.vector.tensor_copy(out=W[:, g0:g0 + n, :], in_=st),
                     reads=[f"{self.skey}{b}"], writes=[key])
            else:
                S.op("act", lambda st=st, W=W, g0=g0, n=n: nc.scalar.copy(out=W[:, g0:g0 + n, :], in_=st),
                     reads=[f"{self.skey}{b}"], writes=[key])


def build(debug=None, stop_after=None):
    debug = debug or []
    nc = bass.Bass("TRN2", target_bir_lowering=False)
    x_in = nc.dram_tensor("x", [S_LEN, D], F32, kind="ExternalInput").ap()
    P = {}
    for name, shp in PARAM_SPECS:
        P[name] = nc.dram_tensor(name, shp, F32, kind="ExternalInput").ap()
    for name, (o_, l_) in SMALL_OFF.items():
        P[name] = P["blob"][o_:o_ + l_]
    C = {}
    for name, arr in host_consts().items():
        dt = BF16 if arr.dtype == ml_dtypes.bfloat16 else F32
        C[name] = nc.dram_tensor(name, list(arr.shape), dt, kind="ExternalInput").ap()
    out = nc.dram_tensor("out", [S_LEN, D], F32, kind="ExternalOutput").ap()
    dbg_out = {}

    es = ExitStack()
    with es:
        sems = [es.enter_context(nc.semaphore(f"s{i}")) for i in range(96)]
        S = Sched(nc, sems)

        def sb(name, shape, dt):
            return es.enter_context(nc.sbuf_tensor(name, shape, dt))

        def dbg(name, ap_fn, shape, dt, reads):
            if name in debug:
                d = nc.dram_tensor("dbg_" + name, shape, dt, kind="ExternalOutput").ap()
                dbg_out[name] = d
                S.dma("sp", lambda: nc.sync.dma_start(out=d, in_=ap_fn()), "dbg", reads=reads)

        idt = sb("idt", [128, 128], BF16)
        gmix = sb("gmix", [128, 8], F32)
        mid = es.enter_context(ExitStack())

        def sbm(name, shape, dt):
            return mid.enter_context(nc.sbuf_tensor(name, shape, dt))
        hT = sbm("hT", [128, 8, S_LEN + OFF], BF16)
        S.dma("sp", lambda: nc.sync.dma_start(out=idt[:], in_=C["c_ident"]), "const", writes=["idt"])
        S.dma("sp", lambda: nc.sync.dma_start(out=gmix[:], in_=P["norm_mix_g"].rearrange("(c p) -> p c", p=128),
                                              allow_slow_non_contiguous=True), "const", writes=["gmix"])
        S.op("pool", lambda: nc.gpsimd.memset(hT[:, :, 0:OFF], 0.0), writes=["hTpad"])

        with ExitStack() as ph:
            def psb(name, shape, dt):
                return ph.enter_context(nc.sbuf_tensor(name, shape, dt))
            xt = [psb(f"xt{i}", [128, 1024], F32) for i in range(2)]
            xn = [psb(f"xn{i}", [128, 1024], BF16) for i in range(2)]
            junk = psb("junk", [128, 1024], BF16)
            ss = psb("ss", [128, 16], F32)
            rs = psb("rs", [128, 16], F32)
            rstd = psb("rstd", [128, 16], F32)
            tp = [ph.enter_context(nc.psum_tensor(f"tp{i}", [128, 8, 128], BF16)) for i in range(2)]
            pend0 = []
            for t in range(NT):
                b = t % 2
                S.dma("sp", lambda t=t, b=b: nc.sync.dma_start(out=xt[b][:], in_=x_in[t * 128:(t + 1) * 128, :]),
                      f"x{t}", writes=[f"xt{b}"])
                S.op("act", lambda t=t, b=b: nc.scalar.activation(out=junk[:], in_=xt[b][:], func=AF.Square,
                                                                  accum_out=ss[:, t:t + 1]),
                     reads=[f"xt{b}"], writes=["junk", f"ss{t}"])
                S.op("act", lambda t=t: nc.scalar.activation(out=rs[:, t:t + 1], in_=ss[:, t:t + 1], func=AF.Sqrt,
                                                             scale=1.0 / 1024, bias=1e-6),
                     reads=[f"ss{t}"], writes=[f"rs{t}"])
                S.op("dve", lambda t=t: nc.vector.reciprocal(out=rstd[:, t:t + 1], in_=rs[:, t:t + 1]),
                     reads=[f"rs{t}"], writes=[f"rstd{t}"])
                S.op("dve", lambda t=t, b=b: nc.vector.tensor_scalar(out=xn[b][:], in0=xt[b][:], scalar1=rstd[:, t:t + 1],
                                                                     scalar2=None, op0=ALU.mult),
                     reads=[f"xt{b}", f"rstd{t}"], writes=[f"xn{b}"])
                for dc in range(8):
                    S.op("pe", lambda b=b, dc=dc: nc.tensor.transpose(out=tp[b][:, dc, :], in_=xn[b][:, dc * 128:(dc + 1) * 128],
                                                                      identity=idt[:]),
                         reads=[f"xn{b}", "idt"], writes=[f"tp{b}"])
                pend0.append(lambda t=t, b=b: S.op("dve", lambda: nc.vector.tensor_tensor(
                    out=hT[:, :, OFF + t * 128:OFF + (t + 1) * 128], in0=tp[b][:],
                    in1=gmix[:].unsqueeze(2).to_broadcast([128, 8, 128]), op=ALU.mult),
                    reads=[f"tp{b}", "gmix"], writes=[f"hT{t}"]))
                if len(pend0) > 1:
                    pend0.pop(0)()
            while pend0:
                pend0.pop(0)()
            run_block(nc, S)
        HT_ALL = [f"hT{t}" for t in range(NT)] + ["hTpad"]
        dbg("hT", lambda: hT[:, :, OFF:OFF + S_LEN], [128, 8, S_LEN], BF16, HT_ALL)
        if stop_after == "0":
            run_block(nc, S, ["dbg"])
            return nc, dbg_out

        yaT = sbm("yaT", [128, 4, S_LEN], BF16)
        if 'skipA' not in SK:
            phase_A(nc, S, P, C, hT, yaT, HT_ALL, dbg, stop_after)
        if stop_after in ('A1','A2'):
            run_block(nc, S, ['dbg'])
            return nc, dbg_out
        dbg("yaT", lambda: yaT[:], [128, 4, S_LEN], BF16, ["yaT"])
        if stop_after == "A":
            run_block(nc, S, ["dbg"])
            return nc, dbg_out
        ybT = sbm("ybT", [128, 4, S_LEN], BF16)
        phase_B(nc, S, P, C, hT, ybT, idt, HT_ALL, dbg, stop_after)
        dbg("ybT", lambda: ybT[:], [128, 4, S_LEN], BF16, ["ybT"])
        if stop_after in ("B", "B1"):
            run_block(nc, S, ["dbg"])
            return nc, dbg_out
        xmid = nc.dram_tensor("xmid_scratch", [S_LEN, D], F32, kind="Internal").ap()
        S.pe_inorder = True
        phase_C(nc, S, P, C, x_in, xmid, hT, yaT, ybT, idt, gmix)
        if stop_after == "C":
            if "xmid" in debug:
                d = nc.dram_tensor("dbg_xmid", [S_LEN, D], F32, kind="ExternalOutput").ap()
                S.dma("sp", lambda: nc.sync.dma_start(out=d, in_=xmid), "dbg", reads=[f"xmid{t}" for t in range(NT)])
            run_block(nc, S, ["dbg"])
            return nc, dbg_out
        mid.close()
        phase_D(nc, S, P, C, xmid, out, idt)
    return nc, dbg_out


def phase_A(nc, S, P, C, hT, yaT, HT_ALL, dbg, stop_after=None):
    LAMBDA_INIT = 0.8 - 0.6 * math.exp(-0.3 * 0)
    with ExitStack() as ph:
        def psb(name, shape, dt):
            return ph.enter_context(nc.sbuf_tensor(name, shape, dt))

        def pps(name):
            return ph.enter_context(nc.psum_tensor(name, [128, 512], F32))
        Wa = psb("Wa", [128, 8, 512], BF16)
        War = psb("War", [128, 8, 512], BF16)
        Wv = psb("Wv", [128, 8, 512], BF16)
        QrT = psb("QrT", [128, 4, S_LEN], BF16)
        KrT = psb("KrT", [128, 4, S_LEN], BF16)
        Vtok = psb("Vtok", [128, NT, 512], BF16)
        cosT = psb("cosT", [128, S_LEN], F32)
        sinT = psb("sinT", [128, S_LEN], F32)
        ti = psb("ti", [128, 512], I32)
        invf = psb("invf", [128, 1], F32)
        mask2 = psb("mask2", [1, 2, 128], BF16)
        ones_bf = psb("ones_bf", [128, 128], BF16)
        ones_f = psb("ones_f", [128, 128], F32)
        lqk = psb("lqk", [128, 256], F32)
        lprod = psb("lprod", [128, 2, 64], F32)
        lsum = psb("lsum", [128, 2], F32)
        lexp = psb("lexp", [128, 2], F32)
        neglam = psb("neglam", [128, 1], F32)
        gsub = psb("gsub", [128, 1], F32)
        gsub2 = psb("gsub2", [128, 1], F32)
        mhalf = psb("mhalf", [128, 512], F32)
        pT = [psb(f"pT{i}", [128, 512], BF16) for i in range(3)]
        t1 = [psb(f"ra{i}", [128, 512], F32) for i in range(2)]
        t2 = [psb(f"rb{i}", [128, 512], F32) for i in range(2)]
        rc = psb("rc", [128, 512], F32)
        s0s = psb("s0s", [128, 512], F32)
        Pq = psb("Pq", [128, 512], F32)
        P2 = psb("P2", [128, 512], F32)
        on = [psb(f"on{i}", [128, 512], F32) for i in range(2)]
        yv = psb("yv", [128, 512], F32)
        ysq = psb("ysq", [128, 512], F32)
        vv = psb("vv", [128, 512], F32)
        ri = psb("ri", [128, 512], F32)
        sc = [pps(f"sc{i}") for i in range(3)]
        oT = [pps(f"oT{i}") for i in range(2)]
        sm = [pps(f"sm{i}") for i in range(2)]
        ssq = pps("ssq")

        w_in = P["w_in"]
        stage = [psb(f"stage{i}", [128, 4, 512], F32) for i in range(2)]
        ld = LoadCast(nc, S, stage)
        ld.load(Wv, w_in, 8, 1024, 512, "Wv")
        if 'allA' in SK:
            run_block(nc, S, ['dbg'] if 'dbg' in S.groups else [])
            return
        if 'noinvf' not in SK:
            S.dma("sp", lambda: nc.sync.dma_start(out=invf[:], in_=C["c_invf"]), "cA", writes=["invf"])
        if 'nopos' not in SK:
            S.dma("sp", lambda: nc.sync.dma_start(out=sinT[:], in_=C["c_pos"]), "cA", writes=["sinT"])
        if 'nomask' not in SK:
          S.dma("sp", lambda: nc.sync.dma_start(out=mask2[:], in_=C["c_mask2"].rearrange("(o a) b -> o a b", o=1)), "cA", writes=["mask2"])
        o_lq = SMALL_OFF["da_lq1"][0]
        S.dma("sp", lambda: nc.sync.dma_start(out=lqk[:], in_=P["blob"][o_lq:o_lq + 256].partition_broadcast(128)), "cA", writes=["lqk"])
        if 'nogsub' not in SK:
          S.dma("sp", lambda: nc.sync.dma_start(out=gsub[:], in_=P["da_subln_g"].rearrange("(p o) -> p o", o=1)),
              "cA", writes=["gsub"])
        if 'c2' in SK:
            run_block(nc, S, ['dbg'] if 'dbg' in S.groups else [])
            return
        S.op("pool", lambda: nc.gpsimd.memset(ones_bf[:], 1.0), writes=["ones_bf"])
        S.op("pool", lambda: nc.gpsimd.memset(ones_f[:], 1.0), writes=["ones_f"])
        S.op("pool", lambda: nc.gpsimd.memset(mhalf[:], -0.5), writes=["mhalf"])
        def sec_lam():
            for i in range(2):
                S.op("dve", lambda i=i: nc.vector.tensor_tensor(out=lprod[:, i, :], in0=lqk[:, 128 * i:128 * i + 64], in1=lqk[:, 128 * i + 64:128 * i + 128], op=ALU.mult),
                     reads=["lqk"], writes=["lprod"])
            S.op("dve", lambda: nc.vector.reduce_sum(out=lsum[:], in_=lprod[:], axis=AX.X), reads=["lprod"], writes=["lsum"])
            S.op("act", lambda: nc.scalar.activation(out=lexp[:], in_=lsum[:], func=AF.Exp), reads=["lsum"], writes=["lexp"])
            S.op("dve", lambda: nc.vector.tensor_tensor(out=neglam[:], in0=lexp[:, 1:2], in1=lexp[:, 0:1], op=ALU.subtract),
                 reads=["lexp"], writes=["neglam"])
            S.op("dve", lambda: nc.vector.tensor_scalar(out=neglam[:], in0=neglam[:], scalar1=-LAMBDA_INIT, scalar2=None, op0=ALU.add),
                 reads=["neglam"], writes=["neglam"])
            S.op("dve", lambda: nc.vector.tensor_scalar(out=gsub2[:], in0=gsub[:], scalar1=1.0 - LAMBDA_INIT, scalar2=None, op0=ALU.mult),
                 reads=["gsub"], writes=["gsub2"])
        def sec_rope():
            TWO_PI = 2.0 * math.pi
            S.op("dve", lambda: nc.vector.tensor_scalar(out=sinT[:], in0=sinT[:], scalar1=invf[:, 0:1], scalar2=None, op0=ALU.mult),
                 reads=["sinT", "invf"], writes=["sinT"])
            tf = t1[0]
            for j in range(4):
                cs = slice(j * 512, (j + 1) * 512)
                for tab, shift, nm in ((cosT, math.pi / 2, "cosT"), (sinT, 0.0, "sinT")):
                    S.op("dve", lambda shift=shift, cs=cs: nc.vector.tensor_scalar(out=tf[:], in0=sinT[:, cs], scalar1=shift, scalar2=1.0 / TWO_PI,
                                                                                   op0=ALU.add, op1=ALU.mult),
                         reads=["sinT"], writes=["t10"])
                    S.op("dve", lambda: nc.vector.tensor_copy(out=ti[:], in_=tf[:]), reads=["t10"], writes=["ti"])
                    S.op("dve", lambda: nc.vector.tensor_copy(out=tf[:], in_=ti[:]), reads=["ti"], writes=["t10"])
                    S.op("dve", lambda tab=tab, cs=cs: nc.vector.scalar_tensor_tensor(out=tab[:, cs], in0=tf[:], scalar=-TWO_PI, in1=sinT[:, cs],
                                                                                      op0=ALU.mult, op1=ALU.add),
                         reads=["t10", "sinT"], writes=[nm])
                    S.op("dve", lambda tab=tab, cs=cs, shift=shift: nc.vector.tensor_scalar(out=tab[:, cs], in0=tab[:, cs], scalar1=shift, scalar2=-math.pi,
                                                                                            op0=ALU.add, op1=ALU.max),
                         reads=[nm], writes=[nm])
                    S.op("dve", lambda tab=tab, cs=cs: nc.vector.tensor_scalar(out=tab[:, cs], in0=tab[:, cs], scalar1=math.pi, scalar2=None, op0=ALU.min),
                         reads=[nm], writes=[nm])
                    S.op("act", lambda tab=tab, cs=cs: nc.scalar.activation(out=tab[:, cs], in_=tab[:, cs], func=AF.Sin), reads=[nm], writes=[nm])
        def sec_rot(nm):
            Wv4 = Wa[:].rearrange("p c (j h e) -> p (c j) h e", j=8, h=2, e=32)
            Wr4 = War[:].rearrange("p c (j h e) -> p (c j) h e", j=8, h=2, e=32)
            S.op("dve", lambda: nc.vector.tensor_scalar(out=Wr4[:, :, 0, :], in0=Wv4[:, :, 1, :], scalar1=-1.0, scalar2=None, op0=ALU.mult),
                 reads=["Wa"], writes=["War0"])
            S.op("dve", lambda: nc.vector.tensor_copy(out=Wr4[:, :, 1, :], in_=Wv4[:, :, 0, :]), reads=["Wa"], writes=["War1"])
        if 'lam' not in SK: sec_lam()
        if 'rope' not in SK: sec_rope()
        dbg('cosT', lambda: cosT[:], [128, S_LEN], F32, ['cosT'])
        dbg('sinT', lambda: sinT[:], [128, S_LEN], F32, ['sinT'])
        dbg('neglam', lambda: neglam[:], [128, 1], F32, ['neglam'])
        if stop_after == 'A1':
            run_block(nc, S, ['dbg'])
            return
        it = 0
        for nm, c0, dst in (("Wk", 512, KrT), ("Wq", 0, QrT)):
            ld.load(Wa, w_in, 8, c0, 512, "Wa")
            sec_rot(nm)
            W, Wr = Wa, War
            for h in range(4):
                for tg in range(4):
                    b = it % 2
                    it += 1
                    pa, pb = (sc[0], sc[1]) if b == 0 else (oT[0], oT[1])
                    pak, pbk = ("sc0", "sc1") if b == 0 else ("oT0", "oT1")
                    for dc in range(8):
                        S.op("pe", lambda W=W, h=h, dc=dc, tg=tg, pa=pa: nc.tensor.matmul(
                            pa[:], lhsT=W[:, dc, h * 128:(h + 1) * 128], rhs=hT[:, dc, OFF + tg * 512:OFF + (tg + 1) * 512],
                            start=(dc == 0), stop=(dc == 7)), reads=["Wa"] + HT_ALL, writes=[pak])
                    for dc in range(8):
                        S.op("pe", lambda Wr=Wr, h=h, dc=dc, tg=tg, pb=pb: nc.tensor.matmul(
                            pb[:], lhsT=Wr[:, dc, h * 128:(h + 1) * 128], rhs=hT[:, dc, OFF + tg * 512:OFF + (tg + 1) * 512],
                            start=(dc == 0), stop=(dc == 7)), reads=["War0", "War1"] + HT_ALL, writes=[pbk])
                    S.op("dve", lambda b=b, tg=tg, pa=pa: nc.vector.tensor_tensor(out=t1[b][:], in0=pa[:], in1=cosT[:, tg * 512:(tg + 1) * 512], op=ALU.mult),
                         reads=[pak, "cosT"], writes=[f"t1{b}"])
                    S.op("dve", lambda b=b, tg=tg, pb=pb: nc.vector.tensor_tensor(out=t2[b][:], in0=pb[:], in1=sinT[:, tg * 512:(tg + 1) * 512], op=ALU.mult),
                         reads=[pbk, "sinT"], writes=[f"t2{b}"])
                    S.op("dve", lambda b=b, tg=tg, h=h, dst=dst: nc.vector.tensor_tensor(out=dst[:, h, tg * 512:(tg + 1) * 512], in0=t1[b][:], in1=t2[b][:], op=ALU.add),
                         reads=[f"t1{b}", f"t2{b}"], writes=[("qk", id(dst), h, tg)])
        for t in range(NT):
            pv, pvk = (sc[2], "sc2") if t % 2 == 0 else (sm[0], "sm0")
            for dc in range(8):
                S.op("pe", lambda t=t, dc=dc, pv=pv: nc.tensor.matmul(pv[:], lhsT=hT[:, dc, OFF + t * 128:OFF + (t + 1) * 128], rhs=Wv[:, dc, :],
                                                                      start=(dc == 0), stop=(dc == 7)), reads=["Wv"] + HT_ALL, writes=[pvk])
            S.op("act", lambda t=t, pv=pv: nc.scalar.copy(out=Vtok[:, t, :], in_=pv[:]), reads=[pvk], writes=[f"V{t}"])
        dbg("QrT", lambda: QrT[:], [128, 4, S_LEN], BF16, [("qk", id(QrT), h, tg) for h in range(4) for tg in range(4)])
        dbg("KrT", lambda: KrT[:], [128, 4, S_LEN], BF16, [("qk", id(KrT), h, tg) for h in range(4) for tg in range(4)])
        dbg("Vtok", lambda: Vtok[:], [128, NT, 512], BF16, [f"V{t}" for t in range(NT)])

        if stop_after == 'A2':
            run_block(nc, S, ['dbg'])
            return
        LOOK = 2
        its = []
        fi = 0
        for h in range(4):
            for qc in range(4):
                for c in range(2):
                    fb = fi % 2
                    fi += 1
                    nk = 4 * qc + 4
                    for kt in range(nk):
                        its.append((h, qc, c, fb, nk, kt))

        def emit_S(i):
            h, qc, c, fb, nk, kt = its[i]
            off = max(0, kt - 4 * qc) * 128
            diag = kt >= 4 * qc
            sb_ = i % 3
            qk_reads = [("qk", id(QrT), h, qc), ("qk", id(KrT), h, kt // 4)]
            S.op("pe", lambda: nc.tensor.matmul(
                sc[sb_][:, off:512], lhsT=KrT[c * 64:(c + 1) * 64, h, kt * 128:(kt + 1) * 128],
                rhs=QrT[c * 64:(c + 1) * 64, h, qc * 512 + off:(qc + 1) * 512], start=True, stop=not diag),
                reads=qk_reads, writes=[f"sc{sb_}"])
            if diag:
                S.op("pe", lambda: nc.tensor.matmul(
                    sc[sb_][:, off:off + 128], lhsT=mask2[0:1, 0, :], rhs=mask2[0:1, 1, :], start=False, stop=True),
                    reads=["mask2"], writes=[f"sc{sb_}"])
            S.op("act", lambda: nc.scalar.activation(out=pT[sb_][:, off:512], in_=sc[sb_][:, off:512], func=AF.Exp, scale=0.125, bias=-4.0),
                 reads=[f"sc{sb_}"], writes=[f"pT{sb_}"])

        def emit_fin2(h, qc):
            S.op("pe", lambda: nc.tensor.matmul(ssq[:], lhsT=ones_f[:], rhs=ysq[:], start=True, stop=True),
                 reads=["ysq", "ones_f"], writes=["ssq"])
            S.op("dve", lambda: nc.vector.scalar_tensor_tensor(out=vv[:], in0=ssq[:], scalar=1.0 / 128, in1=P2[:], op0=ALU.mult, op1=ALU.add),
                 reads=["ssq", "P2"], writes=["vv"])
            S.op("act", lambda: nc.scalar.activation(out=ri[:], in_=vv[:], func=AF.Ln), reads=["vv"], writes=["ri"])
            S.op("act", lambda: nc.scalar.activation(out=ri[:], in_=ri[:], func=AF.Exp, scale=-0.5), reads=["ri"], writes=["ri"])
            S.op("dve", lambda: nc.vector.scalar_tensor_tensor(out=yaT[:, h, qc * 512:(qc + 1) * 512], in0=yv[:], scalar=gsub2[:, 0:1],
                                                               in1=ri[:], op0=ALU.mult, op1=ALU.mult),
                 reads=["yv", "ri", "gsub2"], writes=["yaT"])

        deferred = []

        def emit_PV(i):
            h, qc, c, fb, nk, kt = its[i]
            off = max(0, kt - 4 * qc) * 128
            sb_ = i % 3
            S.op("pe", lambda: nc.tensor.matmul(
                oT[fb][:, off:512], lhsT=Vtok[:, kt, h * 128:(h + 1) * 128], rhs=pT[sb_][:, off:512],
                start=(kt == 0), stop=(kt == nk - 1)), reads=[f"pT{sb_}", f"V{kt}"], writes=[f"oT{fb}"])
            S.op("pe", lambda: nc.tensor.matmul(
                sm[fb][:, off:512], lhsT=ones_bf[:], rhs=pT[sb_][:, off:512],
                start=(kt == 0), stop=(kt == nk - 1)), reads=[f"pT{sb_}", "ones_bf"], writes=[f"sm{fb}"])
            if kt == nk - 1:
                if c == 0:
                    S.op("dve", lambda: nc.vector.tensor_copy(out=on[0][:], in_=oT[fb][:]), reads=[f"oT{fb}"], writes=["on0"])
                    S.op("act", lambda: nc.scalar.copy(out=s0s[:], in_=sm[fb][:]), reads=[f"sm{fb}"], writes=["s0s"])
                else:
                    S.op("dve", lambda: nc.vector.tensor_tensor(out=on[1][:], in0=on[0][:], in1=sm[fb][:], op=ALU.mult), reads=["on0", f"sm{fb}"], writes=["on1"])
                    S.op("dve", lambda: nc.vector.tensor_tensor(out=rc[:], in0=oT[fb][:], in1=s0s[:], op=ALU.mult), reads=[f"oT{fb}", "s0s"], writes=["rc"])
                    S.op("dve", lambda: nc.vector.tensor_tensor(out=Pq[:], in0=s0s[:], in1=sm[fb][:], op=ALU.mult), reads=["s0s", f"sm{fb}"], writes=["Pq"])
                    S.op("dve", lambda: nc.vector.scalar_tensor_tensor(out=yv[:], in0=rc[:], scalar=neglam[:, 0:1], in1=on[1][:], op0=ALU.mult, op1=ALU.add),
                         reads=["rc", "on1", "neglam"], writes=["yv"])
                    S.op("dve", lambda: nc.vector.tensor_tensor(out=ysq[:], in0=yv[:], in1=yv[:], op=ALU.mult), reads=["yv"], writes=["ysq"])
                    S.op("dve", lambda: nc.vector.scalar_tensor_tensor(out=P2[:], in0=Pq[:], scalar=1e-5, in1=Pq[:], op0=ALU.mult, op1=ALU.mult),
                         reads=["Pq"], writes=["P2"])
                    deferred.append([3, lambda: emit_fin2(h, qc)])

        n_it = len(its)
        for i in range(n_it + LOOK):
            if i < n_it:
                emit_S(i)
            if i - LOOK >= 0:
                emit_PV(i - LOOK)
            for d in deferred:
                d[0] -= 1
            for d in [d for d in deferred if d[0] <= 0]:
                d[1]()
                deferred.remove(d)
        for d in deferred:
            d[1]()
        run_block(nc, S)


def phase_B(nc, S, P, C, hT, ybT, idt, HT_ALL, dbg, stop_after=None):
    RW0 = 1536
    w_in = P["w_in"]
    S.pe_inorder = 'nobf1' not in SK
    with ExitStack() as ph:
        def psb(name, shape, dt):
            return ph.enter_context(nc.sbuf_tensor(name, shape, dt))
        rT = psb("rT", [128, 4, S_LEN], BF16)
        kT = psb("kT", [128, 4, S_LEN], BF16)
        Vt = psb("Vt", [128, NT, 512], BF16)
        tanhwdT = psb("tanhwdT", [64, S_LEN], BF16)
        adT = psb("adT", [64, S_LEN], BF16)
        sgdT = psb("sgdT", [128, S_LEN], BF16)
        w0_bc = psb("w0_bc", [128, 512], F32)
        lng_bc = psb("lng_bc", [128, 512], F32)
        lnb_bc = psb("lnb_bc", [128, 512], F32)
        cols = psb("cols", [128, 4, 4], F32)
        omka = psb("omka", [128, 4], F32)
        w2_bf = psb("w2_bf", [64, 512], BF16)
        a2_bf = psb("a2_bf", [64, 512], BF16)
        g2_bf = psb("g2_bf", [128, 512], BF16)
        tri = psb("tri", [128, 3, 128], F32)
        mskA = psb("mskA", [128, 2, 128], F32)
        mskL = psb("mskL", [128, 128], F32)
        blk = psb("blk", [128, 128], F32)
        hsel = psb("hsel", [128, 2], F32)
        RKt = psb("RKt", [128, 4, 2], F32)
        pA = [ph.enter_context(nc.psum_tensor(f"pA{i}", [128, 512], F32)) for i in range(4)]
        pB = [ph.enter_context(nc.psum_tensor(f"pB{i}", [128, 512], F32)) for i in range(4)]
        _pn = {}
        for i in range(4):
            _pn[id(pA[i])] = f"pA{i}"
            _pn[id(pB[i])] = f"pB{i}"

        def pn(t):
            return _pn[id(t)]

        def cdma(out_ap, in_ap, key, slow=False):
            if 'no_' + key in SK:
                return
            S.dma("sp", lambda: nc.sync.dma_start(out=out_ap, in_=in_ap, allow_slow_non_contiguous=slow), "cB", writes=[key])
        cdma(w0_bc[:], P["rw_w0"].partition_broadcast(128), "w0_bc")
        cdma(lng_bc[:], P["rw_ln_g"].partition_broadcast(128), "lng_bc")
        cdma(lnb_bc[:], P["rw_ln_b"].partition_broadcast(128), "lnb_bc")
        for i, nm in enumerate(("rw_a0", "rw_k_k", "rw_k_a", "rw_r_k")):
            cdma(cols[:, i, :], P[nm].rearrange("(c p) -> p c", p=128), "cols", slow=True)
        cdma(tri[:], C["c_tri"], "tri")
        cdma(mskA[:], C["c_msk"], "mskA")
        cdma(mskL[:], C["c_mskL"], "mskL")
        cdma(blk[:], C["c_blk"], "blk")
        cdma(hsel[:], C["c_hsel"], "hsel")
        S.op("dve", lambda: nc.vector.tensor_scalar(out=omka[:], in0=cols[:, 2, :], scalar1=-1.0, scalar2=1.0, op0=ALU.mult, op1=ALU.add),
             reads=["cols"], writes=["omka"])
        for hp in range(4):
            S.op("dve", lambda hp=hp: nc.vector.tensor_scalar(out=RKt[:, hp, :], in0=hsel[:], scalar1=cols[:, 3, hp:hp + 1], scalar2=None, op0=ALU.mult),
                 reads=["cols", "hsel"], writes=["RKt"])

        with ExitStack() as sc1:
            def s1b(name, shape, dt):
                return sc1.enter_context(nc.sbuf_tensor(name, shape, dt))
            mu_bc = s1b("mu_bc", [128, 1792], F32)
            omu_bc = s1b("omu_bc", [128, 1792], F32)
            stage = [s1b(f"stageB{i}", [128, 4, 512], F32) for i in range(2)]
            W1 = s1b("W1", [128, 8, 512], BF16)
            W2 = s1b("W2", [128, 8, 512], BF16)
            lorf = s1b("lorf", [128, 3, 512], F32)
            cdma(mu_bc[:], P["rw_mu"].partition_broadcast(128), "mu_bc")
            S.op("dve", lambda: nc.vector.tensor_scalar(out=omu_bc[:], in0=mu_bc[:], scalar1=-1.0, scalar2=1.0, op0=ALU.mult, op1=ALU.add),
                 reads=["mu_bc"], writes=["omu_bc"])
            cdma(lorf[0:64, 0, :], P["rw_w2"], "lorf")
            cdma(lorf[0:64, 1, :], P["rw_a2"], "lorf")
            cdma(lorf[:, 2, :], P["rw_g2"], "lorf")
            S.op("dve", lambda: nc.vector.tensor_copy(out=w2_bf[:], in_=lorf[0:64, 0, :]), reads=["lorf"], writes=["w2_bf"])
            S.op("dve", lambda: nc.vector.tensor_copy(out=a2_bf[:], in_=lorf[0:64, 1, :]), reads=["lorf"], writes=["a2_bf"])
            S.op("dve", lambda: nc.vector.tensor_copy(out=g2_bf[:], in_=lorf[:, 2, :]), reads=["lorf"], writes=["g2_bf"])
            stage_i = [0]

            def load_shift(c0, ncols):
                per = 2048 // ncols
                for g0 in range(0, 8, per):
                    n = min(per, 8 - g0)
                    b = stage_i[0] % 2
                    stage_i[0] += 1
                    st = stage[b][:].rearrange("p a b -> p (a b)")[:, 0:n * ncols].rearrange("p (a b) -> p a b", a=n)
                    sap = w_in[g0 * 128:(g0 + n) * 128, c0:c0 + ncols].rearrange("(c p) n -> p c n", p=128)
                    S.dma("sp", lambda st=st, sap=sap: nc.sync.dma_start(out=st, in_=sap), f"ldstageB{b}", writes=[f"stageB{b}"], serial=True)
                    m0 = c0 - RW0
                    S.op("dve", lambda st=st, g0=g0, n=n, m0=m0: nc.vector.tensor_tensor(
                        out=W1[:, g0:g0 + n, 0:ncols], in0=st, in1=omu_bc[:, m0:m0 + ncols].unsqueeze(1).to_broadcast([128, n, ncols]), op=ALU.mult),
                        reads=[f"stageB{b}", "omu_bc"], writes=["W1"])
                    S.op("dve", lambda st=st, g0=g0, n=n, m0=m0: nc.vector.tensor_tensor(
                        out=W2[:, g0:g0 + n, 0:ncols], in0=st, in1=mu_bc[:, m0:m0 + ncols].unsqueeze(1).to_broadcast([128, n, ncols]), op=ALU.mult),
                        reads=[f"stageB{b}", "mu_bc"], writes=["W2"])

            def proj_fm(ps, f0, M, tg):
                for dc in range(8):
                    S.op("pe", lambda dc=dc: nc.tensor.matmul(ps[0:M, :], lhsT=W1[:, dc, f0:f0 + M], rhs=hT[:, dc, OFF + tg * 512:OFF + (tg + 1) * 512],
                                                              start=(dc == 0), stop=False), reads=["W1"] + HT_ALL, writes=[pn(ps)])
                for dc in range(8):
                    S.op("pe", lambda dc=dc: nc.tensor.matmul(ps[0:M, :], lhsT=W2[:, dc, f0:f0 + M], rhs=hT[:, dc, OFF - 1 + tg * 512:OFF - 1 + (tg + 1) * 512],
                                                              start=False, stop=(dc == 7)), reads=["W2"] + HT_ALL, writes=[pn(ps)])
            pi = [0]

            def nextps():
                pi[0] += 1
                return (pA + pB)[pi[0] % 8]
            if 'b1a' in SK:
                run_block(nc, S)
                return
            load_shift(RW0 + 1536, 256)
            for tg in range(4):
                cs = slice(tg * 512, (tg + 1) * 512)
                ps = nextps()
                proj_fm(ps, 0, 64, tg)
                S.op("act", lambda ps=ps, cs=cs: nc.scalar.activation(out=tanhwdT[:, cs], in_=ps[0:64, :], func=AF.Tanh),
                     reads=[pn(ps)], writes=["tanhwdT"])
                ps = nextps()
                proj_fm(ps, 64, 64, tg)
                S.op("act", lambda ps=ps, cs=cs: nc.scalar.copy(out=adT[:, cs], in_=ps[0:64, :]), reads=[pn(ps)], writes=["adT"])
                ps = nextps()
                proj_fm(ps, 128, 128, tg)
                S.op("act", lambda ps=ps, cs=cs: nc.scalar.activation(out=sgdT[:, cs], in_=ps[:], func=AF.Sigmoid),
                     reads=[pn(ps)], writes=["sgdT"])
            if 'b1b' in SK:
                run_block(nc, S)
                return
            load_shift(RW0 + 1024, 512)
            for t in range(NT):
                ps = nextps()
                for dc in range(8):
                    S.op("pe", lambda ps=ps, t=t, dc=dc: nc.tensor.matmul(ps[:], lhsT=hT[:, dc, OFF + t * 128:OFF + (t + 1) * 128], rhs=W1[:, dc, :],
                                                                          start=(dc == 0), stop=False), reads=["W1"] + HT_ALL, writes=[pn(ps)])
                for dc in range(8):
                    S.op("pe", lambda ps=ps, t=t, dc=dc: nc.tensor.matmul(ps[:], lhsT=hT[:, dc, OFF - 1 + t * 128:OFF - 1 + (t + 1) * 128], rhs=W2[:, dc, :],
                                                                          start=False, stop=(dc == 7)), reads=["W2"] + HT_ALL, writes=[pn(ps)])
                S.op("act", lambda ps=ps, t=t: nc.scalar.copy(out=Vt[:, t, :], in_=ps[:]), reads=[pn(ps)], writes=[f"Vt{t}"])
            if 'b1c' in SK:
                run_block(nc, S)
                return
            for c0, dst, nm in ((RW0, rT, "rT"), (RW0 + 512, kT, "kT")):
                load_shift(c0, 512)
                for hp in range(4):
                    for tg in range(4):
                        ps = nextps()
                        proj_fm(ps, hp * 128, 128, tg)
                        eng = "act" if (hp + tg) % 2 == 0 else "dve"
                        if eng == "act":
                            S.op("act", lambda ps=ps, hp=hp, tg=tg, dst=dst: nc.scalar.copy(out=dst[:, hp, tg * 512:(tg + 1) * 512], in_=ps[:]),
                                 reads=[pn(ps)], writes=[(nm, hp, tg)])
                        else:
                            S.op("dve", lambda ps=ps, hp=hp, tg=tg, dst=dst: nc.vector.tensor_copy(out=dst[:, hp, tg * 512:(tg + 1) * 512], in_=ps[:]),
                                 reads=[pn(ps)], writes=[(nm, hp, tg)])
            S.barrier()
            run_block(nc, S)
        dbg("rT", lambda: rT[:], [128, 4, S_LEN], BF16, [])
        dbg("kT", lambda: kT[:], [128, 4, S_LEN], BF16, [])
        dbg("Vt", lambda: Vt[:], [128, NT, 512], BF16, [])
        dbg("sgdT", lambda: sgdT[:], [128, S_LEN], BF16, [])
        if stop_after == "B1":
            return

        hflat = hT[:]
        def hview(dc, n):
            return hflat[:, dc, 0:n]
        ARt = [hview(hp, 1024).rearrange("p (c t i) -> p c t i", c=4, t=2) for hp in range(4)]
        btl = [hflat[:, hp, 1024:1536] for hp in range(4)]
        ktl = [hflat[:, hp, 1536:2048] for hp in range(4)]
        BhT = [hview(4 + hp, 512).rearrange("p (c f) -> p c f", c=4) for hp in range(4)]
        KhT = [hflat[:, 4 + hp, 512:1024].rearrange("p (c f) -> p c f", c=4) for hp in range(4)]
        bhkh = [hflat[:, 4 + hp, 1024:2048].rearrange("p (t n) -> p t n", t=2) for hp in range(4)]
        with ExitStack() as sc2:
            def s2b(name, shape, dt):
                return sc2.enter_context(nc.sbuf_tensor(name, shape, dt))
            NTMP = 12
            idtf = s2b("idtf", [128, 128], F32)
            S.dma("sp", lambda: nc.sync.dma_start(out=idtf[:], in_=C["c_identf"]), "cB2", writes=["idtf"])
            TA = [[s2b(f"TA{par}_{i}", [128, 512], F32) for i in range(6)] for par in range(2)]
            TB = [s2b(f"TB_{i}", [128, 512], F32) for i in range(4)]
            gam = s2b("gam", [128, 4, 16], F32)
            bon = s2b("bon", [128, 16, 8], F32)
            MA = [s2b(f"MA{hp}", [128, 2, 2, 128], BF16) for hp in range(4)]
            MB = [s2b(f"MB{hp}", [128, 2, 2, 128], BF16) for hp in range(4)]
            N0 = [s2b(f"N0{hp}", [128, 2, 128], BF16) for hp in range(4)]
            NMs = [[s2b(f"NMs{hp}_{i}", [128, 2, 2, 128], BF16) for i in range(2)] for hp in range(4)]
            Pb = [[s2b(f"Pb{hp}_{i}", [128, 2, 128], BF16) for i in range(2)] for hp in range(4)]
            S_f = [s2b(f"S_f{hp}", [128, 2, 64], F32) for hp in range(4)]
            S_bf = [s2b(f"S_bf{hp}", [128, 2, 64], BF16) for hp in range(4)]
            X_bf = [s2b(f"X_bf{hp}", [128, 2, 64], BF16) for hp in range(4)]
            U_bf = [s2b(f"U_bf{hp}", [128, 2, 64], BF16) for hp in range(4)]
            Yt = [s2b(f"Yt{i}", [128, 512], F32) for i in range(2)]
            Y1 = s2b("Y1", [128, 512], F32)
            Y2 = s2b("Y2", [128, 512], F32)
            yb = s2b("yb", [128, 512], F32)
            st8 = s2b("st8", [128, 6, 8], F32)
            if os.environ.get('MK_KERNEL_OPLIMIT'):
                S.limit = int(os.environ['MK_KERNEL_OPLIMIT'])
            for hp in range(4):
                S.op("pool", lambda hp=hp: nc.gpsimd.memset(S_f[hp][:], 0.0), writes=[f"S_f{hp}"])
                S.op("pool", lambda hp=hp: nc.gpsimd.memset(S_bf[hp][:], 0.0), writes=[f"S_bf{hp}"])

            def v4(ap):
                return ap.rearrange("p (c i) -> p c i", c=4)

            def prep(tg, hp):
                S.pe_inorder = 'nobfp' not in SK
                if True:
                    cs = slice(tg * 512, (tg + 1) * 512)
                    par = hp % 2
                    tk = lambda i, par=par: (f"TA{par}_{i}" if i < 6 else f"TB_{i}") if i < 10 else (f"TA{par}_1" if i == 10 else "TB_7")
                    A_, B_ = pA[hp], pB[hp]
                    An, Bn = pn(A_), pn(B_)
                    T_a, T_ld, T_Eg, T_Eig, T_Ex, T_Eh = TA[par]
                    T_kk, T_sq, T_kp, T_b = TB
                    T_bh, T_kh = T_ld, T_sq
                    S.op("pe", lambda hp=hp, A_=A_: nc.tensor.matmul(A_[:], lhsT=a2_bf[0:64, hp * 128:(hp + 1) * 128], rhs=adT[0:64, cs], start=True, stop=True),
                         reads=["a2_bf", "adT"], writes=[An])
                    S.op("act", lambda hp=hp, A_=A_, T_a=T_a: nc.scalar.activation(out=T_a[:], in_=A_[:], func=AF.Sigmoid, bias=cols[:, 0, hp:hp + 1]),
                         reads=[An, "cols"], writes=[tk(0)])
                    for cc in range(4):
                        S.op("pe", lambda hp=hp, B_=B_, cc=cc: nc.tensor.matmul(B_[:, cc * 128:(cc + 1) * 128], lhsT=tanhwdT[0:64, tg * 512 + cc * 128:tg * 512 + (cc + 1) * 128],
                                                                              rhs=w2_bf[0:64, hp * 128:(hp + 1) * 128], start=True, stop=True),
                             reads=["w2_bf", "tanhwdT"], writes=[Bn])
                    S.op("dve", lambda hp=hp, B_=B_, T_ld=T_ld: nc.vector.tensor_tensor(out=v4(T_ld[:]), in0=v4(B_[:]),
                                                                                     in1=w0_bc[:, hp * 128:(hp + 1) * 128].unsqueeze(1).to_broadcast([128, 4, 128]), op=ALU.add),
                         reads=[Bn, "w0_bc"], writes=[tk(1)])
                    S.op("act", lambda T_ld=T_ld: nc.scalar.activation(out=T_ld[:], in_=T_ld[:], func=AF.Sigmoid), reads=[tk(1)], writes=[tk(1)])
                    def cum(ps, psn, which):
                        for cc in range(4):
                            S.op("pe", lambda cc=cc, ps=ps, which=which, T_ld=T_ld: nc.tensor.matmul(ps[:, cc * 128:(cc + 1) * 128], lhsT=T_ld[:, cc * 128:(cc + 1) * 128],
                                                                                                rhs=tri[:, which, :], start=True, stop=True),
                                 reads=[tk(1), "tri"], writes=[psn])
                    cum(A_, An, 0)
                    S.op("act", lambda A_=A_, T_Eg=T_Eg: nc.scalar.activation(out=T_Eg[:], in_=A_[:], func=AF.Exp), reads=[An], writes=[tk(2)])
                    S.op("act", lambda A_=A_, T_Eig=T_Eig: nc.scalar.activation(out=T_Eig[:], in_=A_[:], func=AF.Exp, scale=-1.0), reads=[An], writes=[tk(3)])
                    cum(B_, Bn, 1)
                    S.op("act", lambda B_=B_, T_Ex=T_Ex: nc.scalar.activation(out=T_Ex[:], in_=B_[:], func=AF.Exp), reads=[Bn], writes=[tk(4)])
                    cum(A_, An, 2)
                    S.op("act", lambda A_=A_, T_Eh=T_Eh: nc.scalar.activation(out=T_Eh[:], in_=A_[:], func=AF.Exp), reads=[An], writes=[tk(5)])
                    S.op("dve", lambda hp=hp, T_Eg=T_Eg: nc.vector.tensor_copy(out=gam[:, hp, tg * 4:(tg + 1) * 4], in_=v4(T_Eg[:])[:, :, 127]),
                         reads=[tk(2)], writes=[("gam", hp, tg)])
                    if S.capture is not None:
                        S.capture.append("SPLIT")
                    S.op("act", lambda hp=hp, T_kk=T_kk: nc.scalar.activation(out=T_kk[:], in_=kT[:, hp, cs], func=AF.Copy, scale=cols[:, 1, hp:hp + 1]),
                         reads=[("kT", hp, tg), "cols"], writes=[tk(6)])
                    S.op("dve", lambda T_kk=T_kk, T_sq=T_sq: nc.vector.tensor_tensor(out=T_sq[:], in0=T_kk[:], in1=T_kk[:], op=ALU.mult), reads=[tk(6)], writes=[tk(7)])
                    S.op("pe", lambda B_=B_, T_sq=T_sq: nc.tensor.matmul(B_[:], lhsT=blk[:], rhs=T_sq[:], start=True, stop=True), reads=[tk(7), "blk"], writes=[Bn])
                    S.op("dve", lambda B_=B_, T_sq=T_sq: nc.vector.tensor_scalar(out=T_sq[:], in0=B_[:], scalar1=1e-24, scalar2=None, op0=ALU.max), reads=[Bn], writes=[tk(7)])
                    S.op("act", lambda T_sq=T_sq: nc.scalar.activation(out=T_sq[:], in_=T_sq[:], func=AF.Ln), reads=[tk(7)], writes=[tk(7)])
                    S.op("act", lambda T_sq=T_sq: nc.scalar.activation(out=T_sq[:], in_=T_sq[:], func=AF.Exp, scale=-0.5), reads=[tk(7)], writes=[tk(7)])
                    S.op("dve", lambda T_kk=T_kk, T_sq=T_sq: nc.vector.tensor_tensor(out=T_kk[:], in0=T_kk[:], in1=T_sq[:], op=ALU.mult), reads=[tk(6), tk(7)], writes=[tk(6)])
                    S.op("dve", lambda hp=hp, T_kp=T_kp, T_a=T_a: nc.vector.tensor_scalar(out=T_kp[:], in0=T_a[:], scalar1=cols[:, 2, hp:hp + 1], scalar2=omka[:, hp:hp + 1],
                                                                                   op0=ALU.mult, op1=ALU.add), reads=[tk(0), "cols", "omka"], writes=[tk(8)])
                    S.op("dve", lambda hp=hp, T_kp=T_kp: nc.vector.tensor_tensor(out=T_kp[:], in0=T_kp[:], in1=kT[:, hp, cs], op=ALU.mult),
                         reads=[tk(8), ("kT", hp, tg)], writes=[tk(8)])
                    S.op("dve", lambda T_b=T_b, T_kk=T_kk, T_a=T_a: nc.vector.tensor_tensor(out=T_b[:], in0=T_kk[:], in1=T_a[:], op=ALU.mult), reads=[tk(6), tk(0)], writes=[tk(9)])
                    S.op("dve", lambda hp=hp, T_sq=T_sq, T_kp=T_kp: nc.vector.tensor_tensor(out=T_sq[:], in0=T_kp[:], in1=rT[:, hp, cs], op=ALU.mult),
                         reads=[tk(8), tk(7), ("rT", hp, tg)], writes=[tk(7)])
                    for cc in range(4):
                        S.op("pe", lambda hp=hp, cc=cc, B_=B_, T_sq=T_sq: nc.tensor.matmul(B_[:, cc * 2:cc * 2 + 2], lhsT=T_sq[:, cc * 128:(cc + 1) * 128], rhs=RKt[:, hp, :],
                                                                                       start=True, stop=True), reads=[tk(7), "RKt"], writes=[Bn])
                    S.op("act", lambda hp=hp, B_=B_: nc.scalar.copy(out=bon[:, tg * 4:(tg + 1) * 4, 2 * hp:2 * hp + 2], in_=B_[:, 0:8].rearrange("p (c h) -> p c h", c=4)),
                         reads=[Bn], writes=[("bon", hp, tg)])
                    opk = ("ops", hp)
                    S.op("dve", lambda hp=hp, T_kk=T_kk, T_Ex=T_Ex: nc.vector.scalar_tensor_tensor(out=ARt[hp][:, :, 0, :], in0=v4(T_kk[:]), scalar=-1.0, in1=v4(T_Ex[:]),
                                                                                           op0=ALU.mult, op1=ALU.mult), reads=[tk(6), tk(4)], writes=[("atl", hp)])
                    S.op("dve", lambda hp=hp, T_Eg=T_Eg: nc.vector.tensor_tensor(out=ARt[hp][:, :, 1, :], in0=v4(rT[:, hp, cs]), in1=v4(T_Eg[:]), op=ALU.mult),
                         reads=[("rT", hp, tg), tk(2)], writes=[("rtl", hp)])
                    S.op("dve", lambda hp=hp, T_b=T_b, T_Eig=T_Eig: nc.vector.tensor_tensor(out=btl[hp], in0=T_b[:], in1=T_Eig[:], op=ALU.mult),
                         reads=[tk(9), tk(3)], writes=[("btl", hp)])
                    S.op("dve", lambda hp=hp, T_kp=T_kp, T_Eig=T_Eig: nc.vector.tensor_tensor(out=ktl[hp], in0=T_kp[:], in1=T_Eig[:], op=ALU.mult),
                         reads=[tk(8), tk(3)], writes=[("ktl", hp)])
                    S.op("dve", lambda hp=hp, T_b=T_b, T_Eh=T_Eh: nc.vector.tensor_tensor(out=T_bh[:], in0=T_b[:], in1=T_Eh[:], op=ALU.mult),
                         reads=[tk(9), tk(5)], writes=[tk(10)])
                    S.op("dve", lambda hp=hp, T_kp=T_kp, T_Eh=T_Eh: nc.vector.tensor_tensor(out=T_kh[:], in0=T_kp[:], in1=T_Eh[:], op=ALU.mult),
                         reads=[tk(8), tk(5)], writes=[tk(11)])
                    for cc in range(4):
                        S.op("pe", lambda cc=cc, A_=A_, T_bh=T_bh: nc.tensor.transpose(out=A_[:, cc * 128:(cc + 1) * 128], in_=T_bh[:, cc * 128:(cc + 1) * 128], identity=idtf[:]),
                             reads=[tk(10), "idtf"], writes=[An])
                    for cc in range(4):
                        S.op("pe", lambda cc=cc, B_=B_, T_kh=T_kh: nc.tensor.transpose(out=B_[:, cc * 128:(cc + 1) * 128], in_=T_kh[:, cc * 128:(cc + 1) * 128], identity=idtf[:]),
                             reads=[tk(11), "idtf"], writes=[Bn])
                    S.op("act", lambda hp=hp, A_=A_: nc.scalar.copy(out=BhT[hp], in_=A_[:].rearrange("p (c f) -> p c f", c=4)), reads=[An], writes=[("BhT", hp)])
                    S.op("dve", lambda hp=hp, B_=B_: nc.vector.tensor_copy(out=KhT[hp], in_=B_[:].rearrange("p (c f) -> p c f", c=4)), reads=[Bn], writes=[("KhT", hp)])

            def chunk_step(tg, cc):
                S.pe_inorder = 'bfc' in SK or 'amw' in SK
                if True:
                    c = tg * 4 + cc
                    ccs = slice(cc * 128, (cc + 1) * 128)
                    def views(hp):
                        A_, B_ = pA[hp], pB[hp]
                        return (A_, B_, pn(A_), pn(B_), A_[:].rearrange("p (h t i) -> p h t i", h=2, t=2), B_[:].rearrange("p (h t i) -> p h t i", h=2, t=2),
                                A_[:, 0:256].rearrange("p (h i) -> p h i", h=2))
                    for hp in range(4):
                        A_, B_, An, Bn, A4, B4, A3 = views(hp)
                        for h in range(2):
                            hs = slice(h * 64, (h + 1) * 64)
                            S.op("pe", lambda hp=hp, h=h, hs=hs, A4=A4: nc.tensor.matmul(A4[:, h, :, :], lhsT=btl[hp][hs, ccs], rhs=ARt[hp][hs, cc, :, :], start=True, stop=True),
                                 reads=[("btl", hp), ("atl", hp), ("rtl", hp)], writes=[An], pe_wait=(h == 1 or 'amw' not in SK))
                            S.op("pe", lambda hp=hp, h=h, hs=hs, B4=B4: nc.tensor.matmul(B4[:, h, :, :], lhsT=ktl[hp][hs, ccs], rhs=ARt[hp][hs, cc, :, :], start=True, stop=True),
                                 reads=[("ktl", hp), ("atl", hp), ("rtl", hp)], writes=[Bn], pe_wait=(h == 1 or 'amw' not in SK))
                    for hp in range(4):
                        A_, B_, An, Bn, A4, B4, A3 = views(hp)
                        S.op("dve", lambda hp=hp, A4=A4: nc.vector.tensor_tensor(out=MA[hp][:], in0=A4, in1=mskA[:].unsqueeze(1).to_broadcast([128, 2, 2, 128]), op=ALU.mult),
                             reads=[An, "mskA"], writes=[("MA", hp)])
                        S.op("dve", lambda hp=hp, B4=B4: nc.vector.tensor_tensor(out=MB[hp][:], in0=B4, in1=mskA[:].unsqueeze(1).to_broadcast([128, 2, 2, 128]), op=ALU.mult),
                             reads=[Bn, "mskA"], writes=[("MB", hp)])
                    for hp in range(4):
                        A_, B_, An, Bn, A4, B4, A3 = views(hp)
                        for h in range(2):
                            hs = slice(h * 64, (h + 1) * 64)
                            S.op("pe", lambda hp=hp, h=h, hs=hs, A3=A3: nc.tensor.matmul(A3[:, h, :], lhsT=ARt[hp][hs, cc, 0, :], rhs=btl[hp][hs, ccs], start=True, stop=True),
                                 reads=[("btl", hp), ("atl", hp)], writes=[An], pe_wait=True)
                    for hp in range(4):
                        A_, B_, An, Bn, A4, B4, A3 = views(hp)
                        S.op("dve", lambda hp=hp, A3=A3: nc.vector.tensor_tensor(out=N0[hp][:], in0=A3, in1=mskL[:].unsqueeze(1).to_broadcast([128, 2, 128]), op=ALU.mult),
                             reads=[An, "mskL"], writes=[("N", hp, 0)])
                        S.op("dve", lambda hp=hp: nc.vector.tensor_tensor(out=Pb[hp][0][:], in0=MA[hp][:, :, 0, :], in1=idt[:].unsqueeze(1).to_broadcast([128, 2, 128]), op=ALU.add),
                             reads=[("MA", hp), "idt"], writes=[("P", hp, 0)])
                    S.pe_inorder = 'nobfs' not in SK
                    for l in range(1, 7):
                        def Mget(l_, h, hp):
                            return MA[hp][:, h, 0, :] if l_ == 0 else NMs[hp][l_ % 2][:, 1, h, :]

                        def Nget(l_, h, hp):
                            return N0[hp][:, h, :] if l_ == 0 else NMs[hp][l_ % 2][:, 0, h, :]
                        for hp in range(4):
                            A_ = pA[hp]
                            An = pn(A_)
                            A4 = A_[:].rearrange("p (t h i) -> p t h i", t=2, h=2)
                            prevk = [("N", hp, l - 1), ("MA", hp) if l == 1 else ("M", hp, l - 1)]
                            for h in range(2):
                                S.op("pe", lambda h=h, hp=hp, A4=A4, l=l: nc.tensor.matmul(A4[:, 0, h, :], lhsT=Mget(l - 1, h, hp), rhs=Nget(l - 1, h, hp), start=True, stop=True),
                                     reads=prevk, writes=[An])
                                if l < 6:
                                    S.op("pe", lambda h=h, hp=hp, A4=A4, l=l: nc.tensor.matmul(A4[:, 1, h, :], lhsT=Nget(l - 1, h, hp), rhs=Mget(l - 1, h, hp), start=True, stop=True),
                                         reads=prevk, writes=[An])
                            if l < 6:
                                S.op("act", lambda hp=hp, A4=A4, l=l: nc.scalar.copy(out=NMs[hp][l % 2][:], in_=A4), reads=[An], writes=[("N", hp, l), ("M", hp, l)])
                            else:
                                S.op("act", lambda hp=hp, A4=A4, l=l: nc.scalar.copy(out=NMs[hp][l % 2][:, 0, :, :], in_=A4[:, 0, :, :]), reads=[An], writes=[("N", hp, l)])
                        for hp in range(4):
                            B_ = pB[hp]
                            Bn = pn(B_)
                            B3 = B_[:, 0:256].rearrange("p (h i) -> p h i", h=2)
                            for h in range(2):
                                S.op("pe", lambda h=h, B3=B3, hp=hp, l=l: nc.tensor.matmul(B3[:, h, :], lhsT=Nget(l, h, hp), rhs=Pb[hp][(l - 1) % 2][:, h, :], start=True, stop=True),
                                     reads=[("N", hp, l), ("P", hp, l - 1)], writes=[Bn])
                            S.op("dve", lambda hp=hp, B3=B3, l=l: nc.vector.tensor_tensor(out=Pb[hp][l % 2][:], in0=B3, in1=Pb[hp][(l - 1) % 2][:], op=ALU.add),
                                 reads=[Bn, ("P", hp, l - 1)], writes=[("P", hp, l)])
                    S.pe_inorder = 'nobfscan' not in SK
                    def TT(hp, h):
                        return Pb[hp][0][:, h, :]
                    def Bsl(hp, lo, n):
                        return pB[hp][:, lo:lo + n]
                    for hp in range(4):
                        Bn = pn(pB[hp])
                        for h in range(2):
                            hs = slice(h * 64, (h + 1) * 64)
                            vcols = slice((2 * hp + h) * 64, (2 * hp + h + 1) * 64)
                            S.op("pe", lambda hp=hp, h=h, hs=hs: nc.tensor.matmul(Bsl(hp, 256 + h * 64, 64), lhsT=ARt[hp][hs, cc, 0, :], rhs=S_bf[hp][hs, h, :], start=True, stop=False),
                                 reads=[("atl", hp), f"S_bf{hp}"], writes=[Bn])
                            S.op("pe", lambda hp=hp, h=h, vcols=vcols: nc.tensor.matmul(Bsl(hp, 256 + h * 64, 64), lhsT=MB[hp][:, h, 0, :], rhs=Vt[:, c, vcols], start=False, stop=True),
                                 reads=[("MB", hp), f"Vt{c}"], writes=[Bn])
                        S.op("act", lambda hp=hp: nc.scalar.copy(out=X_bf[hp][:], in_=Bsl(hp, 256, 128).rearrange("p (h v) -> p h v", h=2)), reads=[Bn], writes=[f"X_bf{hp}"])
                    for hp in range(4):
                        Bn = pn(pB[hp])
                        for h in range(2):
                            S.op("pe", lambda hp=hp, h=h: nc.tensor.matmul(Bsl(hp, 384 + h * 64, 64), lhsT=TT(hp, h), rhs=X_bf[hp][:, h, :], start=True, stop=True),
                                 reads=[("P", hp, 6), f"X_bf{hp}"], writes=[Bn])
                        S.op("dve", lambda hp=hp: nc.vector.tensor_copy(out=U_bf[hp][:], in_=Bsl(hp, 384, 128).rearrange("p (h v) -> p h v", h=2)), reads=[Bn], writes=[f"U_bf{hp}"])
                    Ytc = Yt[c % 2]
                    for hp in range(4):
                        Bn = pn(pB[hp])
                        for h in range(2):
                            hs = slice(h * 64, (h + 1) * 64)
                            vcols = slice((2 * hp + h) * 64, (2 * hp + h + 1) * 64)
                            S.op("pe", lambda hp=hp, h=h, hs=hs: nc.tensor.matmul(Bsl(hp, 256 + h * 64, 64), lhsT=ARt[hp][hs, cc, 1, :], rhs=S_bf[hp][hs, h, :], start=True, stop=False),
                                 reads=[("rtl", hp), f"S_bf{hp}"], writes=[Bn])
                            S.op("pe", lambda hp=hp, h=h: nc.tensor.matmul(Bsl(hp, 256 + h * 64, 64), lhsT=MA[hp][:, h, 1, :], rhs=U_bf[hp][:, h, :], start=False, stop=False),
                                 reads=[("MA", hp), f"U_bf{hp}"], writes=[Bn])
                            S.op("pe", lambda hp=hp, h=h, vcols=vcols: nc.tensor.matmul(Bsl(hp, 256 + h * 64, 64), lhsT=MB[hp][:, h, 1, :], rhs=Vt[:, c, vcols], start=False, stop=True),
                                 reads=[("MB", hp), f"Vt{c}"], writes=[Bn])
                        S.op("act", lambda hp=hp, Ytc=Ytc: nc.scalar.copy(out=Ytc[:, hp * 128:(hp + 1) * 128], in_=Bsl(hp, 256, 128)), reads=[Bn], writes=[f"Yt{c % 2}"])
                    for hp in range(4):
                        Bn = pn(pB[hp])
                        for h in range(2):
                            vcols = slice((2 * hp + h) * 64, (2 * hp + h + 1) * 64)
                            S.op("pe", lambda hp=hp, h=h: nc.tensor.matmul(Bsl(hp, h * 64, 64), lhsT=BhT[hp][:, cc, :], rhs=U_bf[hp][:, h, :], start=True, stop=False),
                                 reads=[("BhT", hp), f"U_bf{hp}"], writes=[Bn])
                            S.op("pe", lambda hp=hp, h=h, vcols=vcols: nc.tensor.matmul(Bsl(hp, h * 64, 64), lhsT=KhT[hp][:, cc, :], rhs=Vt[:, c, vcols], start=False, stop=True),
                                 reads=[("KhT", hp), f"Vt{c}"], writes=[Bn])
                        S.op("dve", lambda hp=hp: nc.vector.scalar_tensor_tensor(out=S_f[hp][:].rearrange("p h v -> p (h v)"), in0=S_f[hp][:].rearrange("p h v -> p (h v)"),
                                                                                  scalar=gam[:, hp, c:c + 1], in1=Bsl(hp, 0, 128), op0=ALU.mult, op1=ALU.add),
                             reads=[Bn, f"S_f{hp}", ("gam", hp, tg)], writes=[f"S_f{hp}"])
                        S.op("act", lambda hp=hp: nc.scalar.copy(out=S_bf[hp][:], in_=S_f[hp][:]), reads=[f"S_f{hp}"], writes=[f"S_bf{hp}"])
                    S.pe_inorder = 'nobfpost' not in SK
                    Yk = f"Yt{c % 2}"
                    Y3 = Ytc[:].rearrange("p (g v) -> p g v", g=8)
                    def bc8(i):
                        return st8[:, i, :].unsqueeze(2).to_broadcast([128, 8, 64])
                    S.op("dve", lambda Y3=Y3: nc.vector.reduce_sum(out=st8[:, 0, :], in_=Y3, axis=AX.X), reads=[Yk], writes=["st8_0"])
                    S.op("act", lambda Ytc=Ytc: nc.scalar.activation(out=Y1[:], in_=Ytc[:], func=AF.Square), reads=[Yk], writes=["Y1"])
                    S.op("dve", lambda: nc.vector.reduce_sum(out=st8[:, 1, :], in_=Y1[:].rearrange("p (g v) -> p g v", g=8), axis=AX.X), reads=["Y1"], writes=["st8_1"])
                    S.op("dve", lambda: nc.vector.tensor_scalar(out=st8[:, 2, :], in0=st8[:, 0, :], scalar1=1.0 / 64, scalar2=None, op0=ALU.mult), reads=["st8_0"], writes=["st8_2"])
                    S.op("dve", lambda: nc.vector.tensor_tensor(out=st8[:, 3, :], in0=st8[:, 2, :], in1=st8[:, 2, :], op=ALU.mult), reads=["st8_2"], writes=["st8_3"])
                    S.op("dve", lambda: nc.vector.scalar_tensor_tensor(out=st8[:, 4, :], in0=st8[:, 1, :], scalar=1.0 / 64, in1=st8[:, 3, :], op0=ALU.mult, op1=ALU.subtract),
                         reads=["st8_1", "st8_3"], writes=["st8_4"])
                    S.op("dve", lambda: nc.vector.tensor_scalar(out=st8[:, 4, :], in0=st8[:, 4, :], scalar1=64e-5, scalar2=None, op0=ALU.add), reads=["st8_4"], writes=["st8_4"])
                    S.op("act", lambda: nc.scalar.activation(out=st8[:, 5, :], in_=st8[:, 4, :], func=AF.Ln), reads=["st8_4"], writes=["st8_5"])
                    S.op("act", lambda: nc.scalar.activation(out=st8[:, 5, :], in_=st8[:, 5, :], func=AF.Exp, scale=-0.5), reads=["st8_5"], writes=["st8_5"])
                    S.op("dve", lambda Y3=Y3: nc.vector.tensor_tensor(out=Y1[:].rearrange("p (g v) -> p g v", g=8), in0=Y3, in1=bc8(2), op=ALU.subtract),
                         reads=[Yk, "st8_2", "Y1"], writes=["Y1"])
                    S.op("dve", lambda: nc.vector.tensor_tensor(out=Y1[:].rearrange("p (g v) -> p g v", g=8), in0=Y1[:].rearrange("p (g v) -> p g v", g=8), in1=bc8(5), op=ALU.mult),
                         reads=["Y1", "st8_5"], writes=["Y1"])
                    S.op("dve", lambda: nc.vector.tensor_tensor(out=Y1[:], in0=Y1[:], in1=lng_bc[:], op=ALU.mult), reads=["Y1", "lng_bc"], writes=["Y1"])
                    S.op("dve", lambda: nc.vector.tensor_tensor(out=Y1[:], in0=Y1[:], in1=lnb_bc[:], op=ALU.add), reads=["Y1", "lnb_bc"], writes=["Y1"])
                    S.op("dve", lambda c=c: nc.vector.tensor_tensor(out=Y2[:].rearrange("p (g v) -> p g v", g=8), in0=Vt[:, c, :].rearrange("p (g v) -> p g v", g=8),
                                                                     in1=bon[:, c, :].unsqueeze(2).to_broadcast([128, 8, 64]), op=ALU.mult),
                         reads=[f"Vt{c}"] + [("bon", hp, tg) for hp in range(4)], writes=["Y2"])
                    S.op("dve", lambda: nc.vector.tensor_tensor(out=Y1[:], in0=Y1[:], in1=Y2[:], op=ALU.add), reads=["Y1", "Y2"], writes=["Y1"])
                    G_ = pA[0]
                    Gn = pn(G_)
                    S.op("pe", lambda c=c, G_=G_: nc.tensor.matmul(G_[:], lhsT=sgdT[:, c * 128:(c + 1) * 128], rhs=g2_bf[:], start=True, stop=True), reads=["sgdT", "g2_bf"], writes=[Gn])
                    S.op("dve", lambda G_=G_: nc.vector.tensor_tensor(out=yb[:], in0=Y1[:], in1=G_[:], op=ALU.mult), reads=["Y1", Gn], writes=["yb"])
                    G1 = pA[1]
                    G1n = pn(pA[1])
                    for hp in range(4):
                        S.op("pe", lambda hp=hp, G1=G1: nc.tensor.transpose(out=G1[:, hp * 128:(hp + 1) * 128], in_=yb[:, hp * 128:(hp + 1) * 128], identity=idtf[:]),
                             reads=["yb", "idtf"], writes=[G1n])
                    S.op("act", lambda c=c, G1=G1: nc.scalar.copy(out=ybT[:, :, c * 128:(c + 1) * 128], in_=G1[:].rearrange("p (h i) -> p h i", h=4)),
                         reads=[G1n], writes=["ybT"])
            if 'sbufdbg' in SK:
                print("SBUF remaining in B scope 2:", nc.sbuf_bytes_remaining)
            for tg in range(4):
                parts = []
                for hp in range(4):
                    S.capture = []
                    prep(tg, hp)
                    cap, S.capture = S.capture, None
                    k = cap.index("SPLIT")
                    parts.append((cap[:k], cap[k + 1:]))
                S.replay(parts[0][0])
                for hp in range(4):
                    p2 = parts[hp][1]
                    p1 = parts[hp + 1][0] if hp < 3 else []
                    i1 = i2 = 0
                    while i1 < len(p1) or i2 < len(p2):
                        if i2 < len(p2):
                            S.replay(p2[i2:i2 + 2]); i2 += 2
                        if i1 < len(p1):
                            S.replay(p1[i1:i1 + 1]); i1 += 1
                if 'b2prep1' in SK or 'b2prep' in SK:
                    break
                for cc in range(4):
                    chunk_step(tg, cc)
                    if 'b2c1' in SK:
                        break
                if 'b2c1' in SK or 'b2tg1' in SK:
                    break
            S.barrier()
            run_block(nc, S)


def norm_T(nc, S, tag, get_x, gcol, dstT, idt, bufs, eps=1e-6):
    xn, junk, ss, rs, rstd, tp = bufs
    keys = []
    pending = []
    for t in range(NT):
        b = t % 2
        xa, xk = get_x(t)
        S.op("act", lambda t=t, xa=xa: nc.scalar.activation(out=junk[:], in_=xa, func=AF.Square, accum_out=ss[:, t:t + 1]),
             reads=xk, writes=[tag + "junk", f"{tag}ss{t}"])
        S.op("act", lambda t=t: nc.scalar.activation(out=rs[:, t:t + 1], in_=ss[:, t:t + 1], func=AF.Sqrt, scale=1.0 / 1024, bias=eps),
             reads=[f"{tag}ss{t}"], writes=[f"{tag}rs{t}"])
        S.op("dve", lambda t=t: nc.vector.reciprocal(out=rstd[:, t:t + 1], in_=rs[:, t:t + 1]), reads=[f"{tag}rs{t}"], writes=[f"{tag}rstd{t}"])
        S.op("dve", lambda t=t, b=b, xa=xa: nc.vector.tensor_scalar(out=xn[b][:], in0=xa, scalar1=rstd[:, t:t + 1], scalar2=None, op0=ALU.mult),
             reads=xk + [f"{tag}rstd{t}"], writes=[f"{tag}xn{b}"])
        for dc in range(8):
            S.op("pe", lambda b=b, dc=dc: nc.tensor.transpose(out=tp[b][:, dc, :], in_=xn[b][:, dc * 128:(dc + 1) * 128], identity=idt[:]),
                 reads=[f"{tag}xn{b}", "idt"], writes=[f"{tag}tp{id(tp[b])}"])
        k = f"{tag}T{t}"
        keys.append(k)
        pending.append(lambda t=t, b=b, k=k: S.op("dve", lambda: nc.vector.tensor_tensor(
            out=dstT[:, :, OFF + t * 128:OFF + (t + 1) * 128], in0=tp[b][:],
            in1=gcol[:].unsqueeze(2).to_broadcast([128, 8, 128]), op=ALU.mult),
            reads=[f"{tag}tp{id(tp[b])}", tag + "gcol"], writes=[k]))
        if len(pending) > (1 if tp[0] is not tp[1] else 0):
            pending.pop(0)()
    while pending:
        pending.pop(0)()
    return keys


def phase_C(nc, S, P, C, x_in, xmid, hT, yaT, ybT, idt, gmix):
    w_in = P["w_in"]
    with ExitStack() as ph:
        def psb(name, shape, dt):
            return ph.enter_context(nc.sbuf_tensor(name, shape, dt))
        xt = [psb(f"cxt{i}", [128, 1024], F32) for i in range(2)]
        xn = [psb(f"cxn{i}", [128, 1024], BF16) for i in range(2)]
        junk = psb("cjunk", [128, 1024], BF16)
        ss = psb("css", [128, 16], F32)
        rs = psb("crs", [128, 16], F32)
        rstd = psb("crstd", [128, 16], F32)
        Wg = psb("Wg", [128, 8, 2048], BF16)
        Wa = psb("Wa_", [128, 4, 1024], BF16)
        Wb = psb("Wb_", [128, 4, 1024], BF16)
        Wo = psb("Wo", [128, 8, 1024], BF16)
        stage = [psb(f"stageC{i}", [128, 4, 512], F32) for i in range(2)]
        mT = psb("mT", [128, 8, 512], BF16)
        sga = psb("sga", [128, 512], F32)
        sgb = psb("sgb", [128, 512], F32)
        m1 = psb("m1", [128, 512], F32)
        m2 = psb("m2", [128, 512], F32)
        xo = [psb(f"xo{i}", [128, 1024], F32) for i in range(2)]
        tp = [ph.enter_context(nc.psum_tensor(f"ctp{i}", [128, 8, 128], BF16)) for i in range(2)]
        ps = [ph.enter_context(nc.psum_tensor(f"cps{i}", [128, 512], F32)) for i in range(6)]
        S.op("pool", lambda: nc.gpsimd.memset(hT[:, :, 0:OFF], 0.0), writes=["hTpad"])

        def get_x(t):
            b = t % 2
            S.dma("sp", lambda: nc.sync.dma_start(out=xt[b][:], in_=x_in[t * 128:(t + 1) * 128, :]), f"cx{b}", writes=[f"cxt{b}"], serial=True)
            return xt[b][:], [f"cxt{b}"]
        HK = norm_T(nc, S, "C", get_x, gmix, hT, idt, (xn, junk, ss, rs, rstd, tp))
        ld = LoadCast(nc, S, stage)
        ld.skey = "stageC"
        ld.load(Wg, w_in, 8, 3328, 2048, "Wg")
        ld.load(Wa, P["w_branch_a"], 4, 0, 1024, "Wa_")
        ld.load(Wb, P["w_branch_b"], 4, 0, 1024, "Wb_")
        ld.load(Wo, P["w_o"], 8, 0, 1024, "Wo")
        for tg in range(4):
            cs = slice(tg * 512, (tg + 1) * 512)
            hcs = slice(OFF + tg * 512, OFF + (tg + 1) * 512)
            for fc in range(8):
                fa = slice(fc * 128, (fc + 1) * 128)
                fb = slice(1024 + fc * 128, 1024 + (fc + 1) * 128)
                for dc in range(8):
                    S.op("pe", lambda dc=dc, fa=fa, hcs=hcs: nc.tensor.matmul(ps[0][:], lhsT=Wg[:, dc, fa], rhs=hT[:, dc, hcs], start=(dc == 0), stop=(dc == 7)),
                         reads=["Wg"] + HK, writes=["cps0"])
                for dc in range(8):
                    S.op("pe", lambda dc=dc, fb=fb, hcs=hcs: nc.tensor.matmul(ps[1][:], lhsT=Wg[:, dc, fb], rhs=hT[:, dc, hcs], start=(dc == 0), stop=(dc == 7)),
                         reads=["Wg"] + HK, writes=["cps1"])
                for kc in range(4):
                    S.op("pe", lambda kc=kc, fa=fa, cs=cs: nc.tensor.matmul(ps[2][:], lhsT=Wa[:, kc, fa], rhs=yaT[:, kc, cs], start=(kc == 0), stop=(kc == 3)),
                         reads=["Wa_", "yaT"], writes=["cps2"])
                for kc in range(4):
                    S.op("pe", lambda kc=kc, fa=fa, cs=cs: nc.tensor.matmul(ps[3][:], lhsT=Wb[:, kc, fa], rhs=ybT[:, kc, cs], start=(kc == 0), stop=(kc == 3)),
                         reads=["Wb_", "ybT"], writes=["cps3"])
                S.op("act", lambda: nc.scalar.activation(out=sga[:], in_=ps[0][:], func=AF.Sigmoid), reads=["cps0"], writes=["sga"])
                S.op("act", lambda: nc.scalar.activation(out=sgb[:], in_=ps[1][:], func=AF.Sigmoid), reads=["cps1"], writes=["sgb"])
                S.op("dve", lambda: nc.vector.tensor_tensor(out=m1[:], in0=sga[:], in1=ps[2][:], op=ALU.mult), reads=["sga", "cps2"], writes=["m1"])
                S.op("dve", lambda: nc.vector.tensor_tensor(out=m2[:], in0=sgb[:], in1=ps[3][:], op=ALU.mult), reads=["sgb", "cps3"], writes=["m2"])
                S.op("dve", lambda fc=fc: nc.vector.tensor_tensor(out=mT[:, fc, :], in0=m1[:], in1=m2[:], op=ALU.add), reads=["m1", "m2"], writes=[f"mT{fc}"])
            for tt in range(4):
                t = tg * 4 + tt
                b = t % 2
                S.dma("sp", lambda t=t, b=b: nc.sync.dma_start(out=xt[b][:], in_=x_in[t * 128:(t + 1) * 128, :]), f"cx{b}", writes=[f"cxt{b}"], serial=True)
                for half in range(2):
                    pso = ps[4 + half]
                    for fc in range(8):
                        S.op("pe", lambda fc=fc, tt=tt, half=half, pso=pso: nc.tensor.matmul(pso[:], lhsT=mT[:, fc, tt * 128:(tt + 1) * 128], rhs=Wo[:, fc, half * 512:(half + 1) * 512],
                                                                                         start=(fc == 0), stop=(fc == 7)), reads=["Wo", f"mT{fc}"], writes=[f"cps{4 + half}"])
                    S.op("dve", lambda b=b, half=half, pso=pso: nc.vector.tensor_tensor(out=xo[b][:, half * 512:(half + 1) * 512], in0=pso[:], in1=xt[b][:, half * 512:(half + 1) * 512], op=ALU.add),
                         reads=[f"cps{4 + half}", f"cxt{b}"], writes=[f"xo{b}_{half}"])
                S.dma("sp", lambda t=t, b=b: nc.sync.dma_start(out=xmid[t * 128:(t + 1) * 128, :], in_=xo[b][:]), f"xmidw{b}", reads=[f"xo{b}_0", f"xo{b}_1"], writes=[f"xmid{t}"], serial=True)
        run_block(nc, S)


def phase_D(nc, S, P, C, xmid, out, idt):
    with ExitStack() as ph:
        def psb(name, shape, dt):
            return ph.enter_context(nc.sbuf_tensor(name, shape, dt))
        xres = psb("xres", [128, NT, 1024], F32)
        h2T = psb("h2T", [128, 8, S_LEN + OFF], BF16)
        xn = [psb(f"dxn{i}", [128, 1024], BF16) for i in range(2)]
        junk = psb("djunk", [128, 1024], BF16)
        ss = psb("dss", [128, 16], F32)
        rs = psb("drs", [128, 16], F32)
        rstd = psb("drstd", [128, 16], F32)
        gffn = psb("gffn", [128, 8], F32)
        gfin = psb("gfin", [128, 1024], F32)
        W1q = [psb(f"W1q{i}", [128, 8, 1024], BF16) for i in range(2)]
        W2q = [psb(f"W2q{i}", [128, 8, 1024], BF16) for i in range(2)]
        stage = [psb(f"stageD{i}", [128, 4, 512], F32) for i in range(2)]
        ur = [psb(f"ur{i}", [128, 256], F32) for i in range(3)]
        ub = [psb(f"ub{i}", [128, 256], BF16) for i in range(3)]
        ot = [psb(f"ot{i}", [128, 1024], F32) for i in range(2)]
        tp0 = ph.enter_context(nc.psum_tensor("dtp0", [128, 8, 128], BF16))
        tp = [tp0, tp0]
        acc = [ph.enter_context(nc.psum_tensor(f"acc{i}", [128, 512], F32)) for i in range(4)]
        ups = [ph.enter_context(nc.psum_tensor(f"ups{i}", [128, 512], F32)) for i in range(3)]
        S.dma("sp", lambda: nc.sync.dma_start(out=gffn[:], in_=P["norm_ffn_g"].rearrange("(c p) -> p c", p=128), allow_slow_non_contiguous=True), "cD", writes=["Dgcol"])
        S.dma("sp", lambda: nc.sync.dma_start(out=gfin[:], in_=P["norm_final_g"].partition_broadcast(128)), "cD", writes=["gfin"])
        for t in range(NT):
            S.dma("sp", lambda t=t: nc.sync.dma_start(out=xres[:, t, :], in_=xmid[t * 128:(t + 1) * 128, :]), f"xr{t % 4}", reads=[f"xmid{t}"], writes=[f"xres{t}"])

        def get_x(t):
            return xres[:, t, :], [f"xres{t}"]
        HK = norm_T(nc, S, "D", get_x, gffn, h2T, idt, (xn, junk, ss, rs, rstd, tp))
        ld = LoadCast(nc, S, stage)
        ld.skey = "stageD"

        def wpieces(q):
            bq = q % 2
            return (ld.pieces(W1q[bq], P["w_ff1"], 8, q * 1024, 1024, f"W1q{bq}", eng="dve")
                    + ld.pieces(W2q[bq], P["w_ff2"], 8, 0, 1024, f"W2q{bq}", r0=q * 1024, eng="act"))
        for pc in wpieces(0):
            pc()
        its = [(q, g, j) for q in range(4) for g in range(8) for j in range(8)]
        LOOK = 2

        def emit_U(i):
            q, g, j = its[i]
            b = q % 2
            u3 = i % 3
            gcs = slice(OFF + g * 256, OFF + (g + 1) * 256)
            up = ups[u3]
            for dc in range(8):
                S.op("pe", lambda dc=dc: nc.tensor.matmul(up[:, 0:256], lhsT=W1q[b][:, dc, j * 128:(j + 1) * 128], rhs=h2T[:, dc, gcs],
                                                          start=(dc == 0), stop=(dc == 7)), reads=[f"W1q{b}"] + HK, writes=[f"ups{u3}"])
            S.op("act", lambda: nc.scalar.activation(out=ur[u3][:], in_=up[:, 0:256], func=AF.Relu), reads=[f"ups{u3}"], writes=[f"ur{u3}"])
            S.op("dve", lambda: nc.vector.tensor_tensor(out=ub[u3][:], in0=ur[u3][:], in1=ur[u3][:], op=ALU.mult), reads=[f"ur{u3}"], writes=[f"ub{u3}"])

        def final_norm(t):
            b = t % 2
            S.op("act", lambda: nc.scalar.activation(out=junk[:], in_=xres[:, t, :], func=AF.Square, accum_out=ss[:, t:t + 1]),
                 reads=[f"xres{t}"], writes=["Djunk", f"Fss{t}"])
            S.op("act", lambda: nc.scalar.activation(out=rs[:, t:t + 1], in_=ss[:, t:t + 1], func=AF.Sqrt, scale=1.0 / 1024, bias=1e-6),
                 reads=[f"Fss{t}"], writes=[f"Frs{t}"])
            S.op("dve", lambda: nc.vector.reciprocal(out=rstd[:, t:t + 1], in_=rs[:, t:t + 1]), reads=[f"Frs{t}"], writes=[f"Frstd{t}"])
            S.op("dve", lambda: nc.vector.tensor_scalar(out=ot[b][:], in0=xres[:, t, :], scalar1=rstd[:, t:t + 1], scalar2=None, op0=ALU.mult),
                 reads=[f"xres{t}", f"Frstd{t}"], writes=[f"ot{b}"])
            S.op("dve", lambda: nc.vector.tensor_tensor(out=ot[b][:], in0=ot[b][:], in1=gfin[:], op=ALU.mult), reads=[f"ot{b}", "gfin"], writes=[f"ot{b}"])
            S.dma("sp", lambda: nc.sync.dma_start(out=out[t * 128:(t + 1) * 128, :], in_=ot[b][:]), f"outw{b}", reads=[f"ot{b}"], writes=[f"out{t}"], serial=True)

        def emit_ACC(i):
            q, g, j = its[i]
            b = q % 2
            u3 = i % 3
            for tt in range(2):
                for half in range(2):
                    S.op("pe", lambda tt=tt, half=half: nc.tensor.matmul(acc[tt * 2 + half][:], lhsT=ub[u3][:, tt * 128:(tt + 1) * 128],
                                                                         rhs=W2q[b][:, j, half * 512:(half + 1) * 512], start=(j == 0), stop=(j == 7)),
                         reads=[f"ub{u3}", f"W2q{b}"], writes=[f"acc{tt * 2 + half}"])
            if j == 7:
                for tt in range(2):
                    t = g * 2 + tt
                    for half in range(2):
                        S.op("dve", lambda t=t, tt=tt, half=half: nc.vector.tensor_tensor(out=xres[:, t, half * 512:(half + 1) * 512], in0=acc[tt * 2 + half][:],
                                                                                          in1=xres[:, t, half * 512:(half + 1) * 512], op=ALU.add),
                             reads=[f"acc{tt * 2 + half}", f"xres{t}"], writes=[f"xres{t}"])
                    if q == 3:
                        final_norm(t)

        pend = []
        for i in range(len(its) + LOOK):
            if i < len(its):
                q, g, j = its[i]
                if g == 0 and j == 0 and q + 1 < 4:
                    pend = wpieces(q + 1)
                if pend and (i % 6 == 3):
                    pend.pop(0)()
                emit_U(i)
            if i - LOOK >= 0:
                emit_ACC(i - LOOK)
        assert not pend
        run_block(nc, S, ["all"])


_CACHE = {}


def kernel(**inputs):
    if "nc" not in _CACHE:
        _CACHE["nc"] = build()[0]
    nc = _CACHE["nc"]
    consts = host_consts()
    params = pack_params({k: np.asarray(v) for k, v in inputs.items() if k != "x"})
    x = np.ascontiguousarray(np.asarray(inputs["x"], np.float32))
    in_maps = []
    for b in range(8):
        m = {"x": x[b]}
        m.update(params)
        m.update(consts)
        in_maps.append(m)
    res = run_bass_kernel_spmd(nc, in_maps, core_ids=list(range(8)))
    return np.stack([np.asarray(r["out"], np.float32) for r in res.results], 0)
```

```python
import math
import os
SK = os.environ.get('MK_KERNEL_DEBUG_FLAGS', '').split(',')
from contextlib import ExitStack

import numpy as np
import ml_dtypes
import concourse.bass as bass
import concourse.mybir as mybir
from concourse.bass_utils import run_bass_kernel_spmd

F32 = mybir.dt.float32
BF16 = mybir.dt.bfloat16
I32 = mybir.dt.int32
AF = mybir.ActivationFunctionType
ALU = mybir.AluOpType
AX = mybir.AxisListType

S_LEN = 2048
D = 1024
NT = 16
OFF = 8
D_IN = 5376
ENGS = ("pe", "act", "dve", "pool", "sp")


class Sched:
    def __init__(self, nc, sems):
        self.nc = nc
        self.free_sems = list(sems)
        self.esem = {e: self.free_sems.pop() for e in ("pe", "act", "dve", "pool")}
        self.cnt = {e: 0 for e in ENGS}
        self.ops = {e: [] for e in ENGS}
        self.last_w = {}
        self.readers = {}
        self.known = {e: {} for e in ENGS}
        self.groups = {}

    def _deps(self, reads, writes):
        evs = []
        for r in reads:
            if r in self.last_w:
                evs.append(self.last_w[r])
        for w in writes:
            if w in self.last_w:
                evs.append(self.last_w[w])
            evs.extend(self.readers.get(w, ()))
        return evs

    def _commit(self, ev, reads, writes):
        for r in reads:
            self.readers.setdefault(r, []).append(ev)
        for w in writes:
            self.last_w[w] = ev
            self.readers[w] = []

    limit = None
    nrec = 0
    pe_inorder = True

    capture = None

    def replay(self, items):
        for eng, fn, reads, writes, pe_wait, flag in items:
            keep = self.pe_inorder
            self.pe_inorder = flag
            self.op(eng, fn, reads, writes, pe_wait)
            self.pe_inorder = keep

    def op(self, eng, fn, reads=(), writes=(), pe_wait=False):
        if self.capture is not None:
            self.capture.append((eng, fn, list(reads), list(writes), pe_wait, self.pe_inorder))
            return
        if self.limit is not None:
            self.nrec += 1
            if self.nrec > self.limit:
                return
        evs = self._deps(reads, writes)
        if eng == "pe" and self.pe_inorder and not pe_wait:
            evs = [ev for ev in evs if not (ev[0] == "e" and ev[1] == "pe")]
        self.cnt[eng] += 1
        ev = ("e", eng, self.cnt[eng])
        self.ops[eng].append((fn, evs, ("e", eng)))
        self._commit(ev, reads, writes)

    def dma(self, q, fn, group, reads=(), writes=(), serial=False):
        evs = [ev for ev in self._deps(reads, writes) if not (ev[0] == "g" and ev[1] == group)]
        if group not in self.groups:
            self.groups[group] = [self.free_sems.pop(), 0]
        self.groups[group][1] += 1
        ev = ("gs", group, self.groups[group][1]) if serial else ("g", group)
        self.ops[q].append((fn, evs, ("g", group)))
        self._commit(ev, reads, writes)

    def _resolve(self, ev):
        if ev[0] == "e":
            return self.esem[ev[1]], ev[2]
        g = self.groups[ev[1]]
        if ev[0] == "gs":
            return g[0], 16 * ev[2]
        return g[0], 16 * g[1]

    def barrier(self):
        evs = [("e", e, self.cnt[e]) for e in self.esem if self.cnt[e] > 0]
        evs += [("gs", g, v[1]) for g, v in self.groups.items()]
        for e in ENGS:
            self.ops[e].append((None, list(evs), None))

    def emit(self, eng, engobj):
        known = self.known[eng]
        for fn, evs, inc in self.ops[eng]:
            need = {}
            for ev in evs:
                sem, val = self._resolve(ev)
                k = id(sem)
                if known.get(k, 0) >= val:
                    continue
                if k not in need or need[k][1] < val:
                    need[k] = (sem, val)
            for k, (sem, val) in need.items():
                engobj.wait_ge(sem, val)
                known[k] = val
            if fn is None:
                continue
            ins = fn()
            if inc[0] == "e":
                ins.then_inc(self.esem[inc[1]], 1)
            else:
                ins.then_inc(self.groups[inc[1]][0], 16)
        self.ops[eng] = []

    def final_wait(self, engobj, groups):
        if groups:
            groups = list(self.groups.keys())
        for gname in groups:
            g = self.groups[gname]
            engobj.wait_ge(g[0], 16 * g[1])


def run_block(nc, S, final_groups=()):
    S.barrier()
    with nc.Block() as block:
        @block.sync
        def _(e):
            S.emit("sp", e)
            S.final_wait(e, final_groups)

        @block.scalar
        def _(e):
            S.emit("act", e)

        @block.vector
        def _(e):
            S.emit("dve", e)

        @block.gpsimd
        def _(e):
            S.emit("pool", e)

        @block.tensor
        def _(e):
            S.emit("pe", e)


SMALL = [("norm_mix_g", 1024), ("rw_mu", 1792), ("rw_w0", 512), ("rw_a0", 512), ("rw_k_k", 512), ("rw_k_a", 512),
         ("rw_r_k", 512), ("rw_ln_g", 512), ("rw_ln_b", 512), ("da_lq1", 64), ("da_lk1", 64), ("da_lq2", 64),
         ("da_lk2", 64), ("da_subln_g", 128), ("norm_ffn_g", 1024), ("norm_final_g", 1024)]
SMALL_OFF = {}
_o = 0
for _n, _l in SMALL:
    SMALL_OFF[_n] = (_o, _l)
    _o += _l
BLOB_LEN = _o
PARAM_SPECS = [
    ("blob", [BLOB_LEN]), ("rw_w2", [64, 512]), ("rw_a2", [64, 512]), ("rw_g2", [128, 512]),
    ("w_branch_a", [512, 1024]), ("w_branch_b", [512, 1024]),
    ("w_o", [1024, 1024]), ("w_ff1", [1024, 4096]), ("w_ff2", [4096, 1024]), ("w_in", [1024, D_IN]),
]


def pack_params(inp):
    m = {"blob": np.concatenate([np.asarray(inp[n], np.float32).reshape(-1) for n, _ in SMALL])}
    for name, shp in PARAM_SPECS[1:]:
        m[name] = np.ascontiguousarray(np.asarray(inp[name], np.float32).reshape(shp))
    return m


def host_consts():
    c = {}
    c["c_ident"] = np.eye(128, dtype=np.float32).astype(ml_dtypes.bfloat16)
    c["c_identf"] = np.eye(128, dtype=np.float32)
    p = np.arange(128)
    inv = (10000.0 ** (-(np.arange(0, 64, 2, dtype=np.float32)) / 64)).astype(np.float32)
    c["c_invf"] = inv[p % 32].reshape(128, 1).astype(np.float32)
    c["c_pos"] = np.broadcast_to(np.arange(S_LEN, dtype=np.float32), (128, S_LEN)).copy()
    mrow = (p >= 64).astype(np.float32).reshape(1, 128)
    mneg = np.where(p < 64, -30000.0, 0.0).astype(np.float32).reshape(1, 128)
    c["c_mask2"] = np.concatenate([mrow, mneg], 0).astype(ml_dtypes.bfloat16)
    sc = -math.exp(-0.5)
    j = p[:, None]; i = p[None, :]
    c["c_tri"] = np.stack([(j <= i) * sc, (j < i) * sc, (j > i) * sc], 1).astype(np.float32)
    c["c_mskL"] = (j > i).astype(np.float32)
    c["c_msk"] = np.stack([(j < i), (j <= i)], 1).astype(np.float32)
    blk = ((j // 64) == (i // 64)).astype(np.float32)
    c["c_blk"] = blk
    c["c_hsel"] = np.stack([(p < 64), (p >= 64)], 1).astype(np.float32)
    return c


class LoadCast:
    def __init__(self, nc, S, stage):
        self.nc, self.S, self.stage, self.i = nc, S, stage, 0
        self.skey = "stage"

    def pieces(self, W, src, nchunk, c0, ncols, key, r0=0, eng="act"):
        per = (4 * 512) // ncols
        out = []
        for g0 in range(0, nchunk, per):
            n = min(per, nchunk - g0)
            out.append(lambda g0=g0, n=n: self._piece(W, src, g0, n, c0, ncols, key, r0, eng))
        return out

    def _piece(self, W, src, g0, n, c0, ncols, key, r0, eng):
        nc, S = self.nc, self.S
        b = self.i % len(self.stage)
        self.i += 1
        st = self.stage[b][:].rearrange("p a b -> p (a b)")[:, 0:n * ncols].rearrange("p (a b) -> p a b", a=n)
        sap = src[r0 + g0 * 128:r0 + (g0 + n) * 128, c0:c0 + ncols].rearrange("(c p) n -> p c n", p=128)
        S.dma("sp", lambda: nc.sync.dma_start(out=st, in_=sap), f"ld{self.skey}{b}", writes=[f"{self.skey}{b}"], serial=True)
        if eng == "dve":
            S.op("dve", lambda: nc.vector.tensor_copy(out=W[:, g0:g0 + n, :], in_=st), reads=[f"{self.skey}{b}"], writes=[key])
        else:
            S.op("act", lambda: nc.scalar.copy(out=W[:, g0:g0 + n, :], in_=st), reads=[f"{self.skey}{b}"], writes=[key])

    def load(self, W, src, nchunk, c0, ncols, key, r0=0, eng="act"):
        nc, S = self.nc, self.S
        per = (4 * 512) // ncols
        for g0 in range(0, nchunk, per):
            n = min(per, nchunk - g0)
            b = self.i % len(self.stage)
            self.i += 1
            st = self.stage[b][:].rearrange("p a b -> p (a b)")[:, 0:n * ncols].rearrange("p (a b) -> p a b", a=n)
            sap = src[r0 + g0 * 128:r0 + (g0 + n) * 128, c0:c0 + ncols].rearrange("(c p) n -> p c n", p=128)
            S.dma("sp", lambda st=st, sap=sap: nc.sync.dma_start(out=st, in_=sap), f"ld{self.skey}{b}", writes=[f"{self.skey}{b}"], serial=True)
            if eng == "dve":
                S.op("dve", lambda st=st, W=W, g0=g0, n=n: nc.vector.tensor_copy(out=W[:, g0:g0 + n, :], in_=st),
                     reads=[f"{self.skey}{b}"], writes=[key])
            else:
                S.op("act", lambda st=st, W=W, g0=g0, n=n: nc.scalar.copy(out=W[:, g0:g0 + n, :], in_=st),
                     reads=[f"{self.skey}{b}"], writes=[key])


def build(debug=None, stop_after=None):
    debug = debug or []
    nc = bass.Bass("TRN2", target_bir_lowering=False)
    x_in = nc.dram_tensor("x", [S_LEN, D], F32, kind="ExternalInput").ap()
    P = {}
    for name, shp in PARAM_SPECS:
        P[name] = nc.dram_tensor(name, shp, F32, kind="ExternalInput").ap()
    for name, (o_, l_) in SMALL_OFF.items():
        P[name] = P["blob"][o_:o_ + l_]
    C = {}
    for name, arr in host_consts().items():
        dt = BF16 if arr.dtype == ml_dtypes.bfloat16 else F32
        C[name] = nc.dram_tensor(name, list(arr.shape), dt, kind="ExternalInput").ap()
    out = nc.dram_tensor("out", [S_LEN, D], F32, kind="ExternalOutput").ap()
    dbg_out = {}

    es = ExitStack()
    with es:
        sems = [es.enter_context(nc.semaphore(f"s{i}")) for i in range(96)]
        S = Sched(nc, sems)

        def sb(name, shape, dt):
            return es.enter_context(nc.sbuf_tensor(name, shape, dt))

        def dbg(name, ap_fn, shape, dt, reads):
            if name in debug:
                d = nc.dram_tensor("dbg_" + name, shape, dt, kind="ExternalOutput").ap()
                dbg_out[name] = d
                S.dma("sp", lambda: nc.sync.dma_start(out=d, in_=ap_fn()), "dbg", reads=reads)

        idt = sb("idt", [128, 128], BF16)
        gmix = sb("gmix", [128, 8], F32)
        mid = es.enter_context(ExitStack())

        def sbm(name, shape, dt):
            return mid.enter_context(nc.sbuf_tensor(name, shape, dt))
        hT = sbm("hT", [128, 8, S_LEN + OFF], BF16)
        S.dma("sp", lambda: nc.sync.dma_start(out=idt[:], in_=C["c_ident"]), "const", writes=["idt"])
        S.dma("sp", lambda: nc.sync.dma_start(out=gmix[:], in_=P["norm_mix_g"].rearrange("(c p) -> p c", p=128),
                                              allow_slow_non_contiguous=True), "const", writes=["gmix"])
        S.op("pool", lambda: nc.gpsimd.memset(hT[:, :, 0:OFF], 0.0), writes=["hTpad"])

        with ExitStack() as ph:
            def psb(name, shape, dt):
                return ph.enter_context(nc.sbuf_tensor(name, shape, dt))
            xt = [psb(f"xt{i}", [128, 1024], F32) for i in range(2)]
            xn = [psb(f"xn{i}", [128, 1024], BF16) for i in range(2)]
            junk = psb("junk", [128, 1024], BF16)
            ss = psb("ss", [128, 16], F32)
            rs = psb("rs", [128, 16], F32)
            rstd = psb("rstd", [128, 16], F32)
            tp = [ph.enter_context(nc.psum_tensor(f"tp{i}", [128, 8, 128], BF16)) for i in range(2)]
            pend0 = []
            for t in range(NT):
                b = t % 2
                S.dma("sp", lambda t=t, b=b: nc.sync.dma_start(out=xt[b][:], in_=x_in[t * 128:(t + 1) * 128, :]),
                      f"x{t}", writes=[f"xt{b}"])
                S.op("act", lambda t=t, b=b: nc.scalar.activation(out=junk[:], in_=xt[b][:], func=AF.Square,
                                                                  accum_out=ss[:, t:t + 1]),
                     reads=[f"xt{b}"], writes=["junk", f"ss{t}"])
                S.op("act", lambda t=t: nc.scalar.activation(out=rs[:, t:t + 1], in_=ss[:, t:t + 1], func=AF.Sqrt,
                                                             scale=1.0 / 1024, bias=1e-6),
                     reads=[f"ss{t}"], writes=[f"rs{t}"])
                S.op("dve", lambda t=t: nc.vector.reciprocal(out=rstd[:, t:t + 1], in_=rs[:, t:t + 1]),
                     reads=[f"rs{t}"], writes=[f"rstd{t}"])
                S.op("dve", lambda t=t, b=b: nc.vector.tensor_scalar(out=xn[b][:], in0=xt[b][:], scalar1=rstd[:, t:t + 1],
                                                                     scalar2=None, op0=ALU.mult),
                     reads=[f"xt{b}", f"rstd{t}"], writes=[f"xn{b}"])
                for dc in range(8):
                    S.op("pe", lambda b=b, dc=dc: nc.tensor.transpose(out=tp[b][:, dc, :], in_=xn[b][:, dc * 128:(dc + 1) * 128],
                                                                      identity=idt[:]),
                         reads=[f"xn{b}", "idt"], writes=[f"tp{b}"])
                pend0.append(lambda t=t, b=b: S.op("dve", lambda: nc.vector.tensor_tensor(
                    out=hT[:, :, OFF + t * 128:OFF + (t + 1) * 128], in0=tp[b][:],
                    in1=gmix[:].unsqueeze(2).to_broadcast([128, 8, 128]), op=ALU.mult),
                    reads=[f"tp{b}", "gmix"], writes=[f"hT{t}"]))
                if len(pend0) > 1:
                    pend0.pop(0)()
            while pend0:
                pend0.pop(0)()
            run_block(nc, S)
        HT_ALL = [f"hT{t}" for t in range(NT)] + ["hTpad"]
        dbg("hT", lambda: hT[:, :, OFF:OFF + S_LEN], [128, 8, S_LEN], BF16, HT_ALL)
        if stop_after == "0":
            run_block(nc, S, ["dbg"])
            return nc, dbg_out

        yaT = sbm("yaT", [128, 4, S_LEN], BF16)
        if 'skipA' not in SK:
            phase_A(nc, S, P, C, hT, yaT, HT_ALL, dbg, stop_after)
        if stop_after in ('A1','A2'):
            run_block(nc, S, ['dbg'])
            return nc, dbg_out
        dbg("yaT", lambda: yaT[:], [128, 4, S_LEN], BF16, ["yaT"])
        if stop_after == "A":
            run_block(nc, S, ["dbg"])
            return nc, dbg_out
        ybT = sbm("ybT", [128, 4, S_LEN], BF16)
        phase_B(nc, S, P, C, hT, ybT, idt, HT_ALL, dbg, stop_after)
        dbg("ybT", lambda: ybT[:], [128, 4, S_LEN], BF16, ["ybT"])
        if stop_after in ("B", "B1"):
            run_block(nc, S, ["dbg"])
            return nc, dbg_out
        xmid = nc.dram_tensor("xmid_scratch", [S_LEN, D], F32, kind="Internal").ap()
        S.pe_inorder = True
        phase_C(nc, S, P, C, x_in, xmid, hT, yaT, ybT, idt, gmix)
        if stop_after == "C":
            if "xmid" in debug:
                d = nc.dram_tensor("dbg_xmid", [S_LEN, D], F32, kind="ExternalOutput").ap()
                S.dma("sp", lambda: nc.sync.dma_start(out=d, in_=xmid), "dbg", reads=[f"xmid{t}" for t in range(NT)])
            run_block(nc, S, ["dbg"])
            return nc, dbg_out
        mid.close()
        phase_D(nc, S, P, C, xmid, out, idt)
    return nc, dbg_out


def phase_A(nc, S, P, C, hT, yaT, HT_ALL, dbg, stop_after=None):
    LAMBDA_INIT = 0.8 - 0.6 * math.exp(-0.3 * 0)
    with ExitStack() as ph:
        def psb(name, shape, dt):
            return ph.enter_context(nc.sbuf_tensor(name, shape, dt))

        def pps(name):
            return ph.enter_context(nc.psum_tensor(name, [128, 512], F32))
        Wa = psb("Wa", [128, 8, 512], BF16)
        War = psb("War", [128, 8, 512], BF16)
        Wv = psb("Wv", [128, 8, 512], BF16)
        QrT = psb("QrT", [128, 4, S_LEN], BF16)
        KrT = psb("KrT", [128, 4, S_LEN], BF16)
        Vtok = psb("Vtok", [128, NT, 512], BF16)
        cosT = psb("cosT", [128, S_LEN], F32)
        sinT = psb("sinT", [128, S_LEN], F32)
        ti = psb("ti", [128, 512], I32)
        invf = psb("invf", [128, 1], F32)
        mask2 = psb("mask2", [1, 2, 128], BF16)
        ones_bf = psb("ones_bf", [128, 128], BF16)
        ones_f = psb("ones_f", [128, 128], F32)
        lqk = psb("lqk", [128, 256], F32)
        lprod = psb("lprod", [128, 2, 64], F32)
        lsum = psb("lsum", [128, 2], F32)
        lexp = psb("lexp", [128, 2], F32)
        neglam = psb("neglam", [128, 1], F32)
        gsub = psb("gsub", [128, 1], F32)
        gsub2 = psb("gsub2", [128, 1], F32)
        mhalf = psb("mhalf", [128, 512], F32)
        pT = [psb(f"pT{i}", [128, 512], BF16) for i in range(3)]
        t1 = [psb(f"ra{i}", [128, 512], F32) for i in range(2)]
        t2 = [psb(f"rb{i}", [128, 512], F32) for i in range(2)]
        rc = psb("rc", [128, 512], F32)
        s0s = psb("s0s", [128, 512], F32)
        Pq = psb("Pq", [128, 512], F32)
        P2 = psb("P2", [128, 512], F32)
        on = [psb(f"on{i}", [128, 512], F32) for i in range(2)]
        yv = psb("yv", [128, 512], F32)
        ysq = psb("ysq", [128, 512], F32)
        vv = psb("vv", [128, 512], F32)
        ri = psb("ri", [128, 512], F32)
        sc = [pps(f"sc{i}") for i in range(3)]
        oT = [pps(f"oT{i}") for i in range(2)]
        sm = [pps(f"sm{i}") for i in range(2)]
        ssq = pps("ssq")

        w_in = P["w_in"]
        stage = [psb(f"stage{i}", [128, 4, 512], F32) for i in range(2)]
        ld = LoadCast(nc, S, stage)
        ld.load(Wv, w_in, 8, 1024, 512, "Wv")
        if 'allA' in SK:
            run_block(nc, S, ['dbg'] if 'dbg' in S.groups else [])
            return
        if 'noinvf' not in SK:
            S.dma("sp", lambda: nc.sync.dma_start(out=invf[:], in_=C["c_invf"]), "cA", writes=["invf"])
        if 'nopos' not in SK:
            S.dma("sp", lambda: nc.sync.dma_start(out=sinT[:], in_=C["c_pos"]), "cA", writes=["sinT"])
        if 'nomask' not in SK:
          S.dma("sp", lambda: nc.sync.dma_start(out=mask2[:], in_=C["c_mask2"].rearrange("(o a) b -> o a b", o=1)), "cA", writes=["mask2"])
        o_lq = SMALL_OFF["da_lq1"][0]
        S.dma("sp", lambda: nc.sync.dma_start(out=lqk[:], in_=P["blob"][o_lq:o_lq + 256].partition_broadcast(128)), "cA", writes=["lqk"])
        if 'nogsub' not in SK:
          S.dma("sp", lambda: nc.sync.dma_start(out=gsub[:], in_=P["da_subln_g"].rearrange("(p o) -> p o", o=1)),
              "cA", writes=["gsub"])
        if 'c2' in SK:
            run_block(nc, S, ['dbg'] if 'dbg' in S.groups else [])
            return
        S.op("pool", lambda: nc.gpsimd.memset(ones_bf[:], 1.0), writes=["ones_bf"])
        S.op("pool", lambda: nc.gpsimd.memset(ones_f[:], 1.0), writes=["ones_f"])
        S.op("pool", lambda: nc.gpsimd.memset(mhalf[:], -0.5), writes=["mhalf"])
        def sec_lam():
            for i in range(2):
                S.op("dve", lambda i=i: nc.vector.tensor_tensor(out=lprod[:, i, :], in0=lqk[:, 128 * i:128 * i + 64], in1=lqk[:, 128 * i + 64:128 * i + 128], op=ALU.mult),
                     reads=["lqk"], writes=["lprod"])
            S.op("dve", lambda: nc.vector.reduce_sum(out=lsum[:], in_=lprod[:], axis=AX.X), reads=["lprod"], writes=["lsum"])
            S.op("act", lambda: nc.scalar.activation(out=lexp[:], in_=lsum[:], func=AF.Exp), reads=["lsum"], writes=["lexp"])
            S.op("dve", lambda: nc.vector.tensor_tensor(out=neglam[:], in0=lexp[:, 1:2], in1=lexp[:, 0:1], op=ALU.subtract),
                 reads=["lexp"], writes=["neglam"])
            S.op("dve", lambda: nc.vector.tensor_scalar(out=neglam[:], in0=neglam[:], scalar1=-LAMBDA_INIT, scalar2=None, op0=ALU.add),
                 reads=["neglam"], writes=["neglam"])
            S.op("dve", lambda: nc.vector.tensor_scalar(out=gsub2[:], in0=gsub[:], scalar1=1.0 - LAMBDA_INIT, scalar2=None, op0=ALU.mult),
                 reads=["gsub"], writes=["gsub2"])
        def sec_rope():
            TWO_PI = 2.0 * math.pi
            S.op("dve", lambda: nc.vector.tensor_scalar(out=sinT[:], in0=sinT[:], scalar1=invf[:, 0:1], scalar2=None, op0=ALU.mult),
                 reads=["sinT", "invf"], writes=["sinT"])
            tf = t1[0]
            for j in range(4):
                cs = slice(j * 512, (j + 1) * 512)
                for tab, shift, nm in ((cosT, math.pi / 2, "cosT"), (sinT, 0.0, "sinT")):
                    S.op("dve", lambda shift=shift, cs=cs: nc.vector.tensor_scalar(out=tf[:], in0=sinT[:, cs], scalar1=shift, scalar2=1.0 / TWO_PI,
                                                                                   op0=ALU.add, op1=ALU.mult),
                         reads=["sinT"], writes=["t10"])
                    S.op("dve", lambda: nc.vector.tensor_copy(out=ti[:], in_=tf[:]), reads=["t10"], writes=["ti"])
                    S.op("dve", lambda: nc.vector.tensor_copy(out=tf[:], in_=ti[:]), reads=["ti"], writes=["t10"])
                    S.op("dve", lambda tab=tab, cs=cs: nc.vector.scalar_tensor_tensor(out=tab[:, cs], in0=tf[:], scalar=-TWO_PI, in1=sinT[:, cs],
                                                                                      op0=ALU.mult, op1=ALU.add),
                         reads=["t10", "sinT"], writes=[nm])
                    S.op("dve", lambda tab=tab, cs=cs, shift=shift: nc.vector.tensor_scalar(out=tab[:, cs], in0=tab[:, cs], scalar1=shift, scalar2=-math.pi,
                                                                                            op0=ALU.add, op1=ALU.max),
                         reads=[nm], writes=[nm])
                    S.op("dve", lambda tab=tab, cs=cs: nc.vector.tensor_scalar(out=tab[:, cs], in0=tab[:, cs], scalar1=math.pi, scalar2=None, op0=ALU.min),
                         reads=[nm], writes=[nm])
                    S.op("act", lambda tab=tab, cs=cs: nc.scalar.activation(out=tab[:, cs], in_=tab[:, cs], func=AF.Sin), reads=[nm], writes=[nm])
        def sec_rot(nm):
            Wv4 = Wa[:].rearrange("p c (j h e) -> p (c j) h e", j=8, h=2, e=32)
            Wr4 = War[:].rearrange("p c (j h e) -> p (c j) h e", j=8, h=2, e=32)
            S.op("dve", lambda: nc.vector.tensor_scalar(out=Wr4[:, :, 0, :], in0=Wv4[:, :, 1, :], scalar1=-1.0, scalar2=None, op0=ALU.mult),
                 reads=["Wa"], writes=["War0"])
            S.op("dve", lambda: nc.vector.tensor_copy(out=Wr4[:, :, 1, :], in_=Wv4[:, :, 0, :]), reads=["Wa"], writes=["War1"])
        if 'lam' not in SK: sec_lam()
        if 'rope' not in SK: sec_rope()
        dbg('cosT', lambda: cosT[:], [128, S_LEN], F32, ['cosT'])
        dbg('sinT', lambda: sinT[:], [128, S_LEN], F32, ['sinT'])
        dbg('neglam', lambda: neglam[:], [128, 1], F32, ['neglam'])
        if stop_after == 'A1':
            run_block(nc, S, ['dbg'])
            return
        it = 0
        for nm, c0, dst in (("Wk", 512, KrT), ("Wq", 0, QrT)):
            ld.load(Wa, w_in, 8, c0, 512, "Wa")
            sec_rot(nm)
            W, Wr = Wa, War
            for h in range(4):
                for tg in range(4):
                    b = it % 2
                    it += 1
                    pa, pb = (sc[0], sc[1]) if b == 0 else (oT[0], oT[1])
                    pak, pbk = ("sc0", "sc1") if b == 0 else ("oT0", "oT1")
                    for dc in range(8):
                        S.op("pe", lambda W=W, h=h, dc=dc, tg=tg, pa=pa: nc.tensor.matmul(
                            pa[:], lhsT=W[:, dc, h * 128:(h + 1) * 128], rhs=hT[:, dc, OFF + tg * 512:OFF + (tg + 1) * 512],
                            start=(dc == 0), stop=(dc == 7)), reads=["Wa"] + HT_ALL, writes=[pak])
                    for dc in range(8):
                        S.op("pe", lambda Wr=Wr, h=h, dc=dc, tg=tg, pb=pb: nc.tensor.matmul(
                            pb[:], lhsT=Wr[:, dc, h * 128:(h + 1) * 128], rhs=hT[:, dc, OFF + tg * 512:OFF + (tg + 1) * 512],
                            start=(dc == 0), stop=(dc == 7)), reads=["War0", "War1"] + HT_ALL, writes=[pbk])
                    S.op("dve", lambda b=b, tg=tg, pa=pa: nc.vector.tensor_tensor(out=t1[b][:], in0=pa[:], in1=cosT[:, tg * 512:(tg + 1) * 512], op=ALU.mult),
                         reads=[pak, "cosT"], writes=[f"t1{b}"])
                    S.op("dve", lambda b=b, tg=tg, pb=pb: nc.vector.tensor_tensor(out=t2[b][:], in0=pb[:], in1=sinT[:, tg * 512:(tg + 1) * 512], op=ALU.mult),
                         reads=[pbk, "sinT"], writes=[f"t2{b}"])
                    S.op("dve", lambda b=b, tg=tg, h=h, dst=dst: nc.vector.tensor_tensor(out=dst[:, h, tg * 512:(tg + 1) * 512], in0=t1[b][:], in1=t2[b][:], op=ALU.add),
                         reads=[f"t1{b}", f"t2{b}"], writes=[("qk", id(dst), h, tg)])
        for t in range(NT):
            pv, pvk = (sc[2], "sc2") if t % 2 == 0 else (sm[0], "sm0")
            for dc in range(8):
                S.op("pe", lambda t=t, dc=dc, pv=pv: nc.tensor.matmul(pv[:], lhsT=hT[:, dc, OFF + t * 128:OFF + (t + 1) * 128], rhs=Wv[:, dc, :],
                                                                      start=(dc == 0), stop=(dc == 7)), reads=["Wv"] + HT_ALL, writes=[pvk])
            S.op("act", lambda t=t, pv=pv: nc.scalar.copy(out=Vtok[:, t, :], in_=pv[:]), reads=[pvk], writes=[f"V{t}"])
        dbg("QrT", lambda: QrT[:], [128, 4, S_LEN], BF16, [("qk", id(QrT), h, tg) for h in range(4) for tg in range(4)])
        dbg("KrT", lambda: KrT[:], [128, 4, S_LEN], BF16, [("qk", id(KrT), h, tg) for h in range(4) for tg in range(4)])
        dbg("Vtok", lambda: Vtok[:], [128, NT, 512], BF16, [f"V{t}" for t in range(NT)])

        if stop_after == 'A2':
            run_block(nc, S, ['dbg'])
            return
        LOOK = 2
        its = []
        fi = 0
        for h in range(4):
            for qc in range(4):
                for c in range(2):
                    fb = fi % 2
                    fi += 1
                    nk = 4 * qc + 4
                    for kt in range(nk):
                        its.append((h, qc, c, fb, nk, kt))

        def emit_S(i):
            h, qc, c, fb, nk, kt = its[i]
            off = max(0, kt - 4 * qc) * 128
            diag = kt >= 4 * qc
            sb_ = i % 3
            qk_reads = [("qk", id(QrT), h, qc), ("qk", id(KrT), h, kt // 4)]
            S.op("pe", lambda: nc.tensor.matmul(
                sc[sb_][:, off:512], lhsT=KrT[c * 64:(c + 1) * 64, h, kt * 128:(kt + 1) * 128],
                rhs=QrT[c * 64:(c + 1) * 64, h, qc * 512 + off:(qc + 1) * 512], start=True, stop=not diag),
                reads=qk_reads, writes=[f"sc{sb_}"])
            if diag:
                S.op("pe", lambda: nc.tensor.matmul(
                    sc[sb_][:, off:off + 128], lhsT=mask2[0:1, 0, :], rhs=mask2[0:1, 1, :], start=False, stop=True),
                    reads=["mask2"], writes=[f"sc{sb_}"])
            S.op("act", lambda: nc.scalar.activation(out=pT[sb_][:, off:512], in_=sc[sb_][:, off:512], func=AF.Exp, scale=0.125, bias=-4.0),
                 reads=[f"sc{sb_}"], writes=[f"pT{sb_}"])

        def emit_fin2(h, qc):
            S.op("pe", lambda: nc.tensor.matmul(ssq[:], lhsT=ones_f[:], rhs=ysq[:], start=True, stop=True),
                 reads=["ysq", "ones_f"], writes=["ssq"])
            S.op("dve", lambda: nc.vector.scalar_tensor_tensor(out=vv[:], in0=ssq[:], scalar=1.0 / 128, in1=P2[:], op0=ALU.mult, op1=ALU.add),
                 reads=["ssq", "P2"], writes=["vv"])
            S.op("act", lambda: nc.scalar.activation(out=ri[:], in_=vv[:], func=AF.Ln), reads=["vv"], writes=["ri"])
            S.op("act", lambda: nc.scalar.activation(out=ri[:], in_=ri[:], func=AF.Exp, scale=-0.5), reads=["ri"], writes=["ri"])
            S.op("dve", lambda: nc.vector.scalar_tensor_tensor(out=yaT[:, h, qc * 512:(qc + 1) * 512], in0=yv[:], scalar=gsub2[:, 0:1],
                                                               in1=ri[:], op0=ALU.mult, op1=ALU.mult),
                 reads=["yv", "ri", "gsub2"], writes=["yaT"])

        deferred = []

        def emit_PV(i):
            h, qc, c, fb, nk, kt = its[i]
            off = max(0, kt - 4 * qc) * 128
            sb_ = i % 3
            S.op("pe", lambda: nc.tensor.matmul(
                oT[fb][:, off:512], lhsT=Vtok[:, kt, h * 128:(h + 1) * 128], rhs=pT[sb_][:, off:512],
                start=(kt == 0), stop=(kt == nk - 1)), reads=[f"pT{sb_}", f"V{kt}"], writes=[f"oT{fb}"])
            S.op("pe", lambda: nc.tensor.matmul(
                sm[fb][:, off:512], lhsT=ones_bf[:], rhs=pT[sb_][:, off:512],
                start=(kt == 0), stop=(kt == nk - 1)), reads=[f"pT{sb_}", "ones_bf"], writes=[f"sm{fb}"])
            if kt == nk - 1:
                if c == 0:
                    S.op("dve", lambda: nc.vector.tensor_copy(out=on[0][:], in_=oT[fb][:]), reads=[f"oT{fb}"], writes=["on0"])
                    S.op("act", lambda: nc.scalar.copy(out=s0s[:], in_=sm[fb][:]), reads=[f"sm{fb}"], writes=["s0s"])
                else:
                    S.op("dve", lambda: nc.vector.tensor_tensor(out=on[1][:], in0=on[0][:], in1=sm[fb][:], op=ALU.mult), reads=["on0", f"sm{fb}"], writes=["on1"])
                    S.op("dve", lambda: nc.vector.tensor_tensor(out=rc[:], in0=oT[fb][:], in1=s0s[:], op=ALU.mult), reads=[f"oT{fb}", "s0s"], writes=["rc"])
                    S.op("dve", lambda: nc.vector.tensor_tensor(out=Pq[:], in0=s0s[:], in1=sm[fb][:], op=ALU.mult), reads=["s0s", f"sm{fb}"], writes=["Pq"])
                    S.op("dve", lambda: nc.vector.scalar_tensor_tensor(out=yv[:], in0=rc[:], scalar=neglam[:, 0:1], in1=on[1][:], op0=ALU.mult, op1=ALU.add),
                         reads=["rc", "on1", "neglam"], writes=["yv"])
                    S.op("dve", lambda: nc.vector.tensor_tensor(out=ysq[:], in0=yv[:], in1=yv[:], op=ALU.mult), reads=["yv"], writes=["ysq"])
                    S.op("dve", lambda: nc.vector.scalar_tensor_tensor(out=P2[:], in0=Pq[:], scalar=1e-5, in1=Pq[:], op0=ALU.mult, op1=ALU.mult),
                         reads=["Pq"], writes=["P2"])
                    deferred.append([3, lambda: emit_fin2(h, qc)])

        n_it = len(its)
        for i in range(n_it + LOOK):
            if i < n_it:
                emit_S(i)
            if i - LOOK >= 0:
                emit_PV(i - LOOK)
            for d in deferred:
                d[0] -= 1
            for d in [d for d in deferred if d[0] <= 0]:
                d[1]()
                deferred.remove(d)
        for d in deferred:
            d[1]()
        run_block(nc, S)


def phase_B(nc, S, P, C, hT, ybT, idt, HT_ALL, dbg, stop_after=None):
    RW0 = 1536
    w_in = P["w_in"]
    S.pe_inorder = 'nobf1' not in SK
    with ExitStack() as ph:
        def psb(name, shape, dt):
            return ph.enter_context(nc.sbuf_tensor(name, shape, dt))
        rT = psb("rT", [128, 4, S_LEN], BF16)
        kT = psb("kT", [128, 4, S_LEN], BF16)
        Vt = psb("Vt", [128, NT, 512], BF16)
        tanhwdT = psb("tanhwdT", [64, S_LEN], BF16)
        adT = psb("adT", [64, S_LEN], BF16)
        sgdT = psb("sgdT", [128, S_LEN], BF16)
        w0_bc = psb("w0_bc", [128, 512], F32)
        lng_bc = psb("lng_bc", [128, 512], F32)
        lnb_bc = psb("lnb_bc", [128, 512], F32)
        cols = psb("cols", [128, 4, 4], F32)
        omka = psb("omka", [128, 4], F32)
        w2_bf = psb("w2_bf", [64, 512], BF16)
        a2_bf = psb("a2_bf", [64, 512], BF16)
        g2_bf = psb("g2_bf", [128, 512], BF16)
        tri = psb("tri", [128, 3, 128], F32)
        mskA = psb("mskA", [128, 2, 128], F32)
        mskL = psb("mskL", [128, 128], F32)
        blk = psb("blk", [128, 128], F32)
        hsel = psb("hsel", [128, 2], F32)
        RKt = psb("RKt", [128, 4, 2], F32)
        pA = [ph.enter_context(nc.psum_tensor(f"pA{i}", [128, 512], F32)) for i in range(4)]
        pB = [ph.enter_context(nc.psum_tensor(f"pB{i}", [128, 512], F32)) for i in range(4)]
        _pn = {}
        for i in range(4):
            _pn[id(pA[i])] = f"pA{i}"
            _pn[id(pB[i])] = f"pB{i}"

        def pn(t):
            return _pn[id(t)]

        def cdma(out_ap, in_ap, key, slow=False):
            if 'no_' + key in SK:
                return
            S.dma("sp", lambda: nc.sync.dma_start(out=out_ap, in_=in_ap, allow_slow_non_contiguous=slow), "cB", writes=[key])
        cdma(w0_bc[:], P["rw_w0"].partition_broadcast(128), "w0_bc")
        cdma(lng_bc[:], P["rw_ln_g"].partition_broadcast(128), "lng_bc")
        cdma(lnb_bc[:], P["rw_ln_b"].partition_broadcast(128), "lnb_bc")
        for i, nm in enumerate(("rw_a0", "rw_k_k", "rw_k_a", "rw_r_k")):
            cdma(cols[:, i, :], P[nm].rearrange("(c p) -> p c", p=128), "cols", slow=True)
        cdma(tri[:], C["c_tri"], "tri")
        cdma(mskA[:], C["c_msk"], "mskA")
        cdma(mskL[:], C["c_mskL"], "mskL")
        cdma(blk[:], C["c_blk"], "blk")
        cdma(hsel[:], C["c_hsel"], "hsel")
        S.op("dve", lambda: nc.vector.tensor_scalar(out=omka[:], in0=cols[:, 2, :], scalar1=-1.0, scalar2=1.0, op0=ALU.mult, op1=ALU.add),
             reads=["cols"], writes=["omka"])
        for hp in range(4):
            S.op("dve", lambda hp=hp: nc.vector.tensor_scalar(out=RKt[:, hp, :], in0=hsel[:], scalar1=cols[:, 3, hp:hp + 1], scalar2=None, op0=ALU.mult),
                 reads=["cols", "hsel"], writes=["RKt"])

        with ExitStack() as sc1:
            def s1b(name, shape, dt):
                return sc1.enter_context(nc.sbuf_tensor(name, shape, dt))
            mu_bc = s1b("mu_bc", [128, 1792], F32)
            omu_bc = s1b("omu_bc", [128, 1792], F32)
            stage = [s1b(f"stageB{i}", [128, 4, 512], F32) for i in range(2)]
            W1 = s1b("W1", [128, 8, 512], BF16)
            W2 = s1b("W2", [128, 8, 512], BF16)
            lorf = s1b("lorf", [128, 3, 512], F32)
            cdma(mu_bc[:], P["rw_mu"].partition_broadcast(128), "mu_bc")
            S.op("dve", lambda: nc.vector.tensor_scalar(out=omu_bc[:], in0=mu_bc[:], scalar1=-1.0, scalar2=1.0, op0=ALU.mult, op1=ALU.add),
                 reads=["mu_bc"], writes=["omu_bc"])
            cdma(lorf[0:64, 0, :], P["rw_w2"], "lorf")
            cdma(lorf[0:64, 1, :], P["rw_a2"], "lorf")
            cdma(lorf[:, 2, :], P["rw_g2"], "lorf")
            S.op("dve", lambda: nc.vector.tensor_copy(out=w2_bf[:], in_=lorf[0:64, 0, :]), reads=["lorf"], writes=["w2_bf"])
            S.op("dve", lambda: nc.vector.tensor_copy(out=a2_bf[:], in_=lorf[0:64, 1, :]), reads=["lorf"], writes=["a2_bf"])
            S.op("dve", lambda: nc.vector.tensor_copy(out=g2_bf[:], in_=lorf[:, 2, :]), reads=["lorf"], writes=["g2_bf"])
            stage_i = [0]

            def load_shift(c0, ncols):
                per = 2048 // ncols
                for g0 in range(0, 8, per):
                    n = min(per, 8 - g0)
                    b = stage_i[0] % 2
                    stage_i[0] += 1
                    st = stage[b][:].rearrange("p a b -> p (a b)")[:, 0:n * ncols].rearrange("p (a b) -> p a b", a=n)
                    sap = w_in[g0 * 128:(g0 + n) * 128, c0:c0 + ncols].rearrange("(c p) n -> p c n", p=128)
                    S.dma("sp", lambda st=st, sap=sap: nc.sync.dma_start(out=st, in_=sap), f"ldstageB{b}", writes=[f"stageB{b}"], serial=True)
                    m0 = c0 - RW0
                    S.op("dve", lambda st=st, g0=g0, n=n, m0=m0: nc.vector.tensor_tensor(
                        out=W1[:, g0:g0 + n, 0:ncols], in0=st, in1=omu_bc[:, m0:m0 + ncols].unsqueeze(1).to_broadcast([128, n, ncols]), op=ALU.mult),
                        reads=[f"stageB{b}", "omu_bc"], writes=["W1"])
                    S.op("dve", lambda st=st, g0=g0, n=n, m0=m0: nc.vector.tensor_tensor(
                        out=W2[:, g0:g0 + n, 0:ncols], in0=st, in1=mu_bc[:, m0:m0 + ncols].unsqueeze(1).to_broadcast([128, n, ncols]), op=ALU.mult),
                        reads=[f"stageB{b}", "mu_bc"], writes=["W2"])

            def proj_fm(ps, f0, M, tg):
                for dc in range(8):
                    S.op("pe", lambda dc=dc: nc.tensor.matmul(ps[0:M, :], lhsT=W1[:, dc, f0:f0 + M], rhs=hT[:, dc, OFF + tg * 512:OFF + (tg + 1) * 512],
                                                              start=(dc == 0), stop=False), reads=["W1"] + HT_ALL, writes=[pn(ps)])
                for dc in range(8):
                    S.op("pe", lambda dc=dc: nc.tensor.matmul(ps[0:M, :], lhsT=W2[:, dc, f0:f0 + M], rhs=hT[:, dc, OFF - 1 + tg * 512:OFF - 1 + (tg + 1) * 512],
                                                              start=False, stop=(dc == 7)), reads=["W2"] + HT_ALL, writes=[pn(ps)])
            pi = [0]

            def nextps():
                pi[0] += 1
                return (pA + pB)[pi[0] % 8]
            if 'b1a' in SK:
                run_block(nc, S)
                return
            load_shift(RW0 + 1536, 256)
            for tg in range(4):
                cs = slice(tg * 512, (tg + 1) * 512)
                ps = nextps()
                proj_fm(ps, 0, 64, tg)
                S.op("act", lambda ps=ps, cs=cs: nc.scalar.activation(out=tanhwdT[:, cs], in_=ps[0:64, :], func=AF.Tanh),
                     reads=[pn(ps)], writes=["tanhwdT"])
                ps = nextps()
                proj_fm(ps, 64, 64, tg)
                S.op("act", lambda ps=ps, cs=cs: nc.scalar.copy(out=adT[:, cs], in_=ps[0:64, :]), reads=[pn(ps)], writes=["adT"])
                ps = nextps()
                proj_fm(ps, 128, 128, tg)
                S.op("act", lambda ps=ps, cs=cs: nc.scalar.activation(out=sgdT[:, cs], in_=ps[:], func=AF.Sigmoid),
                     reads=[pn(ps)], writes=["sgdT"])
            if 'b1b' in SK:
                run_block(nc, S)
                return
            load_shift(RW0 + 1024, 512)
            for t in range(NT):
                ps = nextps()
                for dc in range(8):
                    S.op("pe", lambda ps=ps, t=t, dc=dc: nc.tensor.matmul(ps[:], lhsT=hT[:, dc, OFF + t * 128:OFF + (t + 1) * 128], rhs=W1[:, dc, :],
                                                                          start=(dc == 0), stop=False), reads=["W1"] + HT_ALL, writes=[pn(ps)])
                for dc in range(8):
                    S.op("pe", lambda ps=ps, t=t, dc=dc: nc.tensor.matmul(ps[:], lhsT=hT[:, dc, OFF - 1 + t * 128:OFF - 1 + (t + 1) * 128], rhs=W2[:, dc, :],
                                                                          start=False, stop=(dc == 7)), reads=["W2"] + HT_ALL, writes=[pn(ps)])
                S.op("act", lambda ps=ps, t=t: nc.scalar.copy(out=Vt[:, t, :], in_=ps[:]), reads=[pn(ps)], writes=[f"Vt{t}"])
            if 'b1c' in SK:
                run_block(nc, S)
                return
            for c0, dst, nm in ((RW0, rT, "rT"), (RW0 + 512, kT, "kT")):
                load_shift(c0, 512)
                for hp in range(4):
                    for tg in range(4):
                        ps = nextps()
                        proj_fm(ps, hp * 128, 128, tg)
                        eng = "act" if (hp + tg) % 2 == 0 else "dve"
                        if eng == "act":
                            S.op("act", lambda ps=ps, hp=hp, tg=tg, dst=dst: nc.scalar.copy(out=dst[:, hp, tg * 512:(tg + 1) * 512], in_=ps[:]),
                                 reads=[pn(ps)], writes=[(nm, hp, tg)])
                        else:
                            S.op("dve", lambda ps=ps, hp=hp, tg=tg, dst=dst: nc.vector.tensor_copy(out=dst[:, hp, tg * 512:(tg + 1) * 512], in_=ps[:]),
                                 reads=[pn(ps)], writes=[(nm, hp, tg)])
            S.barrier()
            run_block(nc, S)
        dbg("rT", lambda: rT[:], [128, 4, S_LEN], BF16, [])
        dbg("kT", lambda: kT[:], [128, 4, S_LEN], BF16, [])
        dbg("Vt", lambda: Vt[:], [128, NT, 512], BF16, [])
        dbg("sgdT", lambda: sgdT[:], [128, S_LEN], BF16, [])
        if stop_after == "B1":
            return

        hflat = hT[:]
        def hview(dc, n):
            return hflat[:, dc, 0:n]
        ARt = [hview(hp, 1024).rearrange("p (c t i) -> p c t i", c=4, t=2) for hp in range(4)]
        btl = [hflat[:, hp, 1024:1536] for hp in range(4)]
        ktl = [hflat[:, hp, 1536:2048] for hp in range(4)]
        BhT = [hview(4 + hp, 512).rearrange("p (c f) -> p c f", c=4) for hp in range(4)]
        KhT = [hflat[:, 4 + hp, 512:1024].rearrange("p (c f) -> p c f", c=4) for hp in range(4)]
        bhkh = [hflat[:, 4 + hp, 1024:2048].rearrange("p (t n) -> p t n", t=2) for hp in range(4)]
        with ExitStack() as sc2:
            def s2b(name, shape, dt):
                return sc2.enter_context(nc.sbuf_tensor(name, shape, dt))
            NTMP = 12
            idtf = s2b("idtf", [128, 128], F32)
            S.dma("sp", lambda: nc.sync.dma_start(out=idtf[:], in_=C["c_identf"]), "cB2", writes=["idtf"])
            TA = [[s2b(f"TA{par}_{i}", [128, 512], F32) for i in range(6)] for par in range(2)]
            TB = [s2b(f"TB_{i}", [128, 512], F32) for i in range(4)]
            gam = s2b("gam", [128, 4, 16], F32)
            bon = s2b("bon", [128, 16, 8], F32)
            MA = [s2b(f"MA{hp}", [128, 2, 2, 128], BF16) for hp in range(4)]
            MB = [s2b(f"MB{hp}", [128, 2, 2, 128], BF16) for hp in range(4)]
            N0 = [s2b(f"N0{hp}", [128, 2, 128], BF16) for hp in range(4)]
            NMs = [[s2b(f"NMs{hp}_{i}", [128, 2, 2, 128], BF16) for i in range(2)] for hp in range(4)]
            Pb = [[s2b(f"Pb{hp}_{i}", [128, 2, 128], BF16) for i in range(2)] for hp in range(4)]
            S_f = [s2b(f"S_f{hp}", [128, 2, 64], F32) for hp in range(4)]
            S_bf = [s2b(f"S_bf{hp}", [128, 2, 64], BF16) for hp in range(4)]
            X_bf = [s2b(f"X_bf{hp}", [128, 2, 64], BF16) for hp in range(4)]
            U_bf = [s2b(f"U_bf{hp}", [128, 2, 64], BF16) for hp in range(4)]
            Yt = [s2b(f"Yt{i}", [128, 512], F32) for i in range(2)]
            Y1 = s2b("Y1", [128, 512], F32)
            Y2 = s2b("Y2", [128, 512], F32)
            yb = s2b("yb", [128, 512], F32)
            st8 = s2b("st8", [128, 6, 8], F32)
            if os.environ.get('MK_KERNEL_OPLIMIT'):
                S.limit = int(os.environ['MK_KERNEL_OPLIMIT'])
            for hp in range(4):
                S.op("pool", lambda hp=hp: nc.gpsimd.memset(S_f[hp][:], 0.0), writes=[f"S_f{hp}"])
                S.op("pool", lambda hp=hp: nc.gpsimd.memset(S_bf[hp][:], 0.0), writes=[f"S_bf{hp}"])

            def v4(ap):
                return ap.rearrange("p (c i) -> p c i", c=4)

            def prep(tg, hp):
                S.pe_inorder = 'nobfp' not in SK
                if True:
                    cs = slice(tg * 512, (tg + 1) * 512)
                    par = hp % 2
                    tk = lambda i, par=par: (f"TA{par}_{i}" if i < 6 else f"TB_{i}") if i < 10 else (f"TA{par}_1" if i == 10 else "TB_7")
                    A_, B_ = pA[hp], pB[hp]
                    An, Bn = pn(A_), pn(B_)
                    T_a, T_ld, T_Eg, T_Eig, T_Ex, T_Eh = TA[par]
                    T_kk, T_sq, T_kp, T_b = TB
                    T_bh, T_kh = T_ld, T_sq
                    S.op("pe", lambda hp=hp, A_=A_: nc.tensor.matmul(A_[:], lhsT=a2_bf[0:64, hp * 128:(hp + 1) * 128], rhs=adT[0:64, cs], start=True, stop=True),
                         reads=["a2_bf", "adT"], writes=[An])
                    S.op("act", lambda hp=hp, A_=A_, T_a=T_a: nc.scalar.activation(out=T_a[:], in_=A_[:], func=AF.Sigmoid, bias=cols[:, 0, hp:hp + 1]),
                         reads=[An, "cols"], writes=[tk(0)])
                    for cc in range(4):
                        S.op("pe", lambda hp=hp, B_=B_, cc=cc: nc.tensor.matmul(B_[:, cc * 128:(cc + 1) * 128], lhsT=tanhwdT[0:64, tg * 512 + cc * 128:tg * 512 + (cc + 1) * 128],
                                                                              rhs=w2_bf[0:64, hp * 128:(hp + 1) * 128], start=True, stop=True),
                             reads=["w2_bf", "tanhwdT"], writes=[Bn])
                    S.op("dve", lambda hp=hp, B_=B_, T_ld=T_ld: nc.vector.tensor_tensor(out=v4(T_ld[:]), in0=v4(B_[:]),
                                                                                     in1=w0_bc[:, hp * 128:(hp + 1) * 128].unsqueeze(1).to_broadcast([128, 4, 128]), op=ALU.add),
                         reads=[Bn, "w0_bc"], writes=[tk(1)])
                    S.op("act", lambda T_ld=T_ld: nc.scalar.activation(out=T_ld[:], in_=T_ld[:], func=AF.Sigmoid), reads=[tk(1)], writes=[tk(1)])
                    def cum(ps, psn, which):
                        for cc in range(4):
                            S.op("pe", lambda cc=cc, ps=ps, which=which, T_ld=T_ld: nc.tensor.matmul(ps[:, cc * 128:(cc + 1) * 128], lhsT=T_ld[:, cc * 128:(cc + 1) * 128],
                                                                                                rhs=tri[:, which, :], start=True, stop=True),
                                 reads=[tk(1), "tri"], writes=[psn])
                    cum(A_, An, 0)
                    S.op("act", lambda A_=A_, T_Eg=T_Eg: nc.scalar.activation(out=T_Eg[:], in_=A_[:], func=AF.Exp), reads=[An], writes=[tk(2)])
                    S.op("act", lambda A_=A_, T_Eig=T_Eig: nc.scalar.activation(out=T_Eig[:], in_=A_[:], func=AF.Exp, scale=-1.0), reads=[An], writes=[tk(3)])
                    cum(B_, Bn, 1)
                    S.op("act", lambda B_=B_, T_Ex=T_Ex: nc.scalar.activation(out=T_Ex[:], in_=B_[:], func=AF.Exp), reads=[Bn], writes=[tk(4)])
                    cum(A_, An, 2)
                    S.op("act", lambda A_=A_, T_Eh=T_Eh: nc.scalar.activation(out=T_Eh[:], in_=A_[:], func=AF.Exp), reads=[An], writes=[tk(5)])
                    S.op("dve", lambda hp=hp, T_Eg=T_Eg: nc.vector.tensor_copy(out=gam[:, hp, tg * 4:(tg + 1) * 4], in_=v4(T_Eg[:])[:, :, 127]),
                         reads=[tk(2)], writes=[("gam", hp, tg)])
                    if S.capture is not None:
                        S.capture.append("SPLIT")
                    S.op("dve", lambda hp=hp, T_kk=T_kk: nc.vector.tensor_scalar(out=T_kk[:], in0=kT[:, hp, cs], scalar1=cols[:, 1, hp:hp + 1], scalar2=None, op0=ALU.mult),
                         reads=[("kT", hp, tg), "cols"], writes=[tk(6)])
                    S.op("dve", lambda T_kk=T_kk, T_sq=T_sq: nc.vector.tensor_tensor(out=T_sq[:], in0=T_kk[:], in1=T_kk[:], op=ALU.mult), reads=[tk(6)], writes=[tk(7)])
                    S.op("pe", lambda B_=B_, T_sq=T_sq: nc.tensor.matmul(B_[:], lhsT=blk[:], rhs=T_sq[:], start=True, stop=True), reads=[tk(7), "blk"], writes=[Bn])
                    S.op("dve", lambda B_=B_, T_sq=T_sq: nc.vector.tensor_scalar(out=T_sq[:], in0=B_[:], scalar1=1e-24, scalar2=None, op0=ALU.max), reads=[Bn], writes=[tk(7)])
                    S.op("act", lambda T_sq=T_sq: nc.scalar.activation(out=T_sq[:], in_=T_sq[:], func=AF.Ln), reads=[tk(7)], writes=[tk(7)])
                    S.op("act", lambda T_sq=T_sq: nc.scalar.activation(out=T_sq[:], in_=T_sq[:], func=AF.Exp, scale=-0.5), reads=[tk(7)], writes=[tk(7)])
                    S.op("dve", lambda T_kk=T_kk, T_sq=T_sq: nc.vector.tensor_tensor(out=T_kk[:], in0=T_kk[:], in1=T_sq[:], op=ALU.mult), reads=[tk(6), tk(7)], writes=[tk(6)])
                    S.op("dve", lambda hp=hp, T_kp=T_kp, T_a=T_a: nc.vector.tensor_scalar(out=T_kp[:], in0=T_a[:], scalar1=cols[:, 2, hp:hp + 1], scalar2=omka[:, hp:hp + 1],
                                                                                   op0=ALU.mult, op1=ALU.add), reads=[tk(0), "cols", "omka"], writes=[tk(8)])
                    S.op("dve", lambda hp=hp, T_kp=T_kp: nc.vector.tensor_tensor(out=T_kp[:], in0=T_kp[:], in1=kT[:, hp, cs], op=ALU.mult),
                         reads=[tk(8), ("kT", hp, tg)], writes=[tk(8)])
                    S.op("dve", lambda T_b=T_b, T_kk=T_kk, T_a=T_a: nc.vector.tensor_tensor(out=T_b[:], in0=T_kk[:], in1=T_a[:], op=ALU.mult), reads=[tk(6), tk(0)], writes=[tk(9)])
                    S.op("dve", lambda hp=hp, T_sq=T_sq, T_kp=T_kp: nc.vector.tensor_tensor(out=T_sq[:], in0=T_kp[:], in1=rT[:, hp, cs], op=ALU.mult),
                         reads=[tk(8), tk(7), ("rT", hp, tg)], writes=[tk(7)])
                    for cc in range(4):
                        S.op("pe", lambda hp=hp, cc=cc, B_=B_, T_sq=T_sq: nc.tensor.matmul(B_[:, cc * 2:cc * 2 + 2], lhsT=T_sq[:, cc * 128:(cc + 1) * 128], rhs=RKt[:, hp, :],
                                                                                       start=True, stop=True), reads=[tk(7), "RKt"], writes=[Bn])
                    S.op("act", lambda hp=hp, B_=B_: nc.scalar.copy(out=bon[:, tg * 4:(tg + 1) * 4, 2 * hp:2 * hp + 2], in_=B_[:, 0:8].rearrange("p (c h) -> p c h", c=4)),
                         reads=[Bn], writes=[("bon", hp, tg)])
                    opk = ("ops", hp)
                    S.op("dve", lambda hp=hp, T_kk=T_kk, T_Ex=T_Ex: nc.vector.scalar_tensor_tensor(out=ARt[hp][:, :, 0, :], in0=v4(T_kk[:]), scalar=-1.0, in1=v4(T_Ex[:]),
                                                                                           op0=ALU.mult, op1=ALU.mult), reads=[tk(6), tk(4)], writes=[("atl", hp)])
                    S.op("dve", lambda hp=hp, T_Eg=T_Eg: nc.vector.tensor_tensor(out=ARt[hp][:, :, 1, :], in0=v4(rT[:, hp, cs]), in1=v4(T_Eg[:]), op=ALU.mult),
                         reads=[("rT", hp, tg), tk(2)], writes=[("rtl", hp)])
                    S.op("dve", lambda hp=hp, T_b=T_b, T_Eig=T_Eig: nc.vector.tensor_tensor(out=btl[hp], in0=T_b[:], in1=T_Eig[:], op=ALU.mult),
                         reads=[tk(9), tk(3)], writes=[("btl", hp)])
                    S.op("dve", lambda hp=hp, T_kp=T_kp, T_Eig=T_Eig: nc.vector.tensor_tensor(out=ktl[hp], in0=T_kp[:], in1=T_Eig[:], op=ALU.mult),
                         reads=[tk(8), tk(3)], writes=[("ktl", hp)])
                    S.op("dve", lambda hp=hp, T_b=T_b, T_Eh=T_Eh: nc.vector.tensor_tensor(out=T_bh[:], in0=T_b[:], in1=T_Eh[:], op=ALU.mult),
                         reads=[tk(9), tk(5)], writes=[tk(10)])
                    S.op("dve", lambda hp=hp, T_kp=T_kp, T_Eh=T_Eh: nc.vector.tensor_tensor(out=T_kh[:], in0=T_kp[:], in1=T_Eh[:], op=ALU.mult),
                         reads=[tk(8), tk(5)], writes=[tk(11)])
                    for cc in range(4):
                        S.op("pe", lambda cc=cc, A_=A_, T_bh=T_bh: nc.tensor.transpose(out=A_[:, cc * 128:(cc + 1) * 128], in_=T_bh[:, cc * 128:(cc + 1) * 128], identity=idtf[:]),
                             reads=[tk(10), "idtf"], writes=[An])
                    for cc in range(4):
                        S.op("pe", lambda cc=cc, B_=B_, T_kh=T_kh: nc.tensor.transpose(out=B_[:, cc * 128:(cc + 1) * 128], in_=T_kh[:, cc * 128:(cc + 1) * 128], identity=idtf[:]),
                             reads=[tk(11), "idtf"], writes=[Bn])
                    S.op("act", lambda hp=hp, A_=A_: nc.scalar.copy(out=BhT[hp], in_=A_[:].rearrange("p (c f) -> p c f", c=4)), reads=[An], writes=[("BhT", hp)])
                    S.op("dve", lambda hp=hp, B_=B_: nc.vector.tensor_copy(out=KhT[hp], in_=B_[:].rearrange("p (c f) -> p c f", c=4)), reads=[Bn], writes=[("KhT", hp)])

            def chunk_step(tg, cc):
                S.pe_inorder = 'bfc' in SK or 'amw' in SK
                if True:
                    c = tg * 4 + cc
                    ccs = slice(cc * 128, (cc + 1) * 128)
                    def views(hp):
                        A_, B_ = pA[hp], pB[hp]
                        return (A_, B_, pn(A_), pn(B_), A_[:].rearrange("p (h t i) -> p h t i", h=2, t=2), B_[:].rearrange("p (h t i) -> p h t i", h=2, t=2),
                                A_[:, 0:256].rearrange("p (h i) -> p h i", h=2))
                    for hp in range(4):
                        A_, B_, An, Bn, A4, B4, A3 = views(hp)
                        for h in range(2):
                            hs = slice(h * 64, (h + 1) * 64)
                            S.op("pe", lambda hp=hp, h=h, hs=hs, A4=A4: nc.tensor.matmul(A4[:, h, :, :], lhsT=btl[hp][hs, ccs], rhs=ARt[hp][hs, cc, :, :], start=True, stop=True),
                                 reads=[("btl", hp), ("atl", hp), ("rtl", hp)], writes=[An], pe_wait=(h == 1 or 'amw' not in SK))
                            S.op("pe", lambda hp=hp, h=h, hs=hs, B4=B4: nc.tensor.matmul(B4[:, h, :, :], lhsT=ktl[hp][hs, ccs], rhs=ARt[hp][hs, cc, :, :], start=True, stop=True),
                                 reads=[("ktl", hp), ("atl", hp), ("rtl", hp)], writes=[Bn], pe_wait=(h == 1 or 'amw' not in SK))
                    for hp in range(4):
                        A_, B_, An, Bn, A4, B4, A3 = views(hp)
                        S.op("dve", lambda hp=hp, A4=A4: nc.vector.tensor_tensor(out=MA[hp][:], in0=A4, in1=mskA[:].unsqueeze(1).to_broadcast([128, 2, 2, 128]), op=ALU.mult),
                             reads=[An, "mskA"], writes=[("MA", hp)])
                        S.op("dve", lambda hp=hp, B4=B4: nc.vector.tensor_tensor(out=MB[hp][:], in0=B4, in1=mskA[:].unsqueeze(1).to_broadcast([128, 2, 2, 128]), op=ALU.mult),
                             reads=[Bn, "mskA"], writes=[("MB", hp)])
                    for hp in range(4):
                        A_, B_, An, Bn, A4, B4, A3 = views(hp)
                        for h in range(2):
                            hs = slice(h * 64, (h + 1) * 64)
                            S.op("pe", lambda hp=hp, h=h, hs=hs, A3=A3: nc.tensor.matmul(A3[:, h, :], lhsT=ARt[hp][hs, cc, 0, :], rhs=btl[hp][hs, ccs], start=True, stop=True),
                                 reads=[("btl", hp), ("atl", hp)], writes=[An], pe_wait=True)
                    for hp in range(4):
                        A_, B_, An, Bn, A4, B4, A3 = views(hp)
                        S.op("dve", lambda hp=hp, A3=A3: nc.vector.tensor_tensor(out=N0[hp][:], in0=A3, in1=mskL[:].unsqueeze(1).to_broadcast([128, 2, 128]), op=ALU.mult),
                             reads=[An, "mskL"], writes=[("N", hp, 0)])
                        S.op("dve", lambda hp=hp: nc.vector.tensor_tensor(out=Pb[hp][0][:], in0=MA[hp][:, :, 0, :], in1=idt[:].unsqueeze(1).to_broadcast([128, 2, 128]), op=ALU.add),
                             reads=[("MA", hp), "idt"], writes=[("P", hp, 0)])
                    S.pe_inorder = 'nobfs' not in SK
                    for l in range(1, 7):
                        def Mget(l_, h, hp):
                            return MA[hp][:, h, 0, :] if l_ == 0 else NMs[hp][l_ % 2][:, 1, h, :]

                        def Nget(l_, h, hp):
                            return N0[hp][:, h, :] if l_ == 0 else NMs[hp][l_ % 2][:, 0, h, :]
                        for hp in range(4):
                            A_ = pA[hp]
                            An = pn(A_)
                            A4 = A_[:].rearrange("p (t h i) -> p t h i", t=2, h=2)
                            prevk = [("N", hp, l - 1), ("MA", hp) if l == 1 else ("M", hp, l - 1)]
                            for h in range(2):
                                S.op("pe", lambda h=h, hp=hp, A4=A4, l=l: nc.tensor.matmul(A4[:, 0, h, :], lhsT=Mget(l - 1, h, hp), rhs=Nget(l - 1, h, hp), start=True, stop=True),
                                     reads=prevk, writes=[An])
                                if l < 6:
                                    S.op("pe", lambda h=h, hp=hp, A4=A4, l=l: nc.tensor.matmul(A4[:, 1, h, :], lhsT=Nget(l - 1, h, hp), rhs=Mget(l - 1, h, hp), start=True, stop=True),
                                         reads=prevk, writes=[An])
                            if l < 6:
                                S.op("act", lambda hp=hp, A4=A4, l=l: nc.scalar.copy(out=NMs[hp][l % 2][:], in_=A4), reads=[An], writes=[("N", hp, l), ("M", hp, l)])
                            else:
                                S.op("act", lambda hp=hp, A4=A4, l=l: nc.scalar.copy(out=NMs[hp][l % 2][:, 0, :, :], in_=A4[:, 0, :, :]), reads=[An], writes=[("N", hp, l)])
                        for hp in range(4):
                            B_ = pB[hp]
                            Bn = pn(B_)
                            B3 = B_[:, 0:256].rearrange("p (h i) -> p h i", h=2)
                            for h in range(2):
                                S.op("pe", lambda h=h, B3=B3, hp=hp, l=l: nc.tensor.matmul(B3[:, h, :], lhsT=Nget(l, h, hp), rhs=Pb[hp][(l - 1) % 2][:, h, :], start=True, stop=True),
                                     reads=[("N", hp, l), ("P", hp, l - 1)], writes=[Bn])
                            S.op("dve", lambda hp=hp, B3=B3, l=l: nc.vector.tensor_tensor(out=Pb[hp][l % 2][:], in0=B3, in1=Pb[hp][(l - 1) % 2][:], op=ALU.add),
                                 reads=[Bn, ("P", hp, l - 1)], writes=[("P", hp, l)])
                    S.pe_inorder = 'nobfscan' not in SK
                    def TT(hp, h):
                        return Pb[hp][0][:, h, :]
                    def Bsl(hp, lo, n):
                        return pB[hp][:, lo:lo + n]
                    for hp in range(4):
                        Bn = pn(pB[hp])
                        for h in range(2):
                            hs = slice(h * 64, (h + 1) * 64)
                            vcols = slice((2 * hp + h) * 64, (2 * hp + h + 1) * 64)
                            S.op("pe", lambda hp=hp, h=h, hs=hs: nc.tensor.matmul(Bsl(hp, 256 + h * 64, 64), lhsT=ARt[hp][hs, cc, 0, :], rhs=S_bf[hp][hs, h, :], start=True, stop=False),
                                 reads=[("atl", hp), f"S_bf{hp}"], writes=[Bn])
                            S.op("pe", lambda hp=hp, h=h, vcols=vcols: nc.tensor.matmul(Bsl(hp, 256 + h * 64, 64), lhsT=MB[hp][:, h, 0, :], rhs=Vt[:, c, vcols], start=False, stop=True),
                                 reads=[("MB", hp), f"Vt{c}"], writes=[Bn])
                        S.op("act", lambda hp=hp: nc.scalar.copy(out=X_bf[hp][:], in_=Bsl(hp, 256, 128).rearrange("p (h v) -> p h v", h=2)), reads=[Bn], writes=[f"X_bf{hp}"])
                    for hp in range(4):
                        Bn = pn(pB[hp])
                        for h in range(2):
                            S.op("pe", lambda hp=hp, h=h: nc.tensor.matmul(Bsl(hp, 384 + h * 64, 64), lhsT=TT(hp, h), rhs=X_bf[hp][:, h, :], start=True, stop=True),
                                 reads=[("P", hp, 6), f"X_bf{hp}"], writes=[Bn])
                        S.op("dve", lambda hp=hp: nc.vector.tensor_copy(out=U_bf[hp][:], in_=Bsl(hp, 384, 128).rearrange("p (h v) -> p h v", h=2)), reads=[Bn], writes=[f"U_bf{hp}"])
                    Ytc = Yt[c % 2]
                    for hp in range(4):
                        Bn = pn(pB[hp])
                        for h in range(2):
                            hs = slice(h * 64, (h + 1) * 64)
                            vcols = slice((2 * hp + h) * 64, (2 * hp + h + 1) * 64)
                            S.op("pe", lambda hp=hp, h=h, hs=hs: nc.tensor.matmul(Bsl(hp, 256 + h * 64, 64), lhsT=ARt[hp][hs, cc, 1, :], rhs=S_bf[hp][hs, h, :], start=True, stop=False),
                                 reads=[("rtl", hp), f"S_bf{hp}"], writes=[Bn])
                            S.op("pe", lambda hp=hp, h=h: nc.tensor.matmul(Bsl(hp, 256 + h * 64, 64), lhsT=MA[hp][:, h, 1, :], rhs=U_bf[hp][:, h, :], start=False, stop=False),
                                 reads=[("MA", hp), f"U_bf{hp}"], writes=[Bn])
                            S.op("pe", lambda hp=hp, h=h, vcols=vcols: nc.tensor.matmul(Bsl(hp, 256 + h * 64, 64), lhsT=MB[hp][:, h, 1, :], rhs=Vt[:, c, vcols], start=False, stop=True),
                                 reads=[("MB", hp), f"Vt{c}"], writes=[Bn])
                        S.op("act", lambda hp=hp, Ytc=Ytc: nc.scalar.copy(out=Ytc[:, hp * 128:(hp + 1) * 128], in_=Bsl(hp, 256, 128)), reads=[Bn], writes=[f"Yt{c % 2}"])
                    for hp in range(4):
                        Bn = pn(pB[hp])
                        for h in range(2):
                            vcols = slice((2 * hp + h) * 64, (2 * hp + h + 1) * 64)
                            S.op("pe", lambda hp=hp, h=h: nc.tensor.matmul(Bsl(hp, h * 64, 64), lhsT=BhT[hp][:, cc, :], rhs=U_bf[hp][:, h, :], start=True, stop=False),
                                 reads=[("BhT", hp), f"U_bf{hp}"], writes=[Bn])
                            S.op("pe", lambda hp=hp, h=h, vcols=vcols: nc.tensor.matmul(Bsl(hp, h * 64, 64), lhsT=KhT[hp][:, cc, :], rhs=Vt[:, c, vcols], start=False, stop=True),
                                 reads=[("KhT", hp), f"Vt{c}"], writes=[Bn])
                        S.op("dve", lambda hp=hp: nc.vector.scalar_tensor_tensor(out=S_f[hp][:].rearrange("p h v -> p (h v)"), in0=S_f[hp][:].rearrange("p h v -> p (h v)"),
                                                                                  scalar=gam[:, hp, c:c + 1], in1=Bsl(hp, 0, 128), op0=ALU.mult, op1=ALU.add),
                             reads=[Bn, f"S_f{hp}", ("gam", hp, tg)], writes=[f"S_f{hp}"])
                        S.op("act", lambda hp=hp: nc.scalar.copy(out=S_bf[hp][:], in_=S_f[hp][:]), reads=[f"S_f{hp}"], writes=[f"S_bf{hp}"])
                    S.pe_inorder = 'nobfpost' not in SK
                    Yk = f"Yt{c % 2}"
                    Y3 = Ytc[:].rearrange("p (g v) -> p g v", g=8)
                    def bc8(i):
                        return st8[:, i, :].unsqueeze(2).to_broadcast([128, 8, 64])
                    S.op("dve", lambda Y3=Y3: nc.vector.reduce_sum(out=st8[:, 0, :], in_=Y3, axis=AX.X), reads=[Yk], writes=["st8_0"])
                    S.op("dve", lambda Ytc=Ytc: nc.vector.tensor_tensor(out=Y1[:], in0=Ytc[:], in1=Ytc[:], op=ALU.mult), reads=[Yk], writes=["Y1"])
                    S.op("dve", lambda: nc.vector.reduce_sum(out=st8[:, 1, :], in_=Y1[:].rearrange("p (g v) -> p g v", g=8), axis=AX.X), reads=["Y1"], writes=["st8_1"])
                    S.op("dve", lambda: nc.vector.tensor_scalar(out=st8[:, 2, :], in0=st8[:, 0, :], scalar1=1.0 / 64, scalar2=None, op0=ALU.mult), reads=["st8_0"], writes=["st8_2"])
                    S.op("dve", lambda: nc.vector.tensor_tensor(out=st8[:, 3, :], in0=st8[:, 2, :], in1=st8[:, 2, :], op=ALU.mult), reads=["st8_2"], writes=["st8_3"])
                    S.op("dve", lambda: nc.vector.scalar_tensor_tensor(out=st8[:, 4, :], in0=st8[:, 1, :], scalar=1.0 / 64, in1=st8[:, 3, :], op0=ALU.mult, op1=ALU.subtract),
                         reads=["st8_1", "st8_3"], writes=["st8_4"])
                    S.op("dve", lambda: nc.vector.tensor_scalar(out=st8[:, 4, :], in0=st8[:, 4, :], scalar1=64e-5, scalar2=None, op0=ALU.add), reads=["st8_4"], writes=["st8_4"])
                    S.op("act", lambda: nc.scalar.activation(out=st8[:, 5, :], in_=st8[:, 4, :], func=AF.Ln), reads=["st8_4"], writes=["st8_5"])
                    S.op("act", lambda: nc.scalar.activation(out=st8[:, 5, :], in_=st8[:, 5, :], func=AF.Exp, scale=-0.5), reads=["st8_5"], writes=["st8_5"])
                    S.op("dve", lambda Y3=Y3: nc.vector.tensor_tensor(out=Y1[:].rearrange("p (g v) -> p g v", g=8), in0=Y3, in1=bc8(2), op=ALU.subtract),
                         reads=[Yk, "st8_2", "Y1"], writes=["Y1"])
                    S.op("dve", lambda: nc.vector.tensor_tensor(out=Y1[:].rearrange("p (g v) -> p g v", g=8), in0=Y1[:].rearrange("p (g v) -> p g v", g=8), in1=bc8(5), op=ALU.mult),
                         reads=["Y1", "st8_5"], writes=["Y1"])
                    S.op("dve", lambda: nc.vector.tensor_tensor(out=Y1[:], in0=Y1[:], in1=lng_bc[:], op=ALU.mult), reads=["Y1", "lng_bc"], writes=["Y1"])
                    S.op("dve", lambda: nc.vector.tensor_tensor(out=Y1[:], in0=Y1[:], in1=lnb_bc[:], op=ALU.add), reads=["Y1", "lnb_bc"], writes=["Y1"])
                    S.op("dve", lambda c=c: nc.vector.tensor_tensor(out=Y2[:].rearrange("p (g v) -> p g v", g=8), in0=Vt[:, c, :].rearrange("p (g v) -> p g v", g=8),
                                                                     in1=bon[:, c, :].unsqueeze(2).to_broadcast([128, 8, 64]), op=ALU.mult),
                         reads=[f"Vt{c}"] + [("bon", hp, tg) for hp in range(4)], writes=["Y2"])
                    S.op("dve", lambda: nc.vector.tensor_tensor(out=Y1[:], in0=Y1[:], in1=Y2[:], op=ALU.add), reads=["Y1", "Y2"], writes=["Y1"])
                    G_ = pA[0]
                    Gn = pn(G_)
                    S.op("pe", lambda c=c, G_=G_: nc.tensor.matmul(G_[:], lhsT=sgdT[:, c * 128:(c + 1) * 128], rhs=g2_bf[:], start=True, stop=True), reads=["sgdT", "g2_bf"], writes=[Gn])
                    S.op("dve", lambda G_=G_: nc.vector.tensor_tensor(out=yb[:], in0=Y1[:], in1=G_[:], op=ALU.mult), reads=["Y1", Gn], writes=["yb"])
                    G1 = pA[1]
                    G1n = pn(pA[1])
                    for hp in range(4):
                        S.op("pe", lambda hp=hp, G1=G1: nc.tensor.transpose(out=G1[:, hp * 128:(hp + 1) * 128], in_=yb[:, hp * 128:(hp + 1) * 128], identity=idtf[:]),
                             reads=["yb", "idtf"], writes=[G1n])
                    S.op("act", lambda c=c, G1=G1: nc.scalar.copy(out=ybT[:, :, c * 128:(c + 1) * 128], in_=G1[:].rearrange("p (h i) -> p h i", h=4)),
                         reads=[G1n], writes=["ybT"])
            if 'sbufdbg' in SK:
                print("SBUF remaining in B scope 2:", nc.sbuf_bytes_remaining)
            for tg in range(4):
                parts = []
                for hp in range(4):
                    S.capture = []
                    prep(tg, hp)
                    cap, S.capture = S.capture, None
                    k = cap.index("SPLIT")
                    parts.append((cap[:k], cap[k + 1:]))
                S.replay(parts[0][0])
                for hp in range(4):
                    p2 = parts[hp][1]
                    p1 = parts[hp + 1][0] if hp < 3 else []
                    i1 = i2 = 0
                    while i1 < len(p1) or i2 < len(p2):
                        if i2 < len(p2):
                            S.replay(p2[i2:i2 + 2]); i2 += 2
                        if i1 < len(p1):
                            S.replay(p1[i1:i1 + 1]); i1 += 1
                if 'b2prep1' in SK or 'b2prep' in SK:
                    break
                for cc in range(4):
                    chunk_step(tg, cc)
                    if 'b2c1' in SK:
                        break
                if 'b2c1' in SK or 'b2tg1' in SK:
                    break
            S.barrier()
            run_block(nc, S)


def norm_T(nc, S, tag, get_x, gcol, dstT, idt, bufs, eps=1e-6, hook=None):
    xn, junk, ss, rs, rstd, tp = bufs
    keys = []
    pending = []
    for t in range(NT):
        b = t % 2
        if hook is not None:
            hook(t)
        xa, xk = get_x(t)
        S.op("act", lambda t=t, xa=xa: nc.scalar.activation(out=junk[:], in_=xa, func=AF.Square, accum_out=ss[:, t:t + 1]),
             reads=xk, writes=[tag + "junk", f"{tag}ss{t}"])
        S.op("act", lambda t=t: nc.scalar.activation(out=rs[:, t:t + 1], in_=ss[:, t:t + 1], func=AF.Sqrt, scale=1.0 / 1024, bias=eps),
             reads=[f"{tag}ss{t}"], writes=[f"{tag}rs{t}"])
        S.op("dve", lambda t=t: nc.vector.reciprocal(out=rstd[:, t:t + 1], in_=rs[:, t:t + 1]), reads=[f"{tag}rs{t}"], writes=[f"{tag}rstd{t}"])
        S.op("dve", lambda t=t, b=b, xa=xa: nc.vector.tensor_scalar(out=xn[b][:], in0=xa, scalar1=rstd[:, t:t + 1], scalar2=None, op0=ALU.mult),
             reads=xk + [f"{tag}rstd{t}"], writes=[f"{tag}xn{b}"])
        for dc in range(8):
            S.op("pe", lambda b=b, dc=dc: nc.tensor.transpose(out=tp[b][:, dc, :], in_=xn[b][:, dc * 128:(dc + 1) * 128], identity=idt[:]),
                 reads=[f"{tag}xn{b}", "idt"], writes=[f"{tag}tp{id(tp[b])}"])
        k = f"{tag}T{t}"
        keys.append(k)
        pending.append(lambda t=t, b=b, k=k: S.op("dve", lambda: nc.vector.tensor_tensor(
            out=dstT[:, :, OFF + t * 128:OFF + (t + 1) * 128], in0=tp[b][:],
            in1=gcol[:].unsqueeze(2).to_broadcast([128, 8, 128]), op=ALU.mult),
            reads=[f"{tag}tp{id(tp[b])}", tag + "gcol"], writes=[k]))
        if len(pending) > (1 if tp[0] is not tp[1] else 0):
            pending.pop(0)()
    while pending:
        pending.pop(0)()
    return keys


def phase_C(nc, S, P, C, x_in, xmid, hT, yaT, ybT, idt, gmix):
    w_in = P["w_in"]
    with ExitStack() as ph:
        def psb(name, shape, dt):
            return ph.enter_context(nc.sbuf_tensor(name, shape, dt))
        xt = [psb(f"cxt{i}", [128, 1024], F32) for i in range(2)]
        xn = [psb(f"cxn{i}", [128, 1024], BF16) for i in range(2)]
        junk = psb("cjunk", [128, 1024], BF16)
        ss = psb("css", [128, 16], F32)
        rs = psb("crs", [128, 16], F32)
        rstd = psb("crstd", [128, 16], F32)
        Wg = psb("Wg", [128, 8, 2048], BF16)
        Wa = psb("Wa_", [128, 4, 1024], BF16)
        Wb = psb("Wb_", [128, 4, 1024], BF16)
        Wo = psb("Wo", [128, 8, 1024], BF16)
        stage = [psb(f"stageC{i}", [128, 4, 512], F32) for i in range(2)]
        mT = psb("mT", [128, 8, 512], BF16)
        sga = psb("sga", [128, 512], F32)
        sgb = psb("sgb", [128, 512], F32)
        m1 = psb("m1", [128, 512], F32)
        m2 = psb("m2", [128, 512], F32)
        xo = [psb(f"xo{i}", [128, 1024], F32) for i in range(2)]
        tp = [ph.enter_context(nc.psum_tensor(f"ctp{i}", [128, 8, 128], BF16)) for i in range(2)]
        ps = [ph.enter_context(nc.psum_tensor(f"cps{i}", [128, 512], F32)) for i in range(6)]
        S.op("pool", lambda: nc.gpsimd.memset(hT[:, :, 0:OFF], 0.0), writes=["hTpad"])

        def get_x(t):
            b = t % 2
            S.dma("sp", lambda: nc.sync.dma_start(out=xt[b][:], in_=x_in[t * 128:(t + 1) * 128, :]), f"cx{b}", writes=[f"cxt{b}"], serial=True)
            return xt[b][:], [f"cxt{b}"]
        ld = LoadCast(nc, S, stage)
        ld.skey = "stageC"
        wp = (ld.pieces(Wg, w_in, 8, 3328, 2048, "Wg") + ld.pieces(Wa, P["w_branch_a"], 4, 0, 1024, "Wa_")
              + ld.pieces(Wb, P["w_branch_b"], 4, 0, 1024, "Wb_") + ld.pieces(Wo, P["w_o"], 8, 0, 1024, "Wo"))

        def chook(t):
            if wp:
                wp.pop(0)()
        HK = norm_T(nc, S, "C", get_x, gmix, hT, idt, (xn, junk, ss, rs, rstd, tp), hook=chook)
        while wp:
            wp.pop(0)()
        for tg in range(4):
            cs = slice(tg * 512, (tg + 1) * 512)
            hcs = slice(OFF + tg * 512, OFF + (tg + 1) * 512)
            for fc in range(8):
                fa = slice(fc * 128, (fc + 1) * 128)
                fb = slice(1024 + fc * 128, 1024 + (fc + 1) * 128)
                for dc in range(8):
                    S.op("pe", lambda dc=dc, fa=fa, hcs=hcs: nc.tensor.matmul(ps[0][:], lhsT=Wg[:, dc, fa], rhs=hT[:, dc, hcs], start=(dc == 0), stop=(dc == 7)),
                         reads=["Wg"] + HK, writes=["cps0"])
                for dc in range(8):
                    S.op("pe", lambda dc=dc, fb=fb, hcs=hcs: nc.tensor.matmul(ps[1][:], lhsT=Wg[:, dc, fb], rhs=hT[:, dc, hcs], start=(dc == 0), stop=(dc == 7)),
                         reads=["Wg"] + HK, writes=["cps1"])
                for kc in range(4):
                    S.op("pe", lambda kc=kc, fa=fa, cs=cs: nc.tensor.matmul(ps[2][:], lhsT=Wa[:, kc, fa], rhs=yaT[:, kc, cs], start=(kc == 0), stop=(kc == 3)),
                         reads=["Wa_", "yaT"], writes=["cps2"])
                for kc in range(4):
                    S.op("pe", lambda kc=kc, fa=fa, cs=cs: nc.tensor.matmul(ps[3][:], lhsT=Wb[:, kc, fa], rhs=ybT[:, kc, cs], start=(kc == 0), stop=(kc == 3)),
                         reads=["Wb_", "ybT"], writes=["cps3"])
                S.op("act", lambda: nc.scalar.activation(out=sga[:], in_=ps[0][:], func=AF.Sigmoid), reads=["cps0"], writes=["sga"])
                S.op("act", lambda: nc.scalar.activation(out=sgb[:], in_=ps[1][:], func=AF.Sigmoid), reads=["cps1"], writes=["sgb"])
                S.op("dve", lambda: nc.vector.tensor_tensor(out=m1[:], in0=sga[:], in1=ps[2][:], op=ALU.mult), reads=["sga", "cps2"], writes=["m1"])
                S.op("dve", lambda: nc.vector.tensor_tensor(out=m2[:], in0=sgb[:], in1=ps[3][:], op=ALU.mult), reads=["sgb", "cps3"], writes=["m2"])
                S.op("dve", lambda fc=fc: nc.vector.tensor_tensor(out=mT[:, fc, :], in0=m1[:], in1=m2[:], op=ALU.add), reads=["m1", "m2"], writes=[f"mT{fc}"])
            for tt in range(4):
                t = tg * 4 + tt
                b = t % 2
                S.dma("sp", lambda t=t, b=b: nc.sync.dma_start(out=xt[b][:], in_=x_in[t * 128:(t + 1) * 128, :]), f"cx{b}", writes=[f"cxt{b}"], serial=True)
                for half in range(2):
                    pso = ps[4 + half]
                    for fc in range(8):
                        S.op("pe", lambda fc=fc, tt=tt, half=half, pso=pso: nc.tensor.matmul(pso[:], lhsT=mT[:, fc, tt * 128:(tt + 1) * 128], rhs=Wo[:, fc, half * 512:(half + 1) * 512],
                                                                                         start=(fc == 0), stop=(fc == 7)), reads=["Wo", f"mT{fc}"], writes=[f"cps{4 + half}"])
                    S.op("dve", lambda b=b, half=half, pso=pso: nc.vector.tensor_tensor(out=xo[b][:, half * 512:(half + 1) * 512], in0=pso[:], in1=xt[b][:, half * 512:(half + 1) * 512], op=ALU.add),
                         reads=[f"cps{4 + half}", f"cxt{b}"], writes=[f"xo{b}_{half}"])
                S.dma("sp", lambda t=t, b=b: nc.sync.dma_start(out=xmid[t * 128:(t + 1) * 128, :], in_=xo[b][:]), f"xmidw{b}", reads=[f"xo{b}_0", f"xo{b}_1"], writes=[f"xmid{t}"], serial=True)
        run_block(nc, S)


def phase_D(nc, S, P, C, xmid, out, idt):
    with ExitStack() as ph:
        def psb(name, shape, dt):
            return ph.enter_context(nc.sbuf_tensor(name, shape, dt))
        xres = psb("xres", [128, NT, 1024], F32)
        h2T = psb("h2T", [128, 8, S_LEN + OFF], BF16)
        xn = [psb(f"dxn{i}", [128, 1024], BF16) for i in range(2)]
        junk = psb("djunk", [128, 1024], BF16)
        ss = psb("dss", [128, 16], F32)
        rs = psb("drs", [128, 16], F32)
        rstd = psb("drstd", [128, 16], F32)
        gffn = psb("gffn", [128, 8], F32)
        gfin = psb("gfin", [128, 1024], F32)
        W1q = [psb(f"W1q{i}", [128, 8, 1024], BF16) for i in range(2)]
        W2q = [psb(f"W2q{i}", [128, 8, 1024], BF16) for i in range(2)]
        stage = [psb(f"stageD{i}", [128, 4, 512], F32) for i in range(2)]
        ur = [psb(f"ur{i}", [128, 256], F32) for i in range(3)]
        ub = [psb(f"ub{i}", [128, 256], BF16) for i in range(3)]
        ot = [psb(f"ot{i}", [128, 1024], F32) for i in range(2)]
        tp0 = ph.enter_context(nc.psum_tensor("dtp0", [128, 8, 128], BF16))
        tp = [tp0, tp0]
        acc = [ph.enter_context(nc.psum_tensor(f"acc{i}", [128, 512], F32)) for i in range(4)]
        ups = [ph.enter_context(nc.psum_tensor(f"ups{i}", [128, 512], F32)) for i in range(3)]
        S.dma("sp", lambda: nc.sync.dma_start(out=gffn[:], in_=P["norm_ffn_g"].rearrange("(c p) -> p c", p=128), allow_slow_non_contiguous=True), "cD", writes=["Dgcol"])
        S.dma("sp", lambda: nc.sync.dma_start(out=gfin[:], in_=P["norm_final_g"].partition_broadcast(128)), "cD", writes=["gfin"])
        for t in range(NT):
            S.dma("sp", lambda t=t: nc.sync.dma_start(out=xres[:, t, :], in_=xmid[t * 128:(t + 1) * 128, :]), f"xr{t % 4}", reads=[f"xmid{t}"], writes=[f"xres{t}"])

        def get_x(t):
            return xres[:, t, :], [f"xres{t}"]
        ld = LoadCast(nc, S, stage)
        ld.skey = "stageD"
        wp0 = []

        def dhook(t):
            if t % 2 == 1 and wp0:
                wp0.pop(0)()

        def wpieces(q):
            bq = q % 2
            return (ld.pieces(W1q[bq], P["w_ff1"], 8, q * 1024, 1024, f"W1q{bq}", eng="dve")
                    + ld.pieces(W2q[bq], P["w_ff2"], 8, 0, 1024, f"W2q{bq}", r0=q * 1024, eng="act"))
        wp0.extend(wpieces(0))
        HK = norm_T(nc, S, "D", get_x, gffn, h2T, idt, (xn, junk, ss, rs, rstd, tp), hook=dhook)
        while wp0:
            wp0.pop(0)()
        its = [(q, g, j) for q in range(4) for g in range(8) for j in range(8)]
        LOOK = 2

        def emit_U(i):
            q, g, j = its[i]
            b = q % 2
            u3 = i % 3
            gcs = slice(OFF + g * 256, OFF + (g + 1) * 256)
            up = ups[u3]
            for dc in range(8):
                S.op("pe", lambda dc=dc: nc.tensor.matmul(up[:, 0:256], lhsT=W1q[b][:, dc, j * 128:(j + 1) * 128], rhs=h2T[:, dc, gcs],
                                                          start=(dc == 0), stop=(dc == 7)), reads=[f"W1q{b}"] + HK, writes=[f"ups{u3}"])
            S.op("act", lambda: nc.scalar.activation(out=ur[u3][:], in_=up[:, 0:256], func=AF.Relu), reads=[f"ups{u3}"], writes=[f"ur{u3}"])
            S.op("dve", lambda: nc.vector.tensor_tensor(out=ub[u3][:], in0=ur[u3][:], in1=ur[u3][:], op=ALU.mult), reads=[f"ur{u3}"], writes=[f"ub{u3}"])

        def final_norm(t):
            b = t % 2
            S.op("act", lambda: nc.scalar.activation(out=junk[:], in_=xres[:, t, :], func=AF.Square, accum_out=ss[:, t:t + 1]),
                 reads=[f"xres{t}"], writes=["Djunk", f"Fss{t}"])
            S.op("act", lambda: nc.scalar.activation(out=rs[:, t:t + 1], in_=ss[:, t:t + 1], func=AF.Sqrt, scale=1.0 / 1024, bias=1e-6),
                 reads=[f"Fss{t}"], writes=[f"Frs{t}"])
            S.op("dve", lambda: nc.vector.reciprocal(out=rstd[:, t:t + 1], in_=rs[:, t:t + 1]), reads=[f"Frs{t}"], writes=[f"Frstd{t}"])
            S.op("dve", lambda: nc.vector.tensor_scalar(out=ot[b][:], in0=xres[:, t, :], scalar1=rstd[:, t:t + 1], scalar2=None, op0=ALU.mult),
                 reads=[f"xres{t}", f"Frstd{t}"], writes=[f"ot{b}"])
            S.op("dve", lambda: nc.vector.tensor_tensor(out=ot[b][:], in0=ot[b][:], in1=gfin[:], op=ALU.mult), reads=[f"ot{b}", "gfin"], writes=[f"ot{b}"])
            S.dma("sp", lambda: nc.sync.dma_start(out=out[t * 128:(t + 1) * 128, :], in_=ot[b][:]), f"outw{b}", reads=[f"ot{b}"], writes=[f"out{t}"], serial=True)

        def emit_ACC(i):
            q, g, j = its[i]
            b = q % 2
            u3 = i % 3
            for tt in range(2):
                for half in range(2):
                    S.op("pe", lambda tt=tt, half=half: nc.tensor.matmul(acc[tt * 2 + half][:], lhsT=ub[u3][:, tt * 128:(tt + 1) * 128],
                                                                         rhs=W2q[b][:, j, half * 512:(half + 1) * 512], start=(j == 0), stop=(j == 7)),
                         reads=[f"ub{u3}", f"W2q{b}"], writes=[f"acc{tt * 2 + half}"])
            if j == 7:
                for tt in range(2):
                    t = g * 2 + tt
                    for half in range(2):
                        S.op("dve", lambda t=t, tt=tt, half=half: nc.vector.tensor_tensor(out=xres[:, t, half * 512:(half + 1) * 512], in0=acc[tt * 2 + half][:],
                                                                                          in1=xres[:, t, half * 512:(half + 1) * 512], op=ALU.add),
                             reads=[f"acc{tt * 2 + half}", f"xres{t}"], writes=[f"xres{t}"])
                    if q == 3:
                        final_norm(t)

        pend = []
        for i in range(len(its) + LOOK):
            if i < len(its):
                q, g, j = its[i]
                if g == 0 and j == 0 and q + 1 < 4:
                    pend = wpieces(q + 1)
                if pend and (i % 6 == 3):
                    pend.pop(0)()
                emit_U(i)
            if i - LOOK >= 0:
                emit_ACC(i - LOOK)
        assert not pend
        run_block(nc, S, ["all"])


_CACHE = {}


def kernel(**inputs):
    if "nc" not in _CACHE:
        _CACHE["nc"] = build()[0]
    nc = _CACHE["nc"]
    consts = host_consts()
    params = pack_params({k: np.asarray(v) for k, v in inputs.items() if k != "x"})
    x = np.ascontiguousarray(np.asarray(inputs["x"], np.float32))
    in_maps = []
    for b in range(8):
        m = {"x": x[b]}
        m.update(params)
        m.update(consts)
        in_maps.append(m)
    res = run_bass_kernel_spmd(nc, in_maps, core_ids=list(range(8)))
    return np.stack([np.asarray(r["out"], np.float32) for r in res.results], 0)
```
